# Optimizing a Trainium2 kernel written in Bass

```python
import math
import jax, jax.numpy as jnp
from jax import lax
import numpy as np

D_MODEL = 1024
BATCH = 16
SEQ = 256
DEPTH = 4
DEC_BATCH = 2
DEC_SEQ = 2048
PAST_LEN = 512

GRID_W = 64
N_MIXERS = 2
N_SSM_LAYERS = (DEPTH + 1) // 2
N_FOURIER_LAYERS = DEPTH // 2
S5_GROUP_CH = 16
S5_GROUPS = D_MODEL // S5_GROUP_CH
S5_STATE = 64
FOURIER_GROUPS = 4
FOURIER_GROUP_CH = D_MODEL // FOURIER_GROUPS
D_FF = 2816
CONV_W = 3
EPS = 1e-6
DT_MIN = 1e-3
DT_MAX = 1e-1

kernel_name = "s5_fnet_convffn_diffusion_step"


def rms_norm(x, g):
    xf = x.astype(jnp.float32)
    y = xf * lax.rsqrt(jnp.mean(xf * xf, axis=-1, keepdims=True) + EPS)
    return (y * g.astype(jnp.float32)).astype(x.dtype)


def _linear_recurrence(e1, e2):
    a1, b1 = e1
    a2, b2 = e2
    return a1 * a2, a2 * b1 + b2


def s5_direction(u, lam_re, lam_im, log_dt, b_re, b_im, c_re, c_im, h0, reverse):
    lam = lax.complex(lam_re.astype(jnp.float32), lam_im.astype(jnp.float32))
    dt = jnp.exp(log_dt.astype(jnp.float32))[:, None]
    abar = jnp.exp(lam * dt)
    bmat = lax.complex(b_re.astype(jnp.float32), b_im.astype(jnp.float32))
    bbar = ((abar - 1.0) / lam)[..., None] * bmat
    bu = jnp.einsum('blgc,gpc->blgp', u.astype(jnp.complex64), bbar)
    edge = -1 if reverse else 0
    bu = bu.at[:, edge].add(abar * h0)
    a = jnp.broadcast_to(abar, bu.shape)
    _, h = lax.associative_scan(_linear_recurrence, (a, bu), axis=1, reverse=reverse)
    cmat = lax.complex(c_re.astype(jnp.float32), c_im.astype(jnp.float32))
    y = jnp.real(jnp.einsum('gcp,blgp->blgc', cmat, h))
    return y, h[:, edge]


def s5_mixer(h, j, p, h0_re, h0_im):
    bsz, seq, dm = h.shape
    u = h.astype(jnp.float32).reshape(bsz, seq, S5_GROUPS, S5_GROUP_CH)
    y_sum = None
    fin_re, fin_im = [], []
    for d in range(2):
        h0 = lax.complex(h0_re[:, d].astype(jnp.float32), h0_im[:, d].astype(jnp.float32))
        y_d, fin = s5_direction(u, p['ssm_lam_re'][j, d], p['ssm_lam_im'][j, d], p['ssm_log_dt'][j, d],
                                p['ssm_b_re'][j, d], p['ssm_b_im'][j, d], p['ssm_c_re'][j, d], p['ssm_c_im'][j, d],
                                h0, reverse=(d == 1))
        y_sum = y_d if y_sum is None else y_sum + y_d
        fin_re.append(jnp.real(fin))
        fin_im.append(jnp.imag(fin))
    y = y_sum.reshape(bsz, seq, dm) + p['ssm_d'][j].astype(jnp.float32) * h.astype(jnp.float32)
    y = jax.nn.gelu(y).astype(h.dtype)
    z = y @ p['w_glu'][j] + p['b_glu'][j]
    out = z[..., :dm] * jax.nn.sigmoid(z[..., dm:])
    return out, jnp.stack(fin_re, axis=1), jnp.stack(fin_im, axis=1)


def fourier_mixer(h, j, p):
    bsz, seq, dm = h.shape
    hg = h.astype(jnp.float32).reshape(bsz, seq, FOURIER_GROUPS, FOURIER_GROUP_CH)
    f = jnp.real(jnp.fft.fft2(hg, axes=(1, 3), norm='ortho')).reshape(bsz, seq, dm).astype(h.dtype)
    return f @ p['w_fourier'][j] + p['b_fourier'][j]


def dwconv_rows(h, w, b, n_rows):
    bsz, seq, ch = h.shape
    hr = h.reshape(bsz, n_rows, seq // n_rows, ch)
    hp = jnp.pad(hr, ((0, 0), (0, 0), (1, 1), (0, 0)))
    out = w[0] * hp[:, :, :-2] + w[1] * hp[:, :, 1:-1] + w[2] * hp[:, :, 2:] + b
    return out.reshape(bsz, seq, ch)


def conv_ffn(h, i, p, n_rows):
    up = h @ p['w_up'][i]
    up = dwconv_rows(up, p['conv_w'][i], p['conv_b'][i], n_rows)
    gate, val = up[..., :D_FF], up[..., D_FF:]
    return (jax.nn.silu(gate) * val) @ p['w_down'][i]


def trunk(x, cond, h0_re, h0_im, n_rows, p, return_states):
    st_re, st_im = [], []
    for i in range(DEPTH):
        mod = (jax.nn.silu(cond) @ p['w_ada'][i] + p['b_ada'][i])[:, None, :]
        sh1, sc1, g1, sh2, sc2, g2 = jnp.split(mod, 6, axis=-1)
        h = rms_norm(x, p['g_mix'][i]) * (1.0 + sc1) + sh1
        j = i // N_MIXERS
        if i % N_MIXERS == 0:
            out, fr, fi = s5_mixer(h, j, p, h0_re[:, j], h0_im[:, j])
            if return_states:
                st_re.append(fr)
                st_im.append(fi)
        else:
            out = fourier_mixer(h, j, p)
        x = x + g1 * out
        h = rms_norm(x, p['g_ffn'][i]) * (1.0 + sc2) + sh2
        x = x + g2 * conv_ffn(h, i, p, n_rows)
    y = rms_norm(x, p['g_final'])
    if return_states:
        return y, jnp.stack(st_re, axis=1), jnp.stack(st_im, axis=1)
    return y


def setup_inputs(seed: int = 0) -> dict:
    key = jax.random.key(seed)
    ks = jax.random.split(key, 32)
    f32 = jnp.float32
    D = D_MODEL
    nrm = lambda k, shape, s: jax.random.normal(k, shape, f32) * s
    n_idx = jnp.arange(S5_STATE, dtype=f32)
    lam_re = -0.5 + nrm(ks[6], (N_SSM_LAYERS, 2, S5_GROUPS, S5_STATE), 0.01)
    lam_im = math.pi * n_idx + nrm(ks[7], (N_SSM_LAYERS, 2, S5_GROUPS, S5_STATE), 0.01)
    log_dt = jax.random.uniform(ks[8], (N_SSM_LAYERS, 2, S5_GROUPS), f32, math.log(DT_MIN), math.log(DT_MAX))
    bscale = (2.0 * S5_GROUP_CH) ** -0.5
    cscale = (2.0 * S5_STATE) ** -0.5
    return {
        'x_prompt': nrm(ks[0], (BATCH, SEQ, D), 1.0),
        'x_sample': nrm(ks[1], (DEC_BATCH, DEC_SEQ, D), 1.0),
        'state_ssm_re': nrm(ks[2], (DEC_BATCH, N_SSM_LAYERS, 2, S5_GROUPS, S5_STATE), 0.1),
        'state_ssm_im': nrm(ks[3], (DEC_BATCH, N_SSM_LAYERS, 2, S5_GROUPS, S5_STATE), 0.1),
        'c': nrm(ks[4], (DEC_BATCH, D), 1.0),
        'c_ctx': nrm(ks[5], (D,), 1.0),
        'w_ada': nrm(ks[9], (DEPTH, D, 6 * D), 0.5 * D ** -0.5),
        'b_ada': nrm(ks[10], (DEPTH, 6 * D), 0.01),
        'g_mix': 1.0 + nrm(ks[11], (DEPTH, D), 0.02),
        'g_ffn': 1.0 + nrm(ks[12], (DEPTH, D), 0.02),
        'ssm_lam_re': lam_re,
        'ssm_lam_im': lam_im,
        'ssm_log_dt': log_dt,
        'ssm_b_re': nrm(ks[13], (N_SSM_LAYERS, 2, S5_GROUPS, S5_STATE, S5_GROUP_CH), bscale),
        'ssm_b_im': nrm(ks[14], (N_SSM_LAYERS, 2, S5_GROUPS, S5_STATE, S5_GROUP_CH), bscale),
        'ssm_c_re': nrm(ks[15], (N_SSM_LAYERS, 2, S5_GROUPS, S5_GROUP_CH, S5_STATE), cscale),
        'ssm_c_im': nrm(ks[16], (N_SSM_LAYERS, 2, S5_GROUPS, S5_GROUP_CH, S5_STATE), cscale),
        'ssm_d': nrm(ks[17], (N_SSM_LAYERS, D), 1.0),
        'w_glu': nrm(ks[18], (N_SSM_LAYERS, D, 2 * D), D ** -0.5),
        'b_glu': nrm(ks[19], (N_SSM_LAYERS, 2 * D), 0.01),
        'w_fourier': nrm(ks[20], (N_FOURIER_LAYERS, D, D), D ** -0.5),
        'b_fourier': nrm(ks[21], (N_FOURIER_LAYERS, D), 0.01),
        'w_up': nrm(ks[22], (DEPTH, D, 2 * D_FF), D ** -0.5),
        'conv_w': nrm(ks[23], (DEPTH, CONV_W, 2 * D_FF), CONV_W ** -0.5),
        'conv_b': nrm(ks[24], (DEPTH, 2 * D_FF), 0.01),
        'w_down': nrm(ks[25], (DEPTH, D_FF, D), D_FF ** -0.5),
        'g_final': 1.0 + nrm(ks[26], (D,), 0.02),
    }


def reference(x_prompt, x_sample, state_ssm_re, state_ssm_im, c, c_ctx,
              w_ada, b_ada, g_mix, g_ffn,
              ssm_lam_re, ssm_lam_im, ssm_log_dt, ssm_b_re, ssm_b_im, ssm_c_re, ssm_c_im, ssm_d,
              w_glu, b_glu, w_fourier, b_fourier,
              w_up, conv_w, conv_b, w_down, g_final):
    p = {
        'w_ada': w_ada, 'b_ada': b_ada, 'g_mix': g_mix, 'g_ffn': g_ffn,
        'ssm_lam_re': ssm_lam_re, 'ssm_lam_im': ssm_lam_im, 'ssm_log_dt': ssm_log_dt,
        'ssm_b_re': ssm_b_re, 'ssm_b_im': ssm_b_im, 'ssm_c_re': ssm_c_re, 'ssm_c_im': ssm_c_im,
        'ssm_d': ssm_d, 'w_glu': w_glu, 'b_glu': b_glu,
        'w_fourier': w_fourier, 'b_fourier': b_fourier,
        'w_up': w_up, 'conv_w': conv_w, 'conv_b': conv_b, 'w_down': w_down, 'g_final': g_final,
    }
    bsz = x_prompt.shape[0]
    zeros = jnp.zeros((bsz, N_SSM_LAYERS, 2, S5_GROUPS, S5_STATE), jnp.float32)
    y_prompt, new_ssm_re, new_ssm_im = trunk(x_prompt, c_ctx[None, :], zeros, zeros, 1, p, True)
    rows = x_sample.shape[1] // GRID_W
    y_sample = trunk(x_sample, c, state_ssm_re, state_ssm_im, rows, p, False)
    return (y_prompt, y_sample, new_ssm_re, new_ssm_im)
```

```python
import math
from contextlib import ExitStack

import numpy as np
import concourse.bass as bass
import concourse.mybir as mybir
from concourse.bass_utils import run_bass_kernel_spmd

F32 = mybir.dt.float32
BF16 = mybir.dt.bfloat16
I32 = mybir.dt.int32
AF = mybir.ActivationFunctionType
ALU = mybir.AluOpType

D_MODEL = 1024
DEPTH = 4
D_FF = 2816
NH = D_FF // 128
EPS = 1e-6
COMPUTE = ("pe", "act", "dve", "pool")
NDSEM = 16
USE_POOL = False


class Buf:
    __slots__ = ("name", "lw", "rd")

    def __init__(self, name=""):
        self.name = name
        self.lw = None
        self.rd = []


class Op:
    __slots__ = ("eng", "fn", "deps", "dma", "idx", "sig", "cnt", "dsem", "dval", "prev_on_sem")

    def __init__(self, eng, fn, dma):
        self.eng = eng
        self.fn = fn
        self.dma = dma
        self.deps = set()
        self.sig = False
        self.cnt = None
        self.dsem = None
        self.dval = None
        self.prev_on_sem = None


class Sched:
    def __init__(self, nc):
        self.nc = nc
        self.ops = []
        self.phase_buf = Buf("phase")

    def add(self, eng, fn, reads=(), writes=(), dma=False):
        op = Op(eng, fn, dma)
        op.idx = len(self.ops)
        reads = list(reads)
        if self.phase_buf not in writes:
            reads.append(self.phase_buf)
        for b in reads:
            if b.lw is not None:
                op.deps.add(b.lw)
        for b in writes:
            if b.lw is not None:
                op.deps.add(b.lw)
            for r in b.rd:
                op.deps.add(r)
        for b in reads:
            b.rd.append(op.idx)
        for b in writes:
            b.lw = op.idx
            b.rd = []
        op.deps.discard(op.idx)
        self.ops.append(op)
        return op

    def pe(self, fn, reads=(), writes=()):
        return self.add("pe", fn, reads, writes)

    def act(self, fn, reads=(), writes=()):
        return self.add("act", fn, reads, writes)

    def dve(self, fn, reads=(), writes=()):
        return self.add("dve", fn, reads, writes)

    def dma(self, q, fn, reads=(), writes=()):
        return self.add(q, fn, reads, writes, dma=True)

    def emit(self, final_wait_ops=()):
        nc = self.nc
        ops = self.ops
        for op in ops:
            for d in op.deps:
                dop = ops[d]
                if dop.dma:
                    continue
                if dop.eng == op.eng and dop.eng == "pe" and not op.dma:
                    continue
                dop.sig = True
        cnt = {e: 0 for e in COMPUTE}
        for op in ops:
            if not op.dma and op.sig:
                cnt[op.eng] += 1
                op.cnt = cnt[op.eng]
        self.maxcnt = dict(cnt)
        queues = sorted({op.eng for op in ops if op.dma})
        es = ExitStack()
        esem = {e: es.enter_context(nc.semaphore("c_" + e)) for e in COMPUTE}
        dsems = {q: [es.enter_context(nc.semaphore(f"d_{q}_{i}")) for i in range(NDSEM)] for q in queues}
        dcount = {q: [0] * NDSEM for q in queues}
        dlast = {q: [None] * NDSEM for q in queues}
        rr = {q: 0 for q in queues}
        for op in ops:
            if op.dma:
                q = op.eng
                i = rr[q]
                rr[q] = (i + 1) % NDSEM
                dcount[q][i] += 16
                op.dsem = dsems[q][i]
                op.dval = dcount[q][i]
                op.prev_on_sem = dlast[q][i]
                dlast[q][i] = op.idx
        engs = sorted({op.eng for op in ops})
        block = es.enter_context(nc.Block())
        engobj = {"pe": "tensor", "act": "scalar", "dve": "vector", "pool": "gpsimd", "sp": "sync"}
        fin_ops = [op for op in ops if op.dma and op.idx in final_wait_ops]

        def make(ename):
            def body(e):
                waited = {}

                def wait(sem, val):
                    k = id(sem)
                    if waited.get(k, 0) >= val:
                        return
                    waited[k] = val
                    e.wait_ge(sem, val)

                for op in ops:
                    if op.eng != ename:
                        continue
                    for d in sorted(op.deps):
                        dop = ops[d]
                        if dop.dma:
                            wait(dop.dsem, dop.dval)
                        else:
                            if dop.eng == ename and ename == "pe" and not op.dma:
                                continue
                            wait(esem[dop.eng], dop.cnt)
                    if op.dma:
                        if op.prev_on_sem is not None:
                            p = ops[op.prev_on_sem]
                            wait(p.dsem, p.dval)
                        ins = op.fn(e)
                        ins.then_inc(op.dsem, 16)
                    else:
                        ins = op.fn(e)
                        if op.sig:
                            ins.then_inc(esem[ename], 1)
                if ename == "sp":
                    for op in fin_ops:
                        wait(op.dsem, op.dval)
            return body

        if "sp" not in engs:
            engs.append("sp")
        for ename in engs:
            getattr(block, engobj[ename])(make(ename))
        es.close()


class Rot:
    def __init__(self, items):
        self.items = items
        self.i = 0

    def next(self):
        it = self.items[self.i]
        self.i = (self.i + 1) % len(self.items)
        return it


GROUPS = {
    "P": dict(ntok=512, L=256, nseq=2, R=256, cj=0, passes=[(0, 512)]),
    "S": dict(ntok=2048, L=2048, nseq=1, R=64, cj=1, passes=[(0, 1024), (1024, 1024)]),
}


DEBUG_STOP = None


class _Stop(Exception):
    pass


def build_program():
    nc = bass.Bass("TRN2", target_bir_lowering=False)
    stop = DEBUG_STOP

    def check_stop(name):
        if stop == name:
            raise _Stop()

    es = ExitStack()
    S = Sched(nc)

    def din(n, shape, dt=F32):
        return nc.dram_tensor(n, list(shape), dt, kind="ExternalInput").ap()

    def dout(n, shape):
        return nc.dram_tensor(n, list(shape), F32, kind="ExternalOutput").ap()

    def sb(n, shape, dt=F32):
        return es.enter_context(nc.sbuf_tensor("s_" + n, list(shape), dt))

    d_x = {"P": din("xpT", [1024, 512]), "S": din("xsT", [1024, 2048])}
    d_y = {"P": dout("ypT", [1024, 512]), "S": dout("ysT", [1024, 2048])}
    d_fin = dout("fin", [512, 128])
    d_cond = din("cond", [128, 16])
    d_wada = din("w_ada", [4, 1024, 6144])
    d_bada = din("badaT", [128, 4 * 48])
    d_gmix = din("gmixT", [128, 32])
    d_gffn = din("gffnT", [128, 32])
    d_gfin = din("gfinT", [128, 8])
    d_wglu = din("w_glu", [2, 1024, 2048])
    d_bglu = din("bgluT", [128, 32])
    d_wfour = din("w_fourier", [2, 1024, 1024])
    d_bfour = din("bfourT", [128, 16])
    d_wup = din("w_up", [4, 1024, 2 * D_FF])
    d_convw = din("convwT", [128, 4 * 3 * 44])
    d_convb = din("convbT", [128, 4 * 44])
    d_wdown = din("w_down", [4, D_FF, 1024])
    d_ssmd = din("ssmdT", [128, 16])
    d_s5tab = din("s5tab", [2, 8, 128, 2 * 5 * 256])
    d_s5c = din("s5c", [2, 8, 128, 2 * 2 * 4 * 128])
    d_s5q = din("s5q", [128, 5 * 128])
    d_ident = din("ident", [128, 128])
    d_ck = din("ck", [128, 2 * 2 * 256])
    d_clp = din("clp", [128, 2 * 2 * 256])
    d_cls = din("cls", [2, 2048, 2048], BF16)

    x = {g: sb("x" + g, [128, 8, GROUPS[g]["ntok"]]) for g in GROUPS}
    h = {g: sb("h" + g, [128, 8, GROUPS[g]["ntok"]], BF16) for g in GROUPS}
    bx = {g: [[Buf() for _ in range(GROUPS[g]["ntok"] // 512)] for _ in range(8)] for g in GROUPS}
    bh = {g: [[Buf() for _ in range(GROUPS[g]["ntok"] // 512)] for _ in range(8)] for g in GROUPS}

    ones_bf = sb("ones_bf", [128, 128], BF16)
    epsT = sb("epsT", [128, 1])
    halfpi = sb("halfpi", [128, 1])
    identf = sb("identf", [128, 128])
    cond = sb("cond", [128, 16])
    scond = sb("scond", [128, 8, 2], BF16)
    bada = sb("bada", [128, 4, 48])
    gmix = sb("gmix", [128, 4, 8])
    gffn = sb("gffn", [128, 4, 8])
    gfin = sb("gfin", [128, 8])
    bglu = sb("bglu", [128, 2, 16])
    bfour = sb("bfour", [128, 2, 8])
    ssmd = sb("ssmd", [128, 2, 8])
    s5q = sb("s5q", [128, 5, 128])
    modT_all = sb("modT", [128, 4, 48, 2])
    modL = [modT_all[:, li] for li in range(4)]
    Bmods = [Buf() for _ in range(4)]
    Bder = Buf()
    A1 = sb("A1", [128, 8, 2])
    A2 = sb("A2", [128, 8, 2])
    g1b = sb("g1b", [128, 8, 2])
    finsb = sb("finsb", [128, 512])
    Bconst = Buf("const")
    Bmod = Buf("mod")
    Bfin = Buf("fin")

    ARENA_F32 = 20300
    dummy = sb("dummy", [128, 2])
    arena = sb("arena", [128, ARENA_F32])

    class PH:
        pass

    CURPH = [None]

    def begin_phase(kind):
        ph = PH()
        CURPH[0] = ph
        S.dve(lambda e: e.memset(dummy[:], 0.0), writes=[S.phase_buf])
        off = [0]

        def alloc(shape, dt=F32):
            n = 1
            for d_ in shape[1:]:
                n *= d_
            nbytes = n * (4 if dt in (F32, I32) else 2)
            n4 = (nbytes + 3) // 4
            assert off[0] + n4 <= ARENA_F32, (kind, off[0], n4)
            v = arena[:, off[0]:off[0] + n4]
            off[0] += n4
            if dt != F32:
                v = v.bitcast(dt)
            if len(shape) == 3:
                v = v.rearrange("p (a b) -> p a b", a=shape[1])
            elif len(shape) == 4:
                v = v.rearrange("p (a b c) -> p a b c", a=shape[1], b=shape[2])
            elif len(shape) == 5:
                v = v.rearrange("p (a b c d) -> p a b c d", a=shape[1], b=shape[2], c=shape[3])
            return v

        def rot(n, shape, dt=F32):
            return Rot([(alloc(shape, dt), Buf()) for _ in range(n)])

        def normtmps(nb=2):
            ph.sqb = rot(2, [128, 512], BF16)
            ph.rsd = rot(nb, [128, 2, 512])

        if kind == "mod":
            ph.wbig = rot(2, [128, 6144], BF16)
            ph.wf32 = rot(2, [128, 6144])
        elif kind == "norm":
            normtmps()
            ph.t512 = rot(4, [128, 512])
        elif kind == "s5m":
            ph.XS = [alloc([128, 2, 2048]) for _ in range(2)]
            ph.BXS = [Buf(), Buf()]
            ph.BXS2 = [Buf(), Buf()]
            ph.BXP2 = Buf()
            ph.XP = alloc([128, 2, 512])
            ph.BXP = Buf()
            ph.HbS = [alloc([128, 2048], BF16) for _ in range(2)]
            ph.BHbS = [Buf(), Buf()]
            ph.HbP = [alloc([128, 512], BF16) for _ in range(2)]
            ph.BHbP = [Buf(), Buf()]
            ph.modw = alloc([128, 8, 128], BF16)
            ph.Bmodw = Buf()
            ph.tab = alloc([128, 5, 256])
            ph.Bblk = alloc([128, 2, 2, 256], BF16)
            ph.Cb = alloc([128, 2, 2, 4, 128], BF16)
            ph.BCb = Buf()
            ph.tmpq = [alloc([128, 256]) for _ in range(8)]
            ph.inj2re = alloc([128, 64])
            ph.inj2im = alloc([128, 64])
            ph.Bprep = Buf()
            ph.Btab = Buf()
            ph.t512 = rot(1, [128, 512])
            ph.pwre = alloc([128, 64, 11])
            ph.pwim = alloc([128, 64, 11])
            ph.pwimn = alloc([128, 64, 11])
            ph.injre = alloc([128, 64])
            ph.injim = alloc([128, 64])
            ph.Bq = Buf()
        elif kind == "s5g":
            ph.wbig = rot(2, [128, 2048], BF16)
            ph.t512 = rot(4, [128, 512])
        elif kind == "four":
            normtmps()
            ph.t512 = rot(4, [128, 512])
            ph.pq = alloc([128, 2, 16, 256], BF16)
            ph.Bpq = Buf()
            ph.wbig = rot(2, [128, 8192], BF16)
            ph.ck = alloc([128, 2, 2, 256], BF16)
            ph.clp = alloc([128, 2, 2, 256], BF16)
            ph.Bck = Buf()
            S.dma("pool", lambda e: e.dma_start(out=ph.ck, in_=d_ck.rearrange("p (a b c) -> p a b c", a=2, b=2)), writes=[ph.Bck])
            S.dma("pool", lambda e: e.dma_start(out=ph.clp, in_=d_clp.rearrange("p (a b c) -> p a b c", a=2, b=2)), writes=[ph.Bck])
        elif kind == "ffn":
            normtmps(1)
            ph.t512 = rot(6, [128, 512])
            ph.a_t = alloc([128, 6 * 2560], BF16)
            ph.Ba = [Buf() for _ in range(6)]
            ph.wu = rot(2, [128, 8, 2, 256], BF16)
            ph.wd = rot(1, [128, 6, 1024], BF16)
            ph.convw = alloc([128, 4, 3, 44])
            ph.convb = alloc([128, 4, 44])
            ph.Bcv = Buf()
            S.dma("sp", lambda e: e.dma_start(out=ph.convw, in_=d_convw.rearrange("p (a b c) -> p a b c", a=4, b=3)), writes=[ph.Bcv])
            S.dma("sp", lambda e: e.dma_start(out=ph.convb, in_=d_convb.rearrange("p (a b) -> p a b", a=4)), writes=[ph.Bcv])
        elif kind == "final":
            normtmps()
            ph.t512 = rot(4, [128, 512])
            ph.outrot = rot(8, [128, 512])
        return ph

    psb = [(es.enter_context(nc.psum_tensor(f"ps{i}", [128, 512], F32)), Buf()) for i in range(8)]
    psrot = Rot(psb)

    def ld(dst, src, bufs, q="sp"):
        S.dma(q, lambda e, dst=dst, src=src: e.dma_start(out=dst, in_=src), writes=bufs)

    ld(cond[:], d_cond, [Bconst])
    ld(bada[:], d_bada.rearrange("p (a b) -> p a b", a=4), [Bconst])
    ld(gmix[:], d_gmix.rearrange("p (a b) -> p a b", a=4), [Bconst])
    ld(gffn[:], d_gffn.rearrange("p (a b) -> p a b", a=4), [Bconst])
    ld(gfin[:], d_gfin, [Bconst])
    ld(bglu[:], d_bglu.rearrange("p (a b) -> p a b", a=2), [Bconst])
    ld(bfour[:], d_bfour.rearrange("p (a b) -> p a b", a=2), [Bconst])
    ld(ssmd[:], d_ssmd.rearrange("p (a b) -> p a b", a=2), [Bconst])
    ld(s5q[:], d_s5q.rearrange("p (a b) -> p a b", a=5), [Bconst])
    ld(identf[:], d_ident, [Bconst])
    for g in GROUPS:
        for ct in range(8):
            S.dma("sp", lambda e, g=g, ct=ct: e.dma_start(out=x[g][:, ct, :], in_=d_x[g][ct * 128:(ct + 1) * 128, :]),
                  writes=bx[g][ct])
    S.dve(lambda e: e.memset(ones_bf[:], 1.0), writes=[Bconst])
    S.dve(lambda e: e.memset(epsT[:], EPS), writes=[Bconst])
    S.dve(lambda e: e.memset(halfpi[:], math.pi / 2), writes=[Bconst])
    S.dve(lambda e: e.memset(finsb[:], 0.0), writes=[Bfin])
    S.act(lambda e: e.activation(out=scond[:], in_=cond[:].rearrange("p (a b) -> p a b", b=2), func=AF.Silu),
          reads=[Bconst], writes=[Bconst])

    def ts(tt):
        return slice(tt * 512, (tt + 1) * 512)

    def compute_mod(i):
        ph = CURPH[0]
        for c in range(8):
            wf, wfb = ph.wf32.next()
            wfv = wf[:, 0:6144].rearrange("p (k c) -> p k c", k=8)
            for hq, q_ in ((0, "sp"), (1, "act")):
                S.dma(q_, lambda e, wfv=wfv, c=c, hq=hq: e.dma_start(
                    out=wfv[:, hq * 4:(hq + 1) * 4, :],
                    in_=d_wada[i, hq * 512:(hq + 1) * 512, c * 768:(c + 1) * 768].rearrange("(k p) c -> p k c", p=128)),
                    writes=[wfb])
            wt, wb = ph.wbig.next()
            wv = wt[:, 0:6144].rearrange("p (k c) -> p k c", k=8)
            S.dve(lambda e, wt=wt, wf=wf: e.tensor_copy(out=wt[:, 0:6144], in_=wf[:, 0:6144]), reads=[wfb], writes=[wb])
            for t in range(6):
                m = c * 6 + t
                ps, pb = psrot.next()
                for kt in range(8):
                    S.pe(lambda e, ps=ps, wv=wv, t=t, kt=kt: e.matmul(
                        ps[:, 0:2], wv[:, kt, t * 128:(t + 1) * 128], scond[:, kt, :], start=(kt == 0), stop=(kt == 7)),
                        reads=[wb, Bconst], writes=[pb])
                S.act(lambda e, ps=ps, m=m: e.activation(out=modL[i][:, m, :], in_=ps[:, 0:2], func=AF.Identity,
                                                         bias=bada[:, i, m:m + 1]), reads=[pb, Bconst], writes=[Bmods[i]])

    def derive(i):
        for cj in range(2):
            S.dve(lambda e, cj=cj: e.scalar_tensor_tensor(out=A1[:, :, cj], in0=modL[i][:, 8:16, cj], scalar=1.0,
                                                          in1=gmix[:, i, :], op0=ALU.add, op1=ALU.mult),
                  reads=[Bmods[i], Bconst], writes=[Bder])
            S.dve(lambda e, cj=cj: e.scalar_tensor_tensor(out=A2[:, :, cj], in0=modL[i][:, 32:40, cj], scalar=1.0,
                                                          in1=gffn[:, i, :], op0=ALU.add, op1=ALU.mult),
                  reads=[Bmods[i], Bconst], writes=[Bder])
            if i % 2 == 1:
                S.dve(lambda e, cj=cj: e.tensor_tensor(out=g1b[:, :, cj], in0=modL[i][:, 16:24, cj], in1=bfour[:, i // 2, :],
                                                       op=ALU.mult), reads=[Bmods[i], Bconst], writes=[Bder])

    def mod_chunk_emitters(i):
        ems = []
        for m in range(48):
            def em(m=m):
                ph = CURPH[0]
                S.dma("pool", lambda e: e.dma_start(
                    out=ph.modw, in_=d_wada[i, :, m * 128:(m + 1) * 128].rearrange("(k p) c -> p k c", p=128)),
                    writes=[ph.Bmodw])
                ps, pb = psb[5 + (bu_rr[0] % 3)]
                bu_rr[0] += 1
                for kt in range(8):
                    S.pe(lambda e, kt=kt: e.matmul(ps[:, 0:2], ph.modw[:, kt, :], scond[:, kt, :], start=(kt == 0), stop=(kt == 7)),
                         reads=[ph.Bmodw, Bconst], writes=[pb])
                S.act(lambda e: e.activation(out=modL[i][:, m, :], in_=ps[:, 0:2], func=AF.Identity, bias=bada[:, i, m:m + 1]),
                      reads=[pb, Bconst], writes=[Bmods[i]])
            ems.append(em)
        return ems

    def norm(g, tts, scale_fn, bias_fn, out_fn, extra_reads, out_bufs_fn):
        ph = CURPH[0]
        for tt in tts:
            ps, pb = psrot.next()
            for ct in range(8):
                sq, sqbuf = ph.sqb.next()
                S.act(lambda e, sq=sq, ct=ct, tt=tt: e.activation(out=sq[:], in_=x[g][:, ct, ts(tt)], func=AF.Square),
                      reads=[bx[g][ct][tt]], writes=[sqbuf])
                S.pe(lambda e, ps=ps, sq=sq, ct=ct: e.matmul(ps[:], ones_bf[:], sq[:], start=(ct == 0), stop=(ct == 7)),
                     reads=[sqbuf, Bconst], writes=[pb])
            rsd, Brsd = ph.rsd.next()
            S.act(lambda e, ps=ps, rsd=rsd: e.activation(out=rsd[:, 0, :], in_=ps[:], func=AF.Sqrt, scale=1.0 / D_MODEL, bias=epsT[:]),
                  reads=[pb, Bconst], writes=[Brsd])
            S.dve(lambda e, rsd=rsd: e.reciprocal(out=rsd[:, 1, :], in_=rsd[:, 0, :]), reads=[Brsd], writes=[Brsd])
            for ct in range(8):
                tm, tb = ph.t512.next()
                S.dve(lambda e, tm=tm, ct=ct, tt=tt, rsd=rsd: e.tensor_tensor(out=tm[:], in0=x[g][:, ct, ts(tt)], in1=rsd[:, 1, :],
                                                                             op=ALU.mult),
                      reads=[bx[g][ct][tt], Brsd], writes=[tb])
                bias = bias_fn(ct)
                kw = {} if bias is None else {"bias": bias}
                oap = out_fn(ct, tt)
                sc = scale_fn(ct)
                S.act(lambda e, tm=tm, kw=kw, oap=oap, sc=sc: e.activation(out=oap, in_=tm[:], func=AF.Identity,
                                                                           scale=sc, **kw),
                      reads=[tb] + extra_reads, writes=out_bufs_fn(ct, tt))

    def norm_h(g, i, which, tts):
        cj = GROUPS[g]["cj"]
        A = A1 if which == 1 else A2
        sh0 = 0 if which == 1 else 24
        norm(g, tts, lambda ct: A[:, ct, cj:cj + 1], lambda ct: modL[i][:, sh0 + ct, cj:cj + 1],
             lambda ct, tt: h[g][:, ct, ts(tt)], [Bmods[i], Bder], lambda ct, tt: [bh[g][ct][tt]])

    def ffn_all(i):
        ph = CURPH[0]
        tiles = [("S", 0), ("S", 1), ("S", 2), ("S", 3), ("P", 0)]
        toff = {("S", 0): 0, ("S", 1): 512, ("S", 2): 1024, ("S", 3): 1536, ("P", 0): 2048}
        for g in GROUPS:
            norm_h(g, i, 2, list(range(GROUPS[g]["ntok"] // 512)))
        quarters = [(0, 6), (6, 6), (12, 5), (17, 5)]
        for (j0q, nhq) in quarters:
            a3 = ph.a_t[:, 0:nhq * 2560].rearrange("p (j t) -> p j t", j=nhq)
            for c0j in range(0, nhq, 2):
                ncj = min(2, nhq - c0j)
                wt, wb = ph.wu.next()
                for gv in range(2):
                    c0 = gv * D_FF + (j0q + c0j) * 128
                    S.dma("pool", lambda e, wt=wt, gv=gv, c0=c0, ncj=ncj: e.dma_start(
                        out=wt[:, :, gv, 0:ncj * 128], in_=d_wup[i, :, c0:c0 + ncj * 128].rearrange("(k p) c -> p k c", p=128)),
                        writes=[wb])
                for cj in range(ncj):
                    jl = c0j + cj
                    j = j0q + jl
                    for (g, tt) in tiles:
                        R = GROUPS[g]["R"]
                        pss = []
                        for gv in range(2):
                            ps, pb = psrot.next()
                            for kt in range(8):
                                S.pe(lambda e, ps=ps, wt=wt, gv=gv, kt=kt, tt=tt, g=g, cj=cj: e.matmul(
                                    ps[:], wt[:, kt, gv, cj * 128:(cj + 1) * 128], h[g][:, kt, ts(tt)],
                                    start=(kt == 0), stop=(kt == 7)), reads=[wb, bh[g][kt][tt]], writes=[pb])
                            pss.append((ps, pb))
                        outs = []
                        for gv in range(2):
                            ps, pb = pss[gv]
                            col = gv * NH + j
                            tm, tb = ph.t512.next()
                            S.act(lambda e, ps=ps, tm=tm, col=col: e.activation(
                                out=tm[:], in_=ps[:], func=AF.Identity, scale=ph.convw[:, i, 1, col:col + 1],
                                bias=ph.convb[:, i, col:col + 1]), reads=[pb, ph.Bcv], writes=[tb])
                            outs.append((tm, tb))
                        for kk_ in (0, 2):
                            for gv in range(2):
                                ps, pb = pss[gv]
                                tm, tb = outs[gv]
                                col = gv * NH + j
                                p3 = ps[:].rearrange("p (r t) -> p r t", t=R)
                                t3 = tm[:].rearrange("p (r t) -> p r t", t=R)
                                if kk_ == 0:
                                    S.dve(lambda e, p3=p3, t3=t3, col=col, R=R: e.scalar_tensor_tensor(
                                        out=t3[:, :, 1:R], in0=p3[:, :, 0:R - 1], scalar=ph.convw[:, i, 0, col:col + 1],
                                        in1=t3[:, :, 1:R], op0=ALU.mult, op1=ALU.add), reads=[pb, tb, ph.Bcv], writes=[tb])
                                else:
                                    S.dve(lambda e, p3=p3, t3=t3, col=col, R=R: e.scalar_tensor_tensor(
                                        out=t3[:, :, 0:R - 1], in0=p3[:, :, 1:R], scalar=ph.convw[:, i, 2, col:col + 1],
                                        in1=t3[:, :, 0:R - 1], op0=ALU.mult, op1=ALU.add), reads=[pb, tb, ph.Bcv], writes=[tb])
                        (gc, gb), (vc, vb) = outs
                        sg, sgb = ph.t512.next()
                        S.act(lambda e, sg=sg, gc=gc: e.activation(out=sg[:], in_=gc[:], func=AF.Silu),
                              reads=[gb], writes=[sgb])
                        to = toff[(g, tt)]
                        S.dve(lambda e, sg=sg, vc=vc, jl=jl, to=to, a3=a3: e.tensor_tensor(
                            out=a3[:, jl, to:to + 512], in0=sg[:], in1=vc[:], op=ALU.mult),
                            reads=[sgb, vb], writes=[ph.Ba[jl]])
            wt, wb = ph.wd.next()
            S.dma("pool", lambda e, wt=wt, j0q=j0q, nhq=nhq: e.dma_start(
                out=wt[:, 0:nhq, :], in_=d_wdown[i, j0q * 128:(j0q + nhq) * 128, :].rearrange("(j p) c -> p j c", p=128)),
                writes=[wb])
            for ct in range(8):
                for (g, tt) in tiles:
                    cjx = GROUPS[g]["cj"]
                    to = toff[(g, tt)]
                    ps, pb = psrot.next()
                    for jl in range(nhq):
                        S.pe(lambda e, ps=ps, wt=wt, jl=jl, to=to, ct=ct, a3=a3, nhq=nhq: e.matmul(
                            ps[:], wt[:, jl, ct * 128:(ct + 1) * 128], a3[:, jl, to:to + 512], start=(jl == 0),
                            stop=(jl == nhq - 1)), reads=[wb, ph.Ba[jl]], writes=[pb])
                    S.dve(lambda e, ps=ps, ct=ct, tt=tt, g=g, cjx=cjx: e.scalar_tensor_tensor(
                        out=x[g][:, ct, ts(tt)], in0=ps[:], scalar=modL[i][:, 40 + ct, cjx:cjx + 1],
                        in1=x[g][:, ct, ts(tt)], op0=ALU.mult, op1=ALU.add),
                        reads=[pb, Bmods[i], bx[g][ct][tt]], writes=[bx[g][ct][tt]])

    def fourier(g, i):
        ph = CURPH[0]
        G = GROUPS[g]
        cj = G["cj"]
        jf = i // 2
        ntok, L, nseq = G["ntok"], G["L"], G["nseq"]
        nlt = ntok // 128
        ntt = ntok // 512
        norm_h(g, i, 1, list(range(ntt)))
        lts = L // 128
        for q in range(4):
            for lt in range(nlt):
                for tb_ in range(2):
                    ps, pb = psrot.next()
                    for kk in range(2):
                        S.pe(lambda e, ps=ps, kk=kk, lt=lt, tb_=tb_, q=q: e.matmul(
                            ps[:, 0:256], h[g][:, 2 * q + kk, lt * 128:(lt + 1) * 128], ph.ck[:, tb_, kk, :],
                            start=(kk == 0), stop=(kk == 1)),
                            reads=[bh[g][2 * q + kk][lt // 4], ph.Bck], writes=[pb])
                    if tb_ == 0:
                        S.act(lambda e, ps=ps, lt=lt: e.activation(out=ph.pq[:, 0, lt, :], in_=ps[:, 0:256], func=AF.Copy),
                              reads=[pb], writes=[ph.Bpq])
                    else:
                        S.dve(lambda e, ps=ps, lt=lt: e.tensor_copy(out=ph.pq[:, 1, lt, :], in_=ps[:, 0:256]),
                              reads=[pb], writes=[ph.Bpq])
            for s in range(nseq):
                for lb in range(L // 256):
                    if g == "S":
                        wt, wb = ph.wbig.next()
                        tv = wt[:, :].rearrange("p (a l c) -> p a l c", a=2, l=16)
                        for tb_ in range(2):
                            S.dma("sp", lambda e, tv=tv, tb_=tb_, lb=lb: e.dma_start(
                                out=tv[:, tb_, :, :],
                                in_=d_cls[tb_, :, lb * 256:(lb + 1) * 256].rearrange("(l p) c -> p l c", p=128)),
                                writes=[wb])
                        tabs = lambda tb_, l, tv=tv: tv[:, tb_, l, :]
                        tbuf = wb
                    else:
                        tabs = lambda tb_, l: ph.clp[:, tb_, l, :]
                        tbuf = ph.Bck
                    for kk in range(2):
                        ps, pb = psrot.next()
                        n = 0
                        for tb_ in range(2):
                            for l in range(lts):
                                S.pe(lambda e, ps=ps, tb_=tb_, l=l, kk=kk, s=s, tabs=tabs, n=n: e.matmul(
                                    ps[:, 0:256], ph.pq[:, tb_, s * lts + l, kk * 128:(kk + 1) * 128], tabs(tb_, l),
                                    start=(n == 0), stop=(n == 2 * lts - 1)), reads=[ph.Bpq, tbuf], writes=[pb])
                                n += 1
                        tok0 = s * L + lb * 256
                        S.act(lambda e, ps=ps, kk=kk, tok0=tok0, q=q: e.activation(
                            out=h[g][:, 2 * q + kk, tok0:tok0 + 256], in_=ps[:, 0:256], func=AF.Copy),
                            reads=[pb], writes=[bh[g][2 * q + kk][tok0 // 512]])
        for oc in range(8):
            wt, wb = ph.wbig.next()
            wv = wt[:, 0:1024].rearrange("p (k c) -> p k c", k=8)
            S.dma("pool", lambda e, wv=wv, oc=oc: e.dma_start(
                out=wv, in_=d_wfour[jf, :, oc * 128:(oc + 1) * 128].rearrange("(k p) c -> p k c", p=128)), writes=[wb])
            for tt in range(ntt):
                ps, pb = psrot.next()
                for kt in range(8):
                    S.pe(lambda e, ps=ps, wv=wv, kt=kt, tt=tt: e.matmul(
                        ps[:], wv[:, kt, :], h[g][:, kt, ts(tt)], start=(kt == 0), stop=(kt == 7)),
                        reads=[wb, bh[g][kt][tt]], writes=[pb])
                tm, tb = ph.t512.next()
                S.act(lambda e, ps=ps, tm=tm, oc=oc: e.activation(
                    out=tm[:], in_=ps[:], func=AF.Identity, scale=modL[i][:, 16 + oc, cj:cj + 1],
                    bias=g1b[:, oc, cj:cj + 1]), reads=[pb, Bmods[i], Bder], writes=[tb])
                S.dve(lambda e, tm=tm, oc=oc, tt=tt: e.tensor_tensor(
                    out=x[g][:, oc, ts(tt)], in0=x[g][:, oc, ts(tt)], in1=tm[:], op=ALU.add),
                    reads=[tb, bx[g][oc][tt]], writes=[bx[g][oc][tt]])

    TWO_PI = 2.0 * math.pi

    def cexp_ops(xr, xi, W, are, aim, T, bufs):
        ph = CURPH[0]
        rw = bufs
        t0, t1, t2, t3 = [t[:, 0:W] for t in T[:4]]
        ti = T[2][:, 0:W].bitcast(I32)
        S.act(lambda e: e.activation(out=t0, in_=xr, func=AF.Exp), reads=rw, writes=rw)
        S.dve(lambda e: e.tensor_scalar(out=ti, in0=xi, scalar1=1.0 / TWO_PI, scalar2=None, op0=ALU.mult),
              reads=rw, writes=rw)
        S.dve(lambda e: e.tensor_copy(out=t1, in_=ti), reads=rw, writes=rw)
        S.dve(lambda e: e.scalar_tensor_tensor(out=t1, in0=t1, scalar=-TWO_PI, in1=xi, op0=ALU.mult, op1=ALU.add),
              reads=rw, writes=rw)
        S.dve(lambda e: e.tensor_scalar(out=t1, in0=t1, scalar1=math.pi, scalar2=-math.pi, op0=ALU.min, op1=ALU.max),
              reads=rw, writes=rw)
        S.act(lambda e: e.activation(out=t2, in_=t1, func=AF.Sin), reads=rw, writes=rw)
        S.act(lambda e: e.activation(out=t3, in_=t1, func=AF.Abs), reads=rw, writes=rw)
        S.act(lambda e: e.activation(out=t3, in_=t3, func=AF.Sin, scale=-1.0, bias=halfpi[:]), reads=rw + [Bconst], writes=rw)
        S.dve(lambda e: e.tensor_tensor(out=are, in0=t0, in1=t3, op=ALU.mult), reads=rw, writes=rw)
        S.dve(lambda e: e.tensor_tensor(out=aim, in0=t0, in1=t2, op=ALU.mult), reads=rw, writes=rw)

    def s5_qprep(j):
        ph = CURPH[0]
        rw = [ph.Bq, ph.Bprep, Bconst]
        sl = slice(j * 64, (j + 1) * 64)
        T = ph.tmpq
        dtv = T[4][:, 0:64]
        xr = T[5][:, 0:64]
        xi = T[6][:, 0:64]
        S.act(lambda e: e.activation(out=dtv, in_=s5q[:, 2, sl], func=AF.Exp), reads=rw, writes=rw)
        S.dve(lambda e: e.tensor_tensor(out=xr, in0=s5q[:, 0, sl], in1=dtv, op=ALU.mult), reads=rw, writes=rw)
        S.dve(lambda e: e.tensor_tensor(out=xi, in0=s5q[:, 1, sl], in1=dtv, op=ALU.mult), reads=rw, writes=rw)
        cexp_ops(xr, xi, 64, ph.pwre[:, :, 0], ph.pwim[:, :, 0], T, rw)
        tA = T[0][:, 0:64]
        tB = T[1][:, 0:64]
        for k in range(10):
            S.dve(lambda e, k=k: e.tensor_tensor(out=tA, in0=ph.pwre[:, :, k], in1=ph.pwre[:, :, k], op=ALU.mult), reads=rw, writes=rw)
            S.dve(lambda e, k=k: e.tensor_tensor(out=tB, in0=ph.pwim[:, :, k], in1=ph.pwim[:, :, k], op=ALU.mult), reads=rw, writes=rw)
            S.dve(lambda e, k=k: e.tensor_tensor(out=ph.pwre[:, :, k + 1], in0=tA, in1=tB, op=ALU.subtract), reads=rw, writes=rw)
            S.dve(lambda e, k=k: e.scalar_tensor_tensor(out=ph.pwim[:, :, k + 1], in0=ph.pwre[:, :, k], scalar=2.0,
                                                        in1=ph.pwim[:, :, k], op0=ALU.mult, op1=ALU.mult), reads=rw, writes=rw)
        S.dve(lambda e: e.tensor_scalar(out=ph.pwimn[:], in0=ph.pwim[:], scalar1=-1.0, scalar2=None, op0=ALU.mult),
              reads=rw, writes=rw)
        S.dve(lambda e: e.tensor_tensor(out=tA, in0=ph.pwre[:, :, 0], in1=s5q[:, 3, sl], op=ALU.mult), reads=rw, writes=rw)
        S.dve(lambda e: e.tensor_tensor(out=tB, in0=ph.pwim[:, :, 0], in1=s5q[:, 4, sl], op=ALU.mult), reads=rw, writes=rw)
        S.dve(lambda e: e.tensor_tensor(out=ph.injre[:], in0=tA, in1=tB, op=ALU.subtract), reads=rw, writes=rw)
        S.dve(lambda e: e.tensor_tensor(out=tA, in0=ph.pwre[:, :, 0], in1=s5q[:, 4, sl], op=ALU.mult), reads=rw, writes=rw)
        S.dve(lambda e: e.tensor_tensor(out=tB, in0=ph.pwim[:, :, 0], in1=s5q[:, 3, sl], op=ALU.mult), reads=rw, writes=rw)
        S.dve(lambda e: e.tensor_tensor(out=ph.injim[:], in0=tA, in1=tB, op=ALU.add), reads=rw, writes=rw)
        S.dve(lambda e: e.tensor_tensor(out=tA, in0=ph.pwre[:, :, 0], in1=ph.injre[:], op=ALU.mult), reads=rw, writes=rw)
        S.dve(lambda e: e.tensor_tensor(out=tB, in0=ph.pwim[:, :, 0], in1=ph.injim[:], op=ALU.mult), reads=rw, writes=rw)
        S.dve(lambda e: e.tensor_tensor(out=ph.inj2re[:], in0=tA, in1=tB, op=ALU.subtract), reads=rw, writes=rw)
        S.dve(lambda e: e.tensor_tensor(out=tA, in0=ph.pwre[:, :, 0], in1=ph.injim[:], op=ALU.mult), reads=rw, writes=rw)
        S.dve(lambda e: e.tensor_tensor(out=tB, in0=ph.pwim[:, :, 0], in1=ph.injre[:], op=ALU.mult), reads=rw, writes=rw)
        S.dve(lambda e: e.tensor_tensor(out=ph.inj2im[:], in0=tA, in1=tB, op=ALU.add), reads=rw, writes=rw)

    def s5_prep(j, ct, d):
        ph = CURPH[0]
        rw = [ph.Bprep]
        S.dma("sp", lambda e: e.dma_start(
            out=ph.tab[:], in_=d_s5tab[j, ct].rearrange("p (d t c) -> p d t c", d=2, t=5)[:, d]), writes=[ph.Btab])
        S.dve(lambda e: e.memset(dummy[:], 0.0), reads=[ph.Btab], writes=[ph.Bprep])
        T = ph.tmpq
        W = 256
        lamre, lamim, logdt, bre, bim = [ph.tab[:, k, :] for k in range(5)]
        dtv, xr, xi, are, aim = T[4][:, :], T[5][:, :], T[6][:, :], T[7][:, :], T[4][:, :]
        S.act(lambda e: e.activation(out=dtv, in_=logdt, func=AF.Exp), reads=rw, writes=rw)
        S.dve(lambda e: e.tensor_tensor(out=xr, in0=lamre, in1=dtv, op=ALU.mult), reads=rw, writes=rw)
        S.dve(lambda e: e.tensor_tensor(out=xi, in0=lamim, in1=dtv, op=ALU.mult), reads=rw, writes=rw)
        cexp_ops(xr, xi, W, are, aim, T, rw)
        nr, den, cr, ci, t5 = T[7][:, :], T[0][:, :], T[1][:, :], T[2][:, :], T[3][:, :]
        S.dve(lambda e: e.tensor_scalar(out=nr, in0=are, scalar1=-1.0, scalar2=None, op0=ALU.add), reads=rw, writes=rw)
        S.dve(lambda e: e.tensor_tensor(out=den, in0=lamre, in1=lamre, op=ALU.mult), reads=rw, writes=rw)
        S.dve(lambda e: e.tensor_tensor(out=t5, in0=lamim, in1=lamim, op=ALU.mult), reads=rw, writes=rw)
        S.dve(lambda e: e.tensor_tensor(out=den, in0=den, in1=t5, op=ALU.add), reads=rw, writes=rw)
        S.dve(lambda e: e.reciprocal(out=den, in_=den), reads=rw, writes=rw)
        S.dve(lambda e: e.tensor_tensor(out=cr, in0=nr, in1=lamre, op=ALU.mult), reads=rw, writes=rw)
        S.dve(lambda e: e.tensor_tensor(out=t5, in0=aim, in1=lamim, op=ALU.mult), reads=rw, writes=rw)
        S.dve(lambda e: e.tensor_tensor(out=cr, in0=cr, in1=t5, op=ALU.add), reads=rw, writes=rw)
        S.dve(lambda e: e.tensor_tensor(out=cr, in0=cr, in1=den, op=ALU.mult), reads=rw, writes=rw)
        S.dve(lambda e: e.tensor_tensor(out=ci, in0=aim, in1=lamre, op=ALU.mult), reads=rw, writes=rw)
        S.dve(lambda e: e.tensor_tensor(out=t5, in0=nr, in1=lamim, op=ALU.mult), reads=rw, writes=rw)
        S.dve(lambda e: e.tensor_tensor(out=ci, in0=ci, in1=t5, op=ALU.subtract), reads=rw, writes=rw)
        S.dve(lambda e: e.tensor_tensor(out=ci, in0=ci, in1=den, op=ALU.mult), reads=rw, writes=rw)
        u0, u1 = T[5][:, :], T[6][:, :]
        b32r, b32i = T[0][:, :], T[3][:, :]
        S.dve(lambda e: e.tensor_tensor(out=u0, in0=cr, in1=bre, op=ALU.mult), reads=rw, writes=rw)
        S.dve(lambda e: e.tensor_tensor(out=u1, in0=ci, in1=bim, op=ALU.mult), reads=rw, writes=rw)
        S.dve(lambda e: e.tensor_tensor(out=b32r, in0=u0, in1=u1, op=ALU.subtract), reads=rw, writes=rw)
        S.dve(lambda e: e.tensor_tensor(out=u0, in0=cr, in1=bim, op=ALU.mult), reads=rw, writes=rw)
        S.dve(lambda e: e.tensor_tensor(out=u1, in0=ci, in1=bre, op=ALU.mult), reads=rw, writes=rw)
        S.dve(lambda e: e.tensor_tensor(out=b32i, in0=u0, in1=u1, op=ALU.add), reads=rw + [ph.Btab], writes=rw)
        S.act(lambda e: e.activation(out=ph.Bblk[:, 0, 0, :], in_=b32r, func=AF.Copy), reads=rw, writes=rw)
        S.act(lambda e: e.activation(out=ph.Bblk[:, 0, 1, :], in_=b32i, func=AF.Copy), reads=rw, writes=rw)
        S.dve(lambda e: e.tensor_tensor(out=u0, in0=nr, in1=b32r, op=ALU.mult), reads=rw, writes=rw)
        S.dve(lambda e: e.tensor_tensor(out=u0, in0=u0, in1=b32r, op=ALU.add), reads=rw, writes=rw)
        S.dve(lambda e: e.tensor_tensor(out=u1, in0=aim, in1=b32i, op=ALU.mult), reads=rw, writes=rw)
        S.dve(lambda e: e.tensor_tensor(out=ph.Bblk[:, 1, 0, :], in0=u0, in1=u1, op=ALU.subtract), reads=rw, writes=rw)
        S.dve(lambda e: e.tensor_tensor(out=u0, in0=nr, in1=b32i, op=ALU.mult), reads=rw, writes=rw)
        S.dve(lambda e: e.tensor_tensor(out=u0, in0=u0, in1=b32i, op=ALU.add), reads=rw, writes=rw)
        S.dve(lambda e: e.tensor_tensor(out=u1, in0=aim, in1=b32r, op=ALU.mult), reads=rw, writes=rw)
        S.dve(lambda e: e.tensor_tensor(out=ph.Bblk[:, 1, 1, :], in0=u0, in1=u1, op=ALU.add), reads=rw, writes=rw)

    bu_rr = [0]
    tile_no = [0]
    pending = []
    carry = {"h2": [], "post": None, "evac": None}

    def s5_flush():
        for fn, rd, wr in carry["h2"]:
            if rd is None:
                fn()
            else:
                S.dve(fn, reads=rd, writes=wr)
        carry["post"]()
        carry["evac"]()
        carry["h2"], carry["post"], carry["evac"] = [], None, None

    def s5_main_ct(j, ct):
        ph = CURPH[0]
        psy = {"S": psb[0:4], "P": psb[4:5]}
        nmm = {"S": 0, "P": 0}

        def tile(g, d, gl):
            G = GROUPS[g]
            ntok, L, nseq = G["ntok"], G["L"], G["nseq"]
            ntt = ntok // 512
            nlev = int(math.log2(L))
            gp = ct * 4 + gl
            half, gpl = gl // 2, gl % 2
            tix = d * 32 + gp
            hs = slice(64 * half, 64 * half + 64)
            if g == "S":
                k_ = tile_no[0] % 2
                tile_no[0] += 1
                xs, Bset, Bset2 = ph.XS[k_], ph.BXS[k_], ph.BXS2[k_]
                Hb, BHb = ph.HbS, ph.BHbS
            else:
                xs, Bset, Bset2 = ph.XP, ph.BXP, ph.BXP2
                Hb, BHb = ph.HbP, ph.BHbP
            head = []
            qraw, qpair = (0, 1) if d == 0 else (1, 0)
            cs = slice(gpl * 128, (gpl + 1) * 128)
            for r in range(2):
                for tt in range(ntt):
                    ps, pb = psb[5 + (bu_rr[0] % 3)]
                    bu_rr[0] += 1
                    hv = h[g][hs, ct, ts(tt)].rearrange("p (m q) -> p m q", q=2)
                    xv = xs[:, r, ts(tt)].rearrange("p (m q) -> p m q", q=2)
                    rd_ = [ph.Bprep, bh[g][ct][tt]]
                    S.pe(lambda e, ps=ps, r=r, hv=hv: e.matmul(ps[:, 0:256], ph.Bblk[hs, 0, r, cs], hv[:, :, qraw],
                                                               start=True, stop=True), reads=rd_, writes=[pb])
                    S.pe(lambda e, ps=ps, r=r, hv=hv: e.matmul(ps[:, 256:512], ph.Bblk[hs, 0, r, cs], hv[:, :, qpair],
                                                               start=True, stop=False), reads=rd_, writes=[pb])
                    S.pe(lambda e, ps=ps, r=r, hv=hv: e.matmul(ps[:, 256:512], ph.Bblk[hs, 1, r, cs], hv[:, :, qraw],
                                                               start=False, stop=True), reads=rd_, writes=[pb])
                    S.act(lambda e, ps=ps, xv=xv: e.activation(out=xv[:, :, qraw], in_=ps[:, 0:256], func=AF.Copy),
                          reads=[pb], writes=[(Bset, Bset2)[r]])
                    S.act(lambda e, ps=ps, xv=xv: e.activation(out=xv[:, :, qpair], in_=ps[:, 256:512], func=AF.Copy),
                          reads=[pb], writes=[(Bset, Bset2)[r]])
            if pending:
                pending.pop(0)()
            col = 0 if d == 0 else L - 1
            if g == "S":
                col2 = 1 if d == 0 else L - 2
                for r, inj, c_ in ((0, ph.injre, col), (1, ph.injim, col), (0, ph.inj2re, col2), (1, ph.inj2im, col2)):
                    head.append((lambda e, r=r, inj=inj, c_=c_: e.tensor_tensor(
                        out=xs[:, r, c_:c_ + 1], in0=xs[:, r, c_:c_ + 1], in1=inj[:, tix:tix + 1], op=ALU.add),
                        [(Bset, Bset2)[r], ph.Bq], [(Bset, Bset2)[r]]))
            else:
                for r in range(2):
                    o = ((j * 2 + d) * 32 + gp) * 4 + r * 2
                    src = xs[:, r, 0:ntok].rearrange("p (s t) -> p s t", s=nseq)[:, :, col]
                    head.append((lambda e, o=o, src=src: e.tensor_copy(out=finsb[:, o:o + 2], in_=src),
                                 [(Bset, Bset2)[r]], [Bfin]))

            def level(k, down):
                dd = 1 << k
                M = L // (2 * dd)
                v = xs[:, :, 0:ntok].rearrange("p r (s m q) -> p r s m q", s=nseq, q=2 * dd)
                if d == 0:
                    if not down:
                        tg, sr, n = v[:, :, :, :, 2 * dd - 1], v[:, :, :, :, dd - 1], M
                    else:
                        tg, sr, n = v[:, :, :, 1:M, dd - 1], v[:, :, :, 0:M - 1, 2 * dd - 1], M - 1
                else:
                    if not down:
                        tg, sr, n = v[:, :, :, :, 0], v[:, :, :, :, dd], M
                    else:
                        tg, sr, n = v[:, :, :, 0:M - 1, dd], v[:, :, :, 1:M, 0], M - 1
                if n == 0:
                    return
                pr = ph.pwre[:, tix, k:k + 1]
                pi_ = ph.pwim[:, tix, k:k + 1]
                pin = ph.pwimn[:, tix, k:k + 1]
                scan.append((lambda e: e.scalar_tensor_tensor(out=tg, in0=sr, scalar=pr, in1=tg, op0=ALU.mult, op1=ALU.add),
                             [Bset, Bset2, ph.Bq], [Bset, Bset2]))
                scan.append((lambda e: e.scalar_tensor_tensor(out=tg[:, 0], in0=sr[:, 1], scalar=pin, in1=tg[:, 0],
                                                              op0=ALU.mult, op1=ALU.add), [Bset, ph.Bq], [Bset]))
                scan.append((lambda e: e.scalar_tensor_tensor(out=tg[:, 1], in0=sr[:, 0], scalar=pi_, in1=tg[:, 1],
                                                              op0=ALU.mult, op1=ALU.add), [Bset2, ph.Bq], [Bset2]))

            scan = list(head)
            for k in range(1, nlev):
                level(k, False)
            for k in range(nlev - 2, -1, -1):
                level(k, True)

            def post():
                S.act(lambda e: e.activation(out=Hb[0][:, 0:ntok], in_=xs[:, 0, 0:ntok], func=AF.Copy),
                      reads=[Bset, Bset2], writes=[BHb[0]])
                S.act(lambda e: e.activation(out=Hb[1][:, 0:ntok], in_=xs[:, 1, 0:ntok], func=AF.Copy, scale=-1.0),
                      reads=[Bset, Bset2], writes=[BHb[1]])
                for r in range(2):
                    for tt in range(ntt):
                        ps, pb = psy[g][tt]
                        S.pe(lambda e, ps=ps, r=r, tt=tt, first=(nmm[g] == 0), last=(nmm[g] == 15): e.matmul(
                            ps[:], ph.Cb[:, d, r, gl, :], Hb[r][:, ts(tt)], start=first, stop=last),
                            reads=[ph.BCb, BHb[r]], writes=[pb])
                    nmm[g] += 1
            return scan, post

        def merge2(a, b):
            out = []
            ia = ib = 0
            while ia < len(a) or ib < len(b):
                if ia < len(a) and (ib >= len(b) or ia * len(b) <= ib * len(a)):
                    out.append(a[ia]); ia += 1
                else:
                    out.append(b[ib]); ib += 1
            return out

        def emit(lst):
            for fn, rd, wr in lst:
                if rd is None:
                    fn()
                else:
                    S.dve(fn, reads=rd, writes=wr)

        def load_cb():
            S.dma("pool", lambda e: e.dma_start(
                out=ph.Cb[:], in_=d_s5c[j, ct].rearrange("p (d r g c) -> p d r g c", d=2, r=2, g=4)), writes=[ph.BCb])

        if carry["post"] is None:
            load_cb()

        def capture_prep(ct_, d_):
            cap = []
            orig_add = S.add

            def fake_add(eng, fn, reads=(), writes=(), dma=False):
                cap.append((lambda eng=eng, fn=fn, reads=reads, writes=writes, dma=dma: orig_add(eng, fn, reads, writes, dma),
                            None, None))
            S.add = fake_add
            try:
                s5_prep(j, ct_, d_)
            finally:
                S.add = orig_add
            return cap

        prev_h2, prev_post = carry["h2"], carry["post"]
        carried = carry["post"] is not None
        for d in range(2):
            for gl in range(4):
                if gl == 0 and d == 0 and ct == 0:
                    s5_prep(j, ct, d)
                sP, postP = tile("P", d, gl)
                sS, postS = tile("S", d, gl)
                prep_next = []
                if gl == 3 and d == 0:
                    prep_next = capture_prep(ct, 1)
                elif gl == 3 and d == 1 and ct < 7:
                    prep_next = capture_prep(ct + 1, 0)
                half = len(sS) // 2
                h1, h2 = sS[:half], sS[half:]
                n2 = len(prev_h2)
                a_, b_ = int(n2 * 0.25), int(n2 * 0.65)
                np_ = int(len(sP) * 0.4)
                emit(prev_h2[:a_])
                emit(merge2(prev_h2[a_:b_], sP[:np_]))
                emit(merge2(merge2(merge2(prev_h2[b_:], h1), sP[np_:]), prep_next))
                if carried and d == 0 and gl == 0:
                    prev_post()
                    carry["evac"]()
                    load_cb()
                    postP()
                else:
                    postP()
                    if prev_post is not None:
                        prev_post()
                prev_h2, prev_post = h2, postS
        carry["h2"], carry["post"] = prev_h2, prev_post

        def evac_fn():
          for g in ("P", "S"):
              for tt in range(GROUPS[g]["ntok"] // 512):
                  ps, pb = psy[g][tt]
                  tm, tb = ph.t512.next()
                  S.dve(lambda e, ps=ps, tm=tm, tt=tt, g=g: e.scalar_tensor_tensor(
                      out=tm[:], in0=h[g][:, ct, ts(tt)], scalar=ssmd[:, j, ct:ct + 1], in1=ps[:], op0=ALU.mult, op1=ALU.add),
                      reads=[pb, bh[g][ct][tt], Bconst], writes=[tb])
                  S.act(lambda e, tm=tm, tt=tt, g=g: e.activation(out=h[g][:, ct, ts(tt)], in_=tm[:], func=AF.Gelu_apprx_tanh),
                        reads=[tb], writes=[bh[g][ct][tt]])
        carry["evac"] = evac_fn

    def s5_glu(g, i):
        ph = CURPH[0]
        G = GROUPS[g]
        cj = G["cj"]
        j = i // 2
        ntt = G["ntok"] // 512
        for oc in range(8):
            wt, wb = ph.wbig.next()
            wv = wt[:, 0:2048].rearrange("p (k z c) -> p k z c", k=8, z=2)
            for z in range(2):
                c0 = z * 1024 + oc * 128
                S.dma("pool", lambda e, wv=wv, z=z, c0=c0: e.dma_start(
                    out=wv[:, :, z, :], in_=d_wglu[j, :, c0:c0 + 128].rearrange("(k p) c -> p k c", p=128)), writes=[wb])
            for tt in range(ntt):
                pss = []
                for z in range(2):
                    ps, pb = psrot.next()
                    for kt in range(8):
                        S.pe(lambda e, ps=ps, wv=wv, z=z, kt=kt, tt=tt: e.matmul(
                            ps[:], wv[:, kt, z, :], h[g][:, kt, ts(tt)], start=(kt == 0), stop=(kt == 7)),
                            reads=[wb, bh[g][kt][tt]], writes=[pb])
                    pss.append((ps, pb))
                s2, s2b = ph.t512.next()
                S.act(lambda e, s2=s2, ps=pss[1][0], oc=oc: e.activation(
                    out=s2[:], in_=ps[:], func=AF.Sigmoid, bias=bglu[:, j, 8 + oc:9 + oc]),
                    reads=[pss[1][1], Bconst], writes=[s2b])
                S.dve(lambda e, s2=s2, ps=pss[0][0], oc=oc: e.scalar_tensor_tensor(
                    out=s2[:], in0=ps[:], scalar=bglu[:, j, oc:oc + 1], in1=s2[:], op0=ALU.add, op1=ALU.mult),
                    reads=[pss[0][1], s2b, Bconst], writes=[s2b])
                S.dve(lambda e, s2=s2, oc=oc, tt=tt: e.scalar_tensor_tensor(
                    out=x[g][:, oc, ts(tt)], in0=s2[:], scalar=modL[i][:, 16 + oc, cj:cj + 1], in1=x[g][:, oc, ts(tt)],
                    op0=ALU.mult, op1=ALU.add), reads=[s2b, Bmods[i], bx[g][oc][tt]], writes=[bx[g][oc][tt]])

    def s5_layer(i):
        j = i // 2
        begin_phase("s5m")
        s5_qprep(j)
        for li in ((1, 2) if i == 0 else (3,)):
            pending.extend(mod_chunk_emitters(li))
        for ct in range(8):
            s5_main_ct(j, ct)
        s5_flush()
        while pending:
            pending.pop(0)()
        check_stop(f"s5main{i}")
        begin_phase("s5g")
        for g in GROUPS:
            s5_glu(g, i)

    try:
        for i in range(DEPTH):
            if i == 0:
                begin_phase("mod")
                compute_mod(i)
            check_stop(f"mod{i}")
            if i % 2 == 0:
                begin_phase("norm")
                derive(i)
                for g in GROUPS:
                    norm_h(g, i, 1, list(range(GROUPS[g]["ntok"] // 512)))
                check_stop(f"norm{i}")
                s5_layer(i)
            else:
                begin_phase("four")
                derive(i)
                for g in GROUPS:
                    fourier(g, i)
            check_stop(f"mix{i}")
            begin_phase("ffn")
            ffn_all(i)
            check_stop(f"ffn{i}")
    except _Stop:
        pass
    out_ops = set()
    if stop is not None and stop.startswith("s5prep"):
        ph = CURPH[0]
        dB = dout("dbgB", [128, 1024])
        dT = dout("dbgT", [128, 8 * 256])
        dTab = dout("dbgTab", [128, 1280])
        dPw = dout("dbgPw", [128, 2 * 704])
        rw = [ph.Bprep, ph.Bq]
        op = S.dma("pool", lambda e: e.dma_start(out=dB, in_=ph.Bblk[:].rearrange("p a b c -> p (a b c)")), reads=rw); out_ops.add(op.idx)
        for k in range(8):
            op = S.dma("sp", lambda e, k=k: e.dma_start(out=dT[:, k * 256:(k + 1) * 256], in_=ph.tmpq[k][:]), reads=rw); out_ops.add(op.idx)
        op = S.dma("sp", lambda e: e.dma_start(out=dTab, in_=ph.tab[:].rearrange("p a b -> p (a b)")), reads=rw); out_ops.add(op.idx)
        op = S.dma("sp", lambda e: e.dma_start(out=dPw[:, 0:704], in_=ph.pwre[:].rearrange("p a b -> p (a b)")), reads=rw); out_ops.add(op.idx)
        op = S.dma("sp", lambda e: e.dma_start(out=dPw[:, 704:1408], in_=ph.pwim[:].rearrange("p a b -> p (a b)")), reads=rw); out_ops.add(op.idx)
        S.emit(final_wait_ops=out_ops)
        es.close()
        return nc

    ph = begin_phase("final")
    for g in GROUPS:
        ntt = GROUPS[g]["ntok"] // 512
        outs = {}

        def out_fn(ct, tt, outs=outs):
            tm, tb = ph.outrot.next()
            outs[(ct, tt)] = (tm, tb)
            return tm[:]

        for tt in range(ntt):
            norm(g, [tt], lambda ct: gfin[:, ct:ct + 1], lambda ct: None, out_fn, [Bconst],
                 lambda ct, tt, outs=outs: [outs[(ct, tt)][1]])
            for ct in range(8):
                tm, tb = outs[(ct, tt)]
                op = S.dma("sp", lambda e, tm=tm, ct=ct, tt=tt, g=g: e.dma_start(
                    out=d_y[g][ct * 128:(ct + 1) * 128, ts(tt)], in_=tm[:]), reads=[tb])
                out_ops.add(op.idx)
    for blk in range(4):
        ps, pb = psrot.next()
        S.pe(lambda e, ps=ps, blk=blk: e.transpose(ps[:, 0:128], finsb[:, blk * 128:(blk + 1) * 128], identf[:]),
             reads=[Bfin, Bconst], writes=[pb])
        tm, tb = ph.t512.next()
        S.act(lambda e, ps=ps, tm=tm: e.activation(out=tm[:, 0:128], in_=ps[:, 0:128], func=AF.Copy),
              reads=[pb], writes=[tb])
        op = S.dma("sp", lambda e, tm=tm, blk=blk: e.dma_start(out=d_fin[blk * 128:(blk + 1) * 128, :], in_=tm[:, 0:128]),
                   reads=[tb])
        out_ops.add(op.idx)

    if stop is not None:
        begin_phase("final")
        d_hd = {g: dout("hd" + g, [1024, GROUPS[g]["ntok"]]) for g in GROUPS}
        d_xd = {g: dout("xd" + g, [1024, GROUPS[g]["ntok"]]) for g in GROUPS}
        d_modo = dout("modo", [128, 96])
        allb = [b for g in GROUPS for ct in range(8) for b in bh[g][ct]] + [b for g in GROUPS for ct in range(8) for b in bx[g][ct]]
        for g in GROUPS:
            for ct in range(8):
                op = S.dma("pool", lambda e, g=g, ct=ct: e.dma_start(out=d_hd[g][ct * 128:(ct + 1) * 128, :], in_=h[g][:, ct, :]),
                           reads=allb)
                out_ops.add(op.idx)
                op = S.dma("sp", lambda e, g=g, ct=ct: e.dma_start(out=d_xd[g][ct * 128:(ct + 1) * 128, :], in_=x[g][:, ct, :]),
                           reads=allb)
                out_ops.add(op.idx)
        op = S.dma("sp", lambda e: e.dma_start(out=d_modo, in_=modL[0].rearrange("p a b -> p (a b)")), reads=[Bmods[0]])
        out_ops.add(op.idx)
    S.emit(final_wait_ops=out_ops)
    es.close()
    return nc


def _fm(v):
    v = np.asarray(v, np.float32)
    lead = v.shape[:-1]
    n = v.shape[-1] // 128
    r = v.reshape(lead + (n, 128))
    r = np.moveaxis(r, -1, 0)
    return np.ascontiguousarray(r.reshape(128, -1))


def _dft_tables():
    import ml_dtypes
    bf = ml_dtypes.bfloat16
    k = np.arange(256, dtype=np.float64)
    ang = 2 * np.pi * np.outer(k, k) / 256.0
    s1 = 1.0 / 16.0
    ckc = (np.cos(ang) * s1).reshape(2, 128, 256)
    cks = (-np.sin(ang) * s1).reshape(2, 128, 256)
    ck = np.stack([ckc, cks], 0)
    ck = np.ascontiguousarray(ck.transpose(2, 0, 1, 3).reshape(128, -1)).astype(np.float32)
    sp = 1.0 / 16.0
    clpc = (np.cos(ang) * sp).reshape(2, 128, 256)
    clps = (np.sin(ang) * sp).reshape(2, 128, 256)
    clp = np.stack([clpc, clps], 0)
    clp = np.ascontiguousarray(clp.transpose(2, 0, 1, 3).reshape(128, -1)).astype(np.float32)
    l = np.arange(2048, dtype=np.int64)
    m = np.outer(l, l) % 2048
    angL = 2 * np.pi * m.astype(np.float64) / 2048.0
    ss = 1.0 / math.sqrt(2048.0)
    cls = np.stack([np.cos(angL) * ss, np.sin(angL) * ss], 0).astype(np.float32).astype(bf)
    return ck, clp, cls


_CACHE = {}


def kernel(x_prompt, x_sample, state_ssm_re, state_ssm_im, c, c_ctx,
           w_ada, b_ada, g_mix, g_ffn,
           ssm_lam_re, ssm_lam_im, ssm_log_dt, ssm_b_re, ssm_b_im, ssm_c_re, ssm_c_im, ssm_d,
           w_glu, b_glu, w_fourier, b_fourier,
           w_up, conv_w, conv_b, w_down, g_final):
    f32 = np.float32
    A = lambda v: np.ascontiguousarray(np.asarray(v, f32))
    if "nc" not in _CACHE:
        _CACHE["nc"] = build_program()
        _CACHE["dft"] = _dft_tables()
    nc = _CACHE["nc"]
    ck, clp, cls = _CACHE["dft"]

    lam_re, lam_im, log_dt = A(ssm_lam_re), A(ssm_lam_im), A(ssm_log_dt)
    b_re, b_im, c_re, c_im = A(ssm_b_re), A(ssm_b_im), A(ssm_c_re), A(ssm_c_im)
    s5tab = np.zeros((2, 8, 128, 2, 5, 2, 2, 64), f32)
    s5c = np.zeros((2, 8, 128, 2, 2, 4, 128), f32)
    for ct in range(8):
        for gi in range(8):
            g = ct * 8 + gi
            half, gpl, gpar = gi // 4, (gi % 4) // 2, gi % 2
            gl = gi // 2
            rows = slice(gi * 16, gi * 16 + 16)
            s5tab[:, ct, rows, :, 3, gpl, gpar, :] = np.transpose(b_re[:, :, g], (0, 3, 1, 2))
            s5tab[:, ct, rows, :, 4, gpl, gpar, :] = np.transpose(b_im[:, :, g], (0, 3, 1, 2))
            prow = slice(gpar * 64, gpar * 64 + 64)
            s5c[:, ct, prow, :, 0, gl, gi * 16:gi * 16 + 16] = np.transpose(c_re[:, :, g], (0, 3, 1, 2))
            s5c[:, ct, prow, :, 1, gl, gi * 16:gi * 16 + 16] = np.transpose(c_im[:, :, g], (0, 3, 1, 2))
        for hf in range(2):
            for gpl in range(2):
                for gpar in range(2):
                    g = ct * 8 + hf * 4 + gpl * 2 + gpar
                    rows = slice(hf * 64, hf * 64 + 64)
                    s5tab[:, ct, rows, :, 0, gpl, gpar, :] = lam_re[:, None, :, g, :]
                    s5tab[:, ct, rows, :, 1, gpl, gpar, :] = lam_im[:, None, :, g, :]
                    s5tab[:, ct, rows, :, 2, gpl, gpar, :] = log_dt[:, None, :, g, None]
    s5tab = s5tab.reshape(2, 8, 128, -1)
    s5c = s5c.reshape(2, 8, 128, -1)

    def qlay(a):
        a = a.reshape(2, 2, 32, 2, 64)
        return np.ascontiguousarray(a.transpose(3, 4, 0, 1, 2).reshape(128, 128))

    ident = np.eye(128, dtype=f32)
    common = {
        "w_ada": A(w_ada), "badaT": _fm(b_ada), "gmixT": _fm(g_mix), "gffnT": _fm(g_ffn), "gfinT": _fm(g_final),
        "w_glu": A(w_glu), "bgluT": _fm(b_glu), "w_fourier": A(w_fourier), "bfourT": _fm(b_fourier),
        "w_up": A(w_up), "convwT": _fm(conv_w), "convbT": _fm(conv_b), "w_down": A(w_down), "ssmdT": _fm(ssm_d),
        "s5tab": s5tab, "s5c": s5c, "ident": ident, "ck": ck, "clp": clp, "cls": cls,
    }
    xp, xs = A(x_prompt), A(x_sample)
    st_re, st_im = A(state_ssm_re), A(state_ssm_im)
    cc, cctx = A(c), A(c_ctx)
    ldt_q = np.broadcast_to(log_dt[..., None], lam_re.shape)
    in_maps = []
    for core in range(8):
        b = core % 2
        m = dict(common)
        m["xpT"] = np.ascontiguousarray(xp[2 * core:2 * core + 2].reshape(512, 1024).T)
        m["xsT"] = np.ascontiguousarray(xs[b].T)
        cond = np.stack([cctx, cc[b]], -1)
        m["cond"] = np.ascontiguousarray(cond.reshape(8, 128, 2).transpose(1, 0, 2).reshape(128, 16))
        m["s5q"] = np.ascontiguousarray(np.stack(
            [qlay(lam_re), qlay(lam_im), qlay(np.ascontiguousarray(ldt_q)), qlay(st_re[b]), qlay(st_im[b])], 1
        ).reshape(128, 5 * 128))
        in_maps.append(m)
    res = run_bass_kernel_spmd(nc, in_maps, core_ids=list(range(8)))
    R = res.results
    if DEBUG_STOP is not None:
        return R
    y_prompt = np.empty((16, 256, 1024), f32)
    y_sample = np.empty((2, 2048, 1024), f32)
    new_re = np.empty((16, 2, 2, 64, 64), f32)
    new_im = np.empty((16, 2, 2, 64, 64), f32)
    for core in range(8):
        y_prompt[2 * core:2 * core + 2] = R[core]["ypT"].T.reshape(2, 256, 1024)
        fin = R[core]["fin"].reshape(2, 2, 32, 2, 2, 2, 64)
        for s in range(2):
            for r, dst in ((0, new_re), (1, new_im)):
                v = fin[:, :, :, r, s]
                dst[2 * core + s] = v.reshape(2, 2, 64, 64)
    for b in range(2):
        y_sample[b] = R[b]["ysT"].T
    return (y_prompt, y_sample, new_re, new_im)
```

```python
import math
from contextlib import ExitStack

import numpy as np
import concourse.bass as bass
import concourse.mybir as mybir
from concourse.bass_utils import run_bass_kernel_spmd

F32 = mybir.dt.float32
BF16 = mybir.dt.bfloat16
I32 = mybir.dt.int32
AF = mybir.ActivationFunctionType
ALU = mybir.AluOpType

D_MODEL = 1024
DEPTH = 4
D_FF = 2816
NH = D_FF // 128
EPS = 1e-6
COMPUTE = ("pe", "act", "dve", "pool")
NDSEM = 16
USE_POOL = False


class Buf:
    __slots__ = ("name", "lw", "rd")

    def __init__(self, name=""):
        self.name = name
        self.lw = None
        self.rd = []


class Op:
    __slots__ = ("eng", "fn", "deps", "dma", "idx", "sig", "cnt", "dsem", "dval", "prev_on_sem")

    def __init__(self, eng, fn, dma):
        self.eng = eng
        self.fn = fn
        self.dma = dma
        self.deps = set()
        self.sig = False
        self.cnt = None
        self.dsem = None
        self.dval = None
        self.prev_on_sem = None


class Sched:
    def __init__(self, nc):
        self.nc = nc
        self.ops = []
        self.phase_buf = Buf("phase")

    def add(self, eng, fn, reads=(), writes=(), dma=False):
        op = Op(eng, fn, dma)
        op.idx = len(self.ops)
        reads = list(reads)
        if self.phase_buf not in writes:
            reads.append(self.phase_buf)
        for b in reads:
            if b.lw is not None:
                op.deps.add(b.lw)
        for b in writes:
            if b.lw is not None:
                op.deps.add(b.lw)
            last = {}
            for r in b.rd:
                rop = self.ops[r]
                if rop.dma or rop.eng == "pool":
                    op.deps.add(r)
                elif r > last.get(rop.eng, -1):
                    last[rop.eng] = r
            op.deps.update(last.values())
        for b in reads:
            b.rd.append(op.idx)
        for b in writes:
            b.lw = op.idx
            b.rd = []
        op.deps.discard(op.idx)
        self.ops.append(op)
        return op

    def pe(self, fn, reads=(), writes=()):
        return self.add("pe", fn, reads, writes)

    def act(self, fn, reads=(), writes=()):
        return self.add("act", fn, reads, writes)

    def dve(self, fn, reads=(), writes=()):
        return self.add("dve", fn, reads, writes)

    def dma(self, q, fn, reads=(), writes=()):
        return self.add(q, fn, reads, writes, dma=True)

    def emit(self, final_wait_ops=()):
        nc = self.nc
        ops = self.ops
        for op in ops:
            for d in op.deps:
                dop = ops[d]
                if dop.dma:
                    continue
                if dop.eng == op.eng and dop.eng == "pe" and not op.dma:
                    continue
                dop.sig = True
        cnt = {e: 0 for e in COMPUTE}
        for op in ops:
            if not op.dma and op.sig:
                cnt[op.eng] += 1
                op.cnt = cnt[op.eng]
        self.maxcnt = dict(cnt)
        queues = sorted({op.eng for op in ops if op.dma})
        es = ExitStack()
        esem = {e: es.enter_context(nc.semaphore("c_" + e)) for e in COMPUTE}
        dsems = {q: [es.enter_context(nc.semaphore(f"d_{q}_{i}")) for i in range(NDSEM)] for q in queues}
        dcount = {q: [0] * NDSEM for q in queues}
        dlast = {q: [None] * NDSEM for q in queues}
        rr = {q: 0 for q in queues}
        for op in ops:
            if op.dma:
                q = op.eng
                i = rr[q]
                rr[q] = (i + 1) % NDSEM
                dcount[q][i] += 16
                op.dsem = dsems[q][i]
                op.dval = dcount[q][i]
                op.prev_on_sem = dlast[q][i]
                dlast[q][i] = op.idx
        engs = sorted({op.eng for op in ops})
        block = es.enter_context(nc.Block())
        engobj = {"pe": "tensor", "act": "scalar", "dve": "vector", "pool": "gpsimd", "sp": "sync"}
        fin_ops = [op for op in ops if op.dma and op.idx in final_wait_ops]

        def make(ename):
            def body(e):
                waited = {}

                def wait(sem, val):
                    k = id(sem)
                    if waited.get(k, 0) >= val:
                        return
                    waited[k] = val
                    e.wait_ge(sem, val)

                for op in ops:
                    if op.eng != ename:
                        continue
                    for d in sorted(op.deps):
                        dop = ops[d]
                        if dop.dma:
                            wait(dop.dsem, dop.dval)
                        else:
                            if dop.eng == ename and ename == "pe" and not op.dma:
                                continue
                            wait(esem[dop.eng], dop.cnt)
                    if op.dma:
                        if op.prev_on_sem is not None:
                            p = ops[op.prev_on_sem]
                            wait(p.dsem, p.dval)
                        ins = op.fn(e)
                        ins.then_inc(op.dsem, 16)
                    else:
                        ins = op.fn(e)
                        if op.sig:
                            ins.then_inc(esem[ename], 1)
                if ename == "sp":
                    for op in fin_ops:
                        wait(op.dsem, op.dval)
            return body

        if "sp" not in engs:
            engs.append("sp")
        for ename in engs:
            getattr(block, engobj[ename])(make(ename))
        es.close()


class Rot:
    def __init__(self, items):
        self.items = items
        self.i = 0

    def next(self):
        it = self.items[self.i]
        self.i = (self.i + 1) % len(self.items)
        return it


GROUPS = {
    "P": dict(ntok=512, L=256, nseq=2, R=256, cj=0, passes=[(0, 512)]),
    "S": dict(ntok=2048, L=2048, nseq=1, R=64, cj=1, passes=[(0, 1024), (1024, 1024)]),
}


DEBUG_STOP = None


class _Stop(Exception):
    pass


def build_program():
    nc = bass.Bass("TRN2", target_bir_lowering=False)
    stop = DEBUG_STOP

    def check_stop(name):
        if stop == name:
            raise _Stop()

    es = ExitStack()
    S = Sched(nc)

    def din(n, shape, dt=F32):
        return nc.dram_tensor(n, list(shape), dt, kind="ExternalInput").ap()

    def dout(n, shape):
        return nc.dram_tensor(n, list(shape), F32, kind="ExternalOutput").ap()

    def sb(n, shape, dt=F32):
        return es.enter_context(nc.sbuf_tensor("s_" + n, list(shape), dt))

    d_x = {"P": din("xpT", [1024, 512]), "S": din("xsT", [1024, 2048])}
    d_y = {"P": dout("ypT", [1024, 512]), "S": dout("ysT", [1024, 2048])}
    d_fin = dout("fin", [512, 128])
    d_cond = din("cond", [128, 16])
    d_wada = din("w_ada", [4, 1024, 6144])
    d_bada = din("badaT", [128, 4 * 48])
    d_gmix = din("gmixT", [128, 32])
    d_gffn = din("gffnT", [128, 32])
    d_gfin = din("gfinT", [128, 8])
    d_wglu = din("w_glu", [2, 1024, 2048])
    d_bglu = din("bgluT", [128, 32])
    d_wfour = din("w_fourier", [2, 1024, 1024])
    d_bfour = din("bfourT", [128, 16])
    d_wup = din("w_up", [4, 1024, 2 * D_FF])
    d_convw = din("convwT", [128, 4 * 3 * 44])
    d_convb = din("convbT", [128, 4 * 44])
    d_wdown = din("w_down", [4, D_FF, 1024])
    d_ssmd = din("ssmdT", [128, 16])
    d_s5tab = din("s5tab", [2, 8, 128, 2 * 5 * 256])
    d_s5c = din("s5c", [2, 8, 128, 2 * 2 * 4 * 128])
    d_s5q = din("s5q", [128, 5 * 128])
    d_ident = din("ident", [128, 128])
    d_ck = din("ck", [128, 2 * 2 * 256])
    d_clp = din("clp", [128, 2 * 2 * 256])
    d_cls = din("cls", [2, 2048, 2048], BF16)

    x = {g: sb("x" + g, [128, 8, GROUPS[g]["ntok"]]) for g in GROUPS}
    h = {g: sb("h" + g, [128, 8, GROUPS[g]["ntok"]], BF16) for g in GROUPS}
    bx = {g: [[Buf() for _ in range(GROUPS[g]["ntok"] // 512)] for _ in range(8)] for g in GROUPS}
    bh = {g: [[Buf() for _ in range(GROUPS[g]["ntok"] // 512)] for _ in range(8)] for g in GROUPS}

    ones_bf = sb("ones_bf", [128, 128], BF16)
    epsT = sb("epsT", [128, 1])
    halfpi = sb("halfpi", [128, 1])
    identf = sb("identf", [128, 128])
    cond = sb("cond", [128, 16])
    scond = sb("scond", [128, 8, 2], BF16)
    bada = sb("bada", [128, 4, 48])
    gmix = sb("gmix", [128, 4, 8])
    gffn = sb("gffn", [128, 4, 8])
    gfin = sb("gfin", [128, 8])
    bglu = sb("bglu", [128, 2, 16])
    bfour = sb("bfour", [128, 2, 8])
    ssmd = sb("ssmd", [128, 2, 8])
    s5q = sb("s5q", [128, 5, 128])
    modT_all = sb("modT", [128, 4, 48, 2])
    modL = [modT_all[:, li] for li in range(4)]
    Bmods = [Buf() for _ in range(4)]
    Bder = Buf()
    A1 = sb("A1", [128, 8, 2])
    A2 = sb("A2", [128, 8, 2])
    g1b = sb("g1b", [128, 8, 2])
    finsb = sb("finsb", [128, 512])
    Bconst = Buf("const")
    Bmod = Buf("mod")
    Bfin = Buf("fin")

    ARENA_F32 = 20300
    dummy = sb("dummy", [128, 2])
    arena = sb("arena", [128, ARENA_F32])

    class PH:
        pass

    CURPH = [None]

    def begin_phase(kind):
        ph = PH()
        CURPH[0] = ph
        S.dve(lambda e: e.memset(dummy[:], 0.0), writes=[S.phase_buf])
        off = [0]

        def alloc(shape, dt=F32):
            n = 1
            for d_ in shape[1:]:
                n *= d_
            nbytes = n * (4 if dt in (F32, I32) else 2)
            n4 = (nbytes + 3) // 4
            assert off[0] + n4 <= ARENA_F32, (kind, off[0], n4)
            v = arena[:, off[0]:off[0] + n4]
            off[0] += n4
            if dt != F32:
                v = v.bitcast(dt)
            if len(shape) == 3:
                v = v.rearrange("p (a b) -> p a b", a=shape[1])
            elif len(shape) == 4:
                v = v.rearrange("p (a b c) -> p a b c", a=shape[1], b=shape[2])
            elif len(shape) == 5:
                v = v.rearrange("p (a b c d) -> p a b c d", a=shape[1], b=shape[2], c=shape[3])
            return v

        def rot(n, shape, dt=F32):
            return Rot([(alloc(shape, dt), Buf()) for _ in range(n)])

        def normtmps(nb=2):
            ph.sqb = rot(2, [128, 512], BF16)
            ph.rsd = rot(nb, [128, 2, 512])

        if kind == "mod":
            ph.wbig = rot(2, [128, 6144], BF16)
            ph.wf32 = rot(2, [128, 6144])
        elif kind == "norm":
            normtmps()
            ph.t512 = rot(4, [128, 512])
        elif kind == "s5m":
            ph.XS = [alloc([128, 2, 2048]) for _ in range(2)]
            ph.BXS = [Buf(), Buf()]
            ph.BXS2 = [Buf(), Buf()]
            ph.BXP2 = Buf()
            ph.XP = alloc([128, 2, 512])
            ph.BXP = Buf()
            ph.HbS = [alloc([128, 2048], BF16) for _ in range(2)]
            ph.BHbS = [Buf(), Buf()]
            ph.HbP = [alloc([128, 512], BF16) for _ in range(2)]
            ph.BHbP = [Buf(), Buf()]
            ph.modw = alloc([128, 8, 128], BF16)
            ph.Bmodw = Buf()
            ph.tab = alloc([128, 5, 256])
            ph.Bblk = alloc([128, 2, 2, 256], BF16)
            ph.Cb = alloc([128, 2, 2, 4, 128], BF16)
            ph.BCb = Buf()
            ph.tmpq = [alloc([128, 256]) for _ in range(8)]
            ph.inj2re = alloc([128, 64])
            ph.inj2im = alloc([128, 64])
            ph.Bprep = Buf()
            ph.Btab = Buf()
            ph.t512 = rot(1, [128, 512])
            ph.pwre = alloc([128, 64, 11])
            ph.pwim = alloc([128, 64, 11])
            ph.pwimn = alloc([128, 64, 11])
            ph.injre = alloc([128, 64])
            ph.injim = alloc([128, 64])
            ph.Bq = Buf()
        elif kind == "s5g":
            ph.wbig = rot(2, [128, 2048], BF16)
            ph.t512 = rot(4, [128, 512])
        elif kind == "four":
            normtmps()
            ph.t512 = rot(4, [128, 512])
            ph.pq = alloc([128, 2, 16, 256], BF16)
            ph.Bpq = Buf()
            ph.wbig = rot(2, [128, 8192], BF16)
            ph.ck = alloc([128, 2, 2, 256], BF16)
            ph.clp = alloc([128, 2, 2, 256], BF16)
            ph.Bck = Buf()
            S.dma("pool", lambda e: e.dma_start(out=ph.ck, in_=d_ck.rearrange("p (a b c) -> p a b c", a=2, b=2)), writes=[ph.Bck])
            S.dma("pool", lambda e: e.dma_start(out=ph.clp, in_=d_clp.rearrange("p (a b c) -> p a b c", a=2, b=2)), writes=[ph.Bck])
        elif kind == "ffn":
            normtmps(1)
            ph.t512 = rot(6, [128, 512])
            ph.a_t = alloc([128, 6 * 2560], BF16)
            ph.Ba = [Buf() for _ in range(6)]
            ph.wu = rot(2, [128, 8, 2, 256], BF16)
            ph.wd = rot(1, [128, 6, 1024], BF16)
            ph.convw = alloc([128, 4, 3, 44])
            ph.convb = alloc([128, 4, 44])
            ph.Bcv = Buf()
            S.dma("sp", lambda e: e.dma_start(out=ph.convw, in_=d_convw.rearrange("p (a b c) -> p a b c", a=4, b=3)), writes=[ph.Bcv])
            S.dma("sp", lambda e: e.dma_start(out=ph.convb, in_=d_convb.rearrange("p (a b) -> p a b", a=4)), writes=[ph.Bcv])
        elif kind == "final":
            normtmps()
            ph.t512 = rot(4, [128, 512])
            ph.outrot = rot(8, [128, 512])
        return ph

    psb = [(es.enter_context(nc.psum_tensor(f"ps{i}", [128, 512], F32)), Buf()) for i in range(8)]
    psrot = Rot(psb)

    def ld(dst, src, bufs, q="sp"):
        S.dma(q, lambda e, dst=dst, src=src: e.dma_start(out=dst, in_=src), writes=bufs)

    ld(cond[:], d_cond, [Bconst])
    ld(bada[:], d_bada.rearrange("p (a b) -> p a b", a=4), [Bconst])
    ld(gmix[:], d_gmix.rearrange("p (a b) -> p a b", a=4), [Bconst])
    ld(gffn[:], d_gffn.rearrange("p (a b) -> p a b", a=4), [Bconst])
    ld(gfin[:], d_gfin, [Bconst])
    ld(bglu[:], d_bglu.rearrange("p (a b) -> p a b", a=2), [Bconst])
    ld(bfour[:], d_bfour.rearrange("p (a b) -> p a b", a=2), [Bconst])
    ld(ssmd[:], d_ssmd.rearrange("p (a b) -> p a b", a=2), [Bconst])
    ld(s5q[:], d_s5q.rearrange("p (a b) -> p a b", a=5), [Bconst])
    ld(identf[:], d_ident, [Bconst])
    for g in GROUPS:
        for ct in range(8):
            S.dma("sp", lambda e, g=g, ct=ct: e.dma_start(out=x[g][:, ct, :], in_=d_x[g][ct * 128:(ct + 1) * 128, :]),
                  writes=bx[g][ct])
    S.dve(lambda e: e.memset(ones_bf[:], 1.0), writes=[Bconst])
    S.dve(lambda e: e.memset(epsT[:], EPS), writes=[Bconst])
    S.dve(lambda e: e.memset(halfpi[:], math.pi / 2), writes=[Bconst])
    S.dve(lambda e: e.memset(finsb[:], 0.0), writes=[Bfin])
    S.act(lambda e: e.activation(out=scond[:], in_=cond[:].rearrange("p (a b) -> p a b", b=2), func=AF.Silu),
          reads=[Bconst], writes=[Bconst])

    def ts(tt):
        return slice(tt * 512, (tt + 1) * 512)

    def compute_mod(i):
        ph = CURPH[0]
        for c in range(8):
            wf, wfb = ph.wf32.next()
            wfv = wf[:, 0:6144].rearrange("p (k c) -> p k c", k=8)
            for hq, q_ in ((0, "sp"), (1, "act")):
                S.dma(q_, lambda e, wfv=wfv, c=c, hq=hq: e.dma_start(
                    out=wfv[:, hq * 4:(hq + 1) * 4, :],
                    in_=d_wada[i, hq * 512:(hq + 1) * 512, c * 768:(c + 1) * 768].rearrange("(k p) c -> p k c", p=128)),
                    writes=[wfb])
            wt, wb = ph.wbig.next()
            wv = wt[:, 0:6144].rearrange("p (k c) -> p k c", k=8)
            S.dve(lambda e, wt=wt, wf=wf: e.tensor_copy(out=wt[:, 0:6144], in_=wf[:, 0:6144]), reads=[wfb], writes=[wb])
            for t in range(6):
                m = c * 6 + t
                ps, pb = psrot.next()
                for kt in range(8):
                    S.pe(lambda e, ps=ps, wv=wv, t=t, kt=kt: e.matmul(
                        ps[:, 0:2], wv[:, kt, t * 128:(t + 1) * 128], scond[:, kt, :], start=(kt == 0), stop=(kt == 7)),
                        reads=[wb, Bconst], writes=[pb])
                S.act(lambda e, ps=ps, m=m: e.activation(out=modL[i][:, m, :], in_=ps[:, 0:2], func=AF.Identity,
                                                         bias=bada[:, i, m:m + 1]), reads=[pb, Bconst], writes=[Bmods[i]])

    def derive(i):
        for cj in range(2):
            S.dve(lambda e, cj=cj: e.scalar_tensor_tensor(out=A1[:, :, cj], in0=modL[i][:, 8:16, cj], scalar=1.0,
                                                          in1=gmix[:, i, :], op0=ALU.add, op1=ALU.mult),
                  reads=[Bmods[i], Bconst], writes=[Bder])
            S.dve(lambda e, cj=cj: e.scalar_tensor_tensor(out=A2[:, :, cj], in0=modL[i][:, 32:40, cj], scalar=1.0,
                                                          in1=gffn[:, i, :], op0=ALU.add, op1=ALU.mult),
                  reads=[Bmods[i], Bconst], writes=[Bder])
            if i % 2 == 1:
                S.dve(lambda e, cj=cj: e.tensor_tensor(out=g1b[:, :, cj], in0=modL[i][:, 16:24, cj], in1=bfour[:, i // 2, :],
                                                       op=ALU.mult), reads=[Bmods[i], Bconst], writes=[Bder])

    def mod_chunk_emitters(i):
        ems = []
        for m in range(48):
            def em(m=m):
                ph = CURPH[0]
                S.dma("pool", lambda e: e.dma_start(
                    out=ph.modw, in_=d_wada[i, :, m * 128:(m + 1) * 128].rearrange("(k p) c -> p k c", p=128)),
                    writes=[ph.Bmodw])
                ps, pb = psb[5 + (bu_rr[0] % 3)]
                bu_rr[0] += 1
                for kt in range(8):
                    S.pe(lambda e, kt=kt: e.matmul(ps[:, 0:2], ph.modw[:, kt, :], scond[:, kt, :], start=(kt == 0), stop=(kt == 7)),
                         reads=[ph.Bmodw, Bconst], writes=[pb])
                S.act(lambda e: e.activation(out=modL[i][:, m, :], in_=ps[:, 0:2], func=AF.Identity, bias=bada[:, i, m:m + 1]),
                      reads=[pb, Bconst], writes=[Bmods[i]])
            ems.append(em)
        return ems

    def norm(g, tts, scale_fn, bias_fn, out_fn, extra_reads, out_bufs_fn):
        ph = CURPH[0]
        for tt in tts:
            ps, pb = psrot.next()
            for ct in range(8):
                sq, sqbuf = ph.sqb.next()
                S.act(lambda e, sq=sq, ct=ct, tt=tt: e.activation(out=sq[:], in_=x[g][:, ct, ts(tt)], func=AF.Square),
                      reads=[bx[g][ct][tt]], writes=[sqbuf])
                S.pe(lambda e, ps=ps, sq=sq, ct=ct: e.matmul(ps[:], ones_bf[:], sq[:], start=(ct == 0), stop=(ct == 7)),
                     reads=[sqbuf, Bconst], writes=[pb])
            rsd, Brsd = ph.rsd.next()
            S.act(lambda e, ps=ps, rsd=rsd: e.activation(out=rsd[:, 0, :], in_=ps[:], func=AF.Sqrt, scale=1.0 / D_MODEL, bias=epsT[:]),
                  reads=[pb, Bconst], writes=[Brsd])
            S.dve(lambda e, rsd=rsd: e.reciprocal(out=rsd[:, 1, :], in_=rsd[:, 0, :]), reads=[Brsd], writes=[Brsd])
            for ct in range(8):
                tm, tb = ph.t512.next()
                S.dve(lambda e, tm=tm, ct=ct, tt=tt, rsd=rsd: e.tensor_tensor(out=tm[:], in0=x[g][:, ct, ts(tt)], in1=rsd[:, 1, :],
                                                                             op=ALU.mult),
                      reads=[bx[g][ct][tt], Brsd], writes=[tb])
                bias = bias_fn(ct)
                kw = {} if bias is None else {"bias": bias}
                oap = out_fn(ct, tt)
                sc = scale_fn(ct)
                S.act(lambda e, tm=tm, kw=kw, oap=oap, sc=sc: e.activation(out=oap, in_=tm[:], func=AF.Identity,
                                                                           scale=sc, **kw),
                      reads=[tb] + extra_reads, writes=out_bufs_fn(ct, tt))

    def norm_h(g, i, which, tts):
        cj = GROUPS[g]["cj"]
        A = A1 if which == 1 else A2
        sh0 = 0 if which == 1 else 24
        norm(g, tts, lambda ct: A[:, ct, cj:cj + 1], lambda ct: modL[i][:, sh0 + ct, cj:cj + 1],
             lambda ct, tt: h[g][:, ct, ts(tt)], [Bmods[i], Bder], lambda ct, tt: [bh[g][ct][tt]])

    def ffn_all(i):
        ph = CURPH[0]
        tiles = [("S", 0), ("S", 1), ("S", 2), ("S", 3), ("P", 0)]
        toff = {("S", 0): 0, ("S", 1): 512, ("S", 2): 1024, ("S", 3): 1536, ("P", 0): 2048}
        for g in GROUPS:
            norm_h(g, i, 2, list(range(GROUPS[g]["ntok"] // 512)))
        quarters = [(0, 6), (6, 6), (12, 5), (17, 5)]
        for (j0q, nhq) in quarters:
            a3 = ph.a_t[:, 0:nhq * 2560].rearrange("p (j t) -> p j t", j=nhq)
            for c0j in range(0, nhq, 2):
                ncj = min(2, nhq - c0j)
                wt, wb = ph.wu.next()
                for gv in range(2):
                    c0 = gv * D_FF + (j0q + c0j) * 128
                    S.dma("pool", lambda e, wt=wt, gv=gv, c0=c0, ncj=ncj: e.dma_start(
                        out=wt[:, :, gv, 0:ncj * 128], in_=d_wup[i, :, c0:c0 + ncj * 128].rearrange("(k p) c -> p k c", p=128)),
                        writes=[wb])
                for cj in range(ncj):
                    jl = c0j + cj
                    j = j0q + jl
                    for (g, tt) in tiles:
                        R = GROUPS[g]["R"]
                        pss = []
                        for gv in range(2):
                            ps, pb = psrot.next()
                            for kt in range(8):
                                S.pe(lambda e, ps=ps, wt=wt, gv=gv, kt=kt, tt=tt, g=g, cj=cj: e.matmul(
                                    ps[:], wt[:, kt, gv, cj * 128:(cj + 1) * 128], h[g][:, kt, ts(tt)],
                                    start=(kt == 0), stop=(kt == 7)), reads=[wb, bh[g][kt][tt]], writes=[pb])
                            pss.append((ps, pb))
                        outs = []
                        for gv in range(2):
                            ps, pb = pss[gv]
                            col = gv * NH + j
                            tm, tb = ph.t512.next()
                            S.act(lambda e, ps=ps, tm=tm, col=col: e.activation(
                                out=tm[:], in_=ps[:], func=AF.Identity, scale=ph.convw[:, i, 1, col:col + 1],
                                bias=ph.convb[:, i, col:col + 1]), reads=[pb, ph.Bcv], writes=[tb])
                            outs.append((tm, tb))
                        for kk_ in (0, 2):
                            for gv in range(2):
                                ps, pb = pss[gv]
                                tm, tb = outs[gv]
                                col = gv * NH + j
                                p3 = ps[:].rearrange("p (r t) -> p r t", t=R)
                                t3 = tm[:].rearrange("p (r t) -> p r t", t=R)
                                if kk_ == 0:
                                    S.dve(lambda e, p3=p3, t3=t3, col=col, R=R: e.scalar_tensor_tensor(
                                        out=t3[:, :, 1:R], in0=p3[:, :, 0:R - 1], scalar=ph.convw[:, i, 0, col:col + 1],
                                        in1=t3[:, :, 1:R], op0=ALU.mult, op1=ALU.add), reads=[pb, tb, ph.Bcv], writes=[tb])
                                else:
                                    S.dve(lambda e, p3=p3, t3=t3, col=col, R=R: e.scalar_tensor_tensor(
                                        out=t3[:, :, 0:R - 1], in0=p3[:, :, 1:R], scalar=ph.convw[:, i, 2, col:col + 1],
                                        in1=t3[:, :, 0:R - 1], op0=ALU.mult, op1=ALU.add), reads=[pb, tb, ph.Bcv], writes=[tb])
                        (gc, gb), (vc, vb) = outs
                        sg, sgb = ph.t512.next()
                        S.act(lambda e, sg=sg, gc=gc: e.activation(out=sg[:], in_=gc[:], func=AF.Silu),
                              reads=[gb], writes=[sgb])
                        to = toff[(g, tt)]
                        S.dve(lambda e, sg=sg, vc=vc, jl=jl, to=to, a3=a3: e.tensor_tensor(
                            out=a3[:, jl, to:to + 512], in0=sg[:], in1=vc[:], op=ALU.mult),
                            reads=[sgb, vb], writes=[ph.Ba[jl]])
            wt, wb = ph.wd.next()
            S.dma("pool", lambda e, wt=wt, j0q=j0q, nhq=nhq: e.dma_start(
                out=wt[:, 0:nhq, :], in_=d_wdown[i, j0q * 128:(j0q + nhq) * 128, :].rearrange("(j p) c -> p j c", p=128)),
                writes=[wb])
            for ct in range(8):
                for (g, tt) in tiles:
                    cjx = GROUPS[g]["cj"]
                    to = toff[(g, tt)]
                    ps, pb = psrot.next()
                    for jl in range(nhq):
                        S.pe(lambda e, ps=ps, wt=wt, jl=jl, to=to, ct=ct, a3=a3, nhq=nhq: e.matmul(
                            ps[:], wt[:, jl, ct * 128:(ct + 1) * 128], a3[:, jl, to:to + 512], start=(jl == 0),
                            stop=(jl == nhq - 1)), reads=[wb, ph.Ba[jl]], writes=[pb])
                    S.dve(lambda e, ps=ps, ct=ct, tt=tt, g=g, cjx=cjx: e.scalar_tensor_tensor(
                        out=x[g][:, ct, ts(tt)], in0=ps[:], scalar=modL[i][:, 40 + ct, cjx:cjx + 1],
                        in1=x[g][:, ct, ts(tt)], op0=ALU.mult, op1=ALU.add),
                        reads=[pb, Bmods[i], bx[g][ct][tt]], writes=[bx[g][ct][tt]])

    def fourier(g, i):
        ph = CURPH[0]
        G = GROUPS[g]
        cj = G["cj"]
        jf = i // 2
        ntok, L, nseq = G["ntok"], G["L"], G["nseq"]
        nlt = ntok // 128
        ntt = ntok // 512
        norm_h(g, i, 1, list(range(ntt)))
        lts = L // 128
        for q in range(4):
            for lt in range(nlt):
                for tb_ in range(2):
                    ps, pb = psrot.next()
                    for kk in range(2):
                        S.pe(lambda e, ps=ps, kk=kk, lt=lt, tb_=tb_, q=q: e.matmul(
                            ps[:, 0:256], h[g][:, 2 * q + kk, lt * 128:(lt + 1) * 128], ph.ck[:, tb_, kk, :],
                            start=(kk == 0), stop=(kk == 1)),
                            reads=[bh[g][2 * q + kk][lt // 4], ph.Bck], writes=[pb])
                    if tb_ == 0:
                        S.act(lambda e, ps=ps, lt=lt: e.activation(out=ph.pq[:, 0, lt, :], in_=ps[:, 0:256], func=AF.Copy),
                              reads=[pb], writes=[ph.Bpq])
                    else:
                        S.dve(lambda e, ps=ps, lt=lt: e.tensor_copy(out=ph.pq[:, 1, lt, :], in_=ps[:, 0:256]),
                              reads=[pb], writes=[ph.Bpq])
            for s in range(nseq):
                for lb in range(L // 256):
                    if g == "S":
                        wt, wb = ph.wbig.next()
                        tv = wt[:, :].rearrange("p (a l c) -> p a l c", a=2, l=16)
                        for tb_ in range(2):
                            S.dma("sp", lambda e, tv=tv, tb_=tb_, lb=lb: e.dma_start(
                                out=tv[:, tb_, :, :],
                                in_=d_cls[tb_, :, lb * 256:(lb + 1) * 256].rearrange("(l p) c -> p l c", p=128)),
                                writes=[wb])
                        tabs = lambda tb_, l, tv=tv: tv[:, tb_, l, :]
                        tbuf = wb
                    else:
                        tabs = lambda tb_, l: ph.clp[:, tb_, l, :]
                        tbuf = ph.Bck
                    for kk in range(2):
                        ps, pb = psrot.next()
                        n = 0
                        for tb_ in range(2):
                            for l in range(lts):
                                S.pe(lambda e, ps=ps, tb_=tb_, l=l, kk=kk, s=s, tabs=tabs, n=n: e.matmul(
                                    ps[:, 0:256], ph.pq[:, tb_, s * lts + l, kk * 128:(kk + 1) * 128], tabs(tb_, l),
                                    start=(n == 0), stop=(n == 2 * lts - 1)), reads=[ph.Bpq, tbuf], writes=[pb])
                                n += 1
                        tok0 = s * L + lb * 256
                        S.act(lambda e, ps=ps, kk=kk, tok0=tok0, q=q: e.activation(
                            out=h[g][:, 2 * q + kk, tok0:tok0 + 256], in_=ps[:, 0:256], func=AF.Copy),
                            reads=[pb], writes=[bh[g][2 * q + kk][tok0 // 512]])
        for oc in range(8):
            wt, wb = ph.wbig.next()
            wv = wt[:, 0:1024].rearrange("p (k c) -> p k c", k=8)
            S.dma("pool", lambda e, wv=wv, oc=oc: e.dma_start(
                out=wv, in_=d_wfour[jf, :, oc * 128:(oc + 1) * 128].rearrange("(k p) c -> p k c", p=128)), writes=[wb])
            for tt in range(ntt):
                ps, pb = psrot.next()
                for kt in range(8):
                    S.pe(lambda e, ps=ps, wv=wv, kt=kt, tt=tt: e.matmul(
                        ps[:], wv[:, kt, :], h[g][:, kt, ts(tt)], start=(kt == 0), stop=(kt == 7)),
                        reads=[wb, bh[g][kt][tt]], writes=[pb])
                tm, tb = ph.t512.next()
                S.act(lambda e, ps=ps, tm=tm, oc=oc: e.activation(
                    out=tm[:], in_=ps[:], func=AF.Identity, scale=modL[i][:, 16 + oc, cj:cj + 1],
                    bias=g1b[:, oc, cj:cj + 1]), reads=[pb, Bmods[i], Bder], writes=[tb])
                S.dve(lambda e, tm=tm, oc=oc, tt=tt: e.tensor_tensor(
                    out=x[g][:, oc, ts(tt)], in0=x[g][:, oc, ts(tt)], in1=tm[:], op=ALU.add),
                    reads=[tb, bx[g][oc][tt]], writes=[bx[g][oc][tt]])

    TWO_PI = 2.0 * math.pi

    def cexp_ops(xr, xi, W, are, aim, T, bufs):
        ph = CURPH[0]
        rw = bufs
        t0, t1, t2, t3 = [t[:, 0:W] for t in T[:4]]
        ti = T[2][:, 0:W].bitcast(I32)
        S.act(lambda e: e.activation(out=t0, in_=xr, func=AF.Exp), reads=rw, writes=rw)
        S.dve(lambda e: e.tensor_scalar(out=ti, in0=xi, scalar1=1.0 / TWO_PI, scalar2=None, op0=ALU.mult),
              reads=rw, writes=rw)
        S.dve(lambda e: e.tensor_copy(out=t1, in_=ti), reads=rw, writes=rw)
        S.dve(lambda e: e.scalar_tensor_tensor(out=t1, in0=t1, scalar=-TWO_PI, in1=xi, op0=ALU.mult, op1=ALU.add),
              reads=rw, writes=rw)
        S.dve(lambda e: e.tensor_scalar(out=t1, in0=t1, scalar1=math.pi, scalar2=-math.pi, op0=ALU.min, op1=ALU.max),
              reads=rw, writes=rw)
        S.act(lambda e: e.activation(out=t2, in_=t1, func=AF.Sin), reads=rw, writes=rw)
        S.act(lambda e: e.activation(out=t3, in_=t1, func=AF.Abs), reads=rw, writes=rw)
        S.act(lambda e: e.activation(out=t3, in_=t3, func=AF.Sin, scale=-1.0, bias=halfpi[:]), reads=rw + [Bconst], writes=rw)
        S.dve(lambda e: e.tensor_tensor(out=are, in0=t0, in1=t3, op=ALU.mult), reads=rw, writes=rw)
        S.dve(lambda e: e.tensor_tensor(out=aim, in0=t0, in1=t2, op=ALU.mult), reads=rw, writes=rw)

    def s5_qprep(j):
        ph = CURPH[0]
        rw = [ph.Bq, ph.Bprep, Bconst]
        sl = slice(j * 64, (j + 1) * 64)
        T = ph.tmpq
        dtv = T[4][:, 0:64]
        xr = T[5][:, 0:64]
        xi = T[6][:, 0:64]
        S.act(lambda e: e.activation(out=dtv, in_=s5q[:, 2, sl], func=AF.Exp), reads=rw, writes=rw)
        S.dve(lambda e: e.tensor_tensor(out=xr, in0=s5q[:, 0, sl], in1=dtv, op=ALU.mult), reads=rw, writes=rw)
        S.dve(lambda e: e.tensor_tensor(out=xi, in0=s5q[:, 1, sl], in1=dtv, op=ALU.mult), reads=rw, writes=rw)
        cexp_ops(xr, xi, 64, ph.pwre[:, :, 0], ph.pwim[:, :, 0], T, rw)
        tA = T[0][:, 0:64]
        tB = T[1][:, 0:64]
        for k in range(10):
            S.dve(lambda e, k=k: e.tensor_tensor(out=tA, in0=ph.pwre[:, :, k], in1=ph.pwre[:, :, k], op=ALU.mult), reads=rw, writes=rw)
            S.dve(lambda e, k=k: e.tensor_tensor(out=tB, in0=ph.pwim[:, :, k], in1=ph.pwim[:, :, k], op=ALU.mult), reads=rw, writes=rw)
            S.dve(lambda e, k=k: e.tensor_tensor(out=ph.pwre[:, :, k + 1], in0=tA, in1=tB, op=ALU.subtract), reads=rw, writes=rw)
            S.dve(lambda e, k=k: e.scalar_tensor_tensor(out=ph.pwim[:, :, k + 1], in0=ph.pwre[:, :, k], scalar=2.0,
                                                        in1=ph.pwim[:, :, k], op0=ALU.mult, op1=ALU.mult), reads=rw, writes=rw)
        S.dve(lambda e: e.tensor_scalar(out=ph.pwimn[:], in0=ph.pwim[:], scalar1=-1.0, scalar2=None, op0=ALU.mult),
              reads=rw, writes=rw)
        S.dve(lambda e: e.tensor_tensor(out=tA, in0=ph.pwre[:, :, 0], in1=s5q[:, 3, sl], op=ALU.mult), reads=rw, writes=rw)
        S.dve(lambda e: e.tensor_tensor(out=tB, in0=ph.pwim[:, :, 0], in1=s5q[:, 4, sl], op=ALU.mult), reads=rw, writes=rw)
        S.dve(lambda e: e.tensor_tensor(out=ph.injre[:], in0=tA, in1=tB, op=ALU.subtract), reads=rw, writes=rw)
        S.dve(lambda e: e.tensor_tensor(out=tA, in0=ph.pwre[:, :, 0], in1=s5q[:, 4, sl], op=ALU.mult), reads=rw, writes=rw)
        S.dve(lambda e: e.tensor_tensor(out=tB, in0=ph.pwim[:, :, 0], in1=s5q[:, 3, sl], op=ALU.mult), reads=rw, writes=rw)
        S.dve(lambda e: e.tensor_tensor(out=ph.injim[:], in0=tA, in1=tB, op=ALU.add), reads=rw, writes=rw)
        S.dve(lambda e: e.tensor_tensor(out=tA, in0=ph.pwre[:, :, 0], in1=ph.injre[:], op=ALU.mult), reads=rw, writes=rw)
        S.dve(lambda e: e.tensor_tensor(out=tB, in0=ph.pwim[:, :, 0], in1=ph.injim[:], op=ALU.mult), reads=rw, writes=rw)
        S.dve(lambda e: e.tensor_tensor(out=ph.inj2re[:], in0=tA, in1=tB, op=ALU.subtract), reads=rw, writes=rw)
        S.dve(lambda e: e.tensor_tensor(out=tA, in0=ph.pwre[:, :, 0], in1=ph.injim[:], op=ALU.mult), reads=rw, writes=rw)
        S.dve(lambda e: e.tensor_tensor(out=tB, in0=ph.pwim[:, :, 0], in1=ph.injre[:], op=ALU.mult), reads=rw, writes=rw)
        S.dve(lambda e: e.tensor_tensor(out=ph.inj2im[:], in0=tA, in1=tB, op=ALU.add), reads=rw, writes=rw)

    def s5_prep(j, ct, d):
        ph = CURPH[0]
        rw = [ph.Bprep]
        S.dma("sp", lambda e: e.dma_start(
            out=ph.tab[:], in_=d_s5tab[j, ct].rearrange("p (d t c) -> p d t c", d=2, t=5)[:, d]), writes=[ph.Btab])
        S.dve(lambda e: e.memset(dummy[:], 0.0), reads=[ph.Btab], writes=[ph.Bprep])
        T = ph.tmpq
        W = 256
        lamre, lamim, logdt, bre, bim = [ph.tab[:, k, :] for k in range(5)]
        dtv, xr, xi, are, aim = T[4][:, :], T[5][:, :], T[6][:, :], T[7][:, :], T[4][:, :]
        S.act(lambda e: e.activation(out=dtv, in_=logdt, func=AF.Exp), reads=rw, writes=rw)
        S.dve(lambda e: e.tensor_tensor(out=xr, in0=lamre, in1=dtv, op=ALU.mult), reads=rw, writes=rw)
        S.dve(lambda e: e.tensor_tensor(out=xi, in0=lamim, in1=dtv, op=ALU.mult), reads=rw, writes=rw)
        cexp_ops(xr, xi, W, are, aim, T, rw)
        nr, den, cr, ci, t5 = T[7][:, :], T[0][:, :], T[1][:, :], T[2][:, :], T[3][:, :]
        S.dve(lambda e: e.tensor_scalar(out=nr, in0=are, scalar1=-1.0, scalar2=None, op0=ALU.add), reads=rw, writes=rw)
        S.dve(lambda e: e.tensor_tensor(out=den, in0=lamre, in1=lamre, op=ALU.mult), reads=rw, writes=rw)
        S.dve(lambda e: e.tensor_tensor(out=t5, in0=lamim, in1=lamim, op=ALU.mult), reads=rw, writes=rw)
        S.dve(lambda e: e.tensor_tensor(out=den, in0=den, in1=t5, op=ALU.add), reads=rw, writes=rw)
        S.dve(lambda e: e.reciprocal(out=den, in_=den), reads=rw, writes=rw)
        S.dve(lambda e: e.tensor_tensor(out=cr, in0=nr, in1=lamre, op=ALU.mult), reads=rw, writes=rw)
        S.dve(lambda e: e.tensor_tensor(out=t5, in0=aim, in1=lamim, op=ALU.mult), reads=rw, writes=rw)
        S.dve(lambda e: e.tensor_tensor(out=cr, in0=cr, in1=t5, op=ALU.add), reads=rw, writes=rw)
        S.dve(lambda e: e.tensor_tensor(out=cr, in0=cr, in1=den, op=ALU.mult), reads=rw, writes=rw)
        S.dve(lambda e: e.tensor_tensor(out=ci, in0=aim, in1=lamre, op=ALU.mult), reads=rw, writes=rw)
        S.dve(lambda e: e.tensor_tensor(out=t5, in0=nr, in1=lamim, op=ALU.mult), reads=rw, writes=rw)
        S.dve(lambda e: e.tensor_tensor(out=ci, in0=ci, in1=t5, op=ALU.subtract), reads=rw, writes=rw)
        S.dve(lambda e: e.tensor_tensor(out=ci, in0=ci, in1=den, op=ALU.mult), reads=rw, writes=rw)
        u0, u1 = T[5][:, :], T[6][:, :]
        b32r, b32i = T[0][:, :], T[3][:, :]
        S.dve(lambda e: e.tensor_tensor(out=u0, in0=cr, in1=bre, op=ALU.mult), reads=rw, writes=rw)
        S.dve(lambda e: e.tensor_tensor(out=u1, in0=ci, in1=bim, op=ALU.mult), reads=rw, writes=rw)
        S.dve(lambda e: e.tensor_tensor(out=b32r, in0=u0, in1=u1, op=ALU.subtract), reads=rw, writes=rw)
        S.dve(lambda e: e.tensor_tensor(out=u0, in0=cr, in1=bim, op=ALU.mult), reads=rw, writes=rw)
        S.dve(lambda e: e.tensor_tensor(out=u1, in0=ci, in1=bre, op=ALU.mult), reads=rw, writes=rw)
        S.dve(lambda e: e.tensor_tensor(out=b32i, in0=u0, in1=u1, op=ALU.add), reads=rw + [ph.Btab], writes=rw)
        S.act(lambda e: e.activation(out=ph.Bblk[:, 0, 0, :], in_=b32r, func=AF.Copy), reads=rw, writes=rw)
        S.act(lambda e: e.activation(out=ph.Bblk[:, 0, 1, :], in_=b32i, func=AF.Copy), reads=rw, writes=rw)
        S.dve(lambda e: e.tensor_tensor(out=u0, in0=nr, in1=b32r, op=ALU.mult), reads=rw, writes=rw)
        S.dve(lambda e: e.tensor_tensor(out=u0, in0=u0, in1=b32r, op=ALU.add), reads=rw, writes=rw)
        S.dve(lambda e: e.tensor_tensor(out=u1, in0=aim, in1=b32i, op=ALU.mult), reads=rw, writes=rw)
        S.dve(lambda e: e.tensor_tensor(out=ph.Bblk[:, 1, 0, :], in0=u0, in1=u1, op=ALU.subtract), reads=rw, writes=rw)
        S.dve(lambda e: e.tensor_tensor(out=u0, in0=nr, in1=b32i, op=ALU.mult), reads=rw, writes=rw)
        S.dve(lambda e: e.tensor_tensor(out=u0, in0=u0, in1=b32i, op=ALU.add), reads=rw, writes=rw)
        S.dve(lambda e: e.tensor_tensor(out=u1, in0=aim, in1=b32r, op=ALU.mult), reads=rw, writes=rw)
        S.dve(lambda e: e.tensor_tensor(out=ph.Bblk[:, 1, 1, :], in0=u0, in1=u1, op=ALU.add), reads=rw, writes=rw)

    bu_rr = [0]
    tile_no = [0]
    pending = []
    carry = {"h2": [], "post": None, "evac": None}

    def s5_flush():
        for fn, rd, wr in carry["h2"]:
            if rd is None:
                fn()
            else:
                S.dve(fn, reads=rd, writes=wr)
        carry["post"]()
        carry["evac"]()
        carry["h2"], carry["post"], carry["evac"] = [], None, None

    def s5_main_ct(j, ct):
        ph = CURPH[0]
        psy = {"S": psb[0:4], "P": psb[4:5]}
        nmm = {"S": 0, "P": 0}

        def tile(g, d, gl):
            G = GROUPS[g]
            ntok, L, nseq = G["ntok"], G["L"], G["nseq"]
            ntt = ntok // 512
            nlev = int(math.log2(L))
            gp = ct * 4 + gl
            half, gpl = gl // 2, gl % 2
            tix = d * 32 + gp
            hs = slice(64 * half, 64 * half + 64)
            if g == "S":
                k_ = tile_no[0] % 2
                tile_no[0] += 1
                xs, Bset, Bset2 = ph.XS[k_], ph.BXS[k_], ph.BXS2[k_]
                Hb, BHb = ph.HbS, ph.BHbS
            else:
                xs, Bset, Bset2 = ph.XP, ph.BXP, ph.BXP2
                Hb, BHb = ph.HbP, ph.BHbP
            head = []
            qraw, qpair = (0, 1) if d == 0 else (1, 0)
            cs = slice(gpl * 128, (gpl + 1) * 128)
            for r in range(2):
                for tt in range(ntt):
                    ps, pb = psb[5 + (bu_rr[0] % 3)]
                    bu_rr[0] += 1
                    hv = h[g][hs, ct, ts(tt)].rearrange("p (m q) -> p m q", q=2)
                    xv = xs[:, r, ts(tt)].rearrange("p (m q) -> p m q", q=2)
                    rd_ = [ph.Bprep, bh[g][ct][tt]]
                    S.pe(lambda e, ps=ps, r=r, hv=hv: e.matmul(ps[:, 0:256], ph.Bblk[hs, 0, r, cs], hv[:, :, qraw],
                                                               start=True, stop=True), reads=rd_, writes=[pb])
                    S.pe(lambda e, ps=ps, r=r, hv=hv: e.matmul(ps[:, 256:512], ph.Bblk[hs, 0, r, cs], hv[:, :, qpair],
                                                               start=True, stop=False), reads=rd_, writes=[pb])
                    S.pe(lambda e, ps=ps, r=r, hv=hv: e.matmul(ps[:, 256:512], ph.Bblk[hs, 1, r, cs], hv[:, :, qraw],
                                                               start=False, stop=True), reads=rd_, writes=[pb])
                    S.act(lambda e, ps=ps, xv=xv: e.activation(out=xv[:, :, qraw], in_=ps[:, 0:256], func=AF.Copy),
                          reads=[pb], writes=[(Bset, Bset2)[r]])
                    S.act(lambda e, ps=ps, xv=xv: e.activation(out=xv[:, :, qpair], in_=ps[:, 256:512], func=AF.Copy),
                          reads=[pb], writes=[(Bset, Bset2)[r]])
            if pending:
                pending.pop(0)()
            col = 0 if d == 0 else L - 1
            if g == "S":
                col2 = 1 if d == 0 else L - 2
                for r, inj, c_ in ((0, ph.injre, col), (1, ph.injim, col), (0, ph.inj2re, col2), (1, ph.inj2im, col2)):
                    head.append((lambda e, r=r, inj=inj, c_=c_: e.tensor_tensor(
                        out=xs[:, r, c_:c_ + 1], in0=xs[:, r, c_:c_ + 1], in1=inj[:, tix:tix + 1], op=ALU.add),
                        [(Bset, Bset2)[r], ph.Bq], [(Bset, Bset2)[r]]))
            else:
                for r in range(2):
                    o = ((j * 2 + d) * 32 + gp) * 4 + r * 2
                    src = xs[:, r, 0:ntok].rearrange("p (s t) -> p s t", s=nseq)[:, :, col]
                    head.append((lambda e, o=o, src=src: e.tensor_copy(out=finsb[:, o:o + 2], in_=src),
                                 [(Bset, Bset2)[r]], [Bfin]))

            def level(k, down):
                dd = 1 << k
                M = L // (2 * dd)
                v = xs[:, :, 0:ntok].rearrange("p r (s m q) -> p r s m q", s=nseq, q=2 * dd)
                if d == 0:
                    if not down:
                        tg, sr, n = v[:, :, :, :, 2 * dd - 1], v[:, :, :, :, dd - 1], M
                    else:
                        tg, sr, n = v[:, :, :, 1:M, dd - 1], v[:, :, :, 0:M - 1, 2 * dd - 1], M - 1
                else:
                    if not down:
                        tg, sr, n = v[:, :, :, :, 0], v[:, :, :, :, dd], M
                    else:
                        tg, sr, n = v[:, :, :, 0:M - 1, dd], v[:, :, :, 1:M, 0], M - 1
                if n == 0:
                    return
                pr = ph.pwre[:, tix, k:k + 1]
                pi_ = ph.pwim[:, tix, k:k + 1]
                pin = ph.pwimn[:, tix, k:k + 1]
                scan.append((lambda e: e.scalar_tensor_tensor(out=tg, in0=sr, scalar=pr, in1=tg, op0=ALU.mult, op1=ALU.add),
                             [Bset, Bset2, ph.Bq], [Bset, Bset2]))
                scan.append((lambda e: e.scalar_tensor_tensor(out=tg[:, 0], in0=sr[:, 1], scalar=pin, in1=tg[:, 0],
                                                              op0=ALU.mult, op1=ALU.add), [Bset, ph.Bq], [Bset]))
                scan.append((lambda e: e.scalar_tensor_tensor(out=tg[:, 1], in0=sr[:, 0], scalar=pi_, in1=tg[:, 1],
                                                              op0=ALU.mult, op1=ALU.add), [Bset2, ph.Bq], [Bset2]))

            scan = list(head)
            for k in range(1, nlev):
                level(k, False)
            for k in range(nlev - 2, -1, -1):
                level(k, True)

            def post():
                S.act(lambda e: e.activation(out=Hb[0][:, 0:ntok], in_=xs[:, 0, 0:ntok], func=AF.Copy),
                      reads=[Bset, Bset2], writes=[BHb[0]])
                S.act(lambda e: e.activation(out=Hb[1][:, 0:ntok], in_=xs[:, 1, 0:ntok], func=AF.Copy, scale=-1.0),
                      reads=[Bset, Bset2], writes=[BHb[1]])
                for r in range(2):
                    for tt in range(ntt):
                        ps, pb = psy[g][tt]
                        S.pe(lambda e, ps=ps, r=r, tt=tt, first=(nmm[g] == 0), last=(nmm[g] == 15): e.matmul(
                            ps[:], ph.Cb[:, d, r, gl, :], Hb[r][:, ts(tt)], start=first, stop=last),
                            reads=[ph.BCb, BHb[r]], writes=[pb])
                    nmm[g] += 1
            return scan, post

        def merge2(a, b):
            out = []
            ia = ib = 0
            while ia < len(a) or ib < len(b):
                if ia < len(a) and (ib >= len(b) or ia * len(b) <= ib * len(a)):
                    out.append(a[ia]); ia += 1
                else:
                    out.append(b[ib]); ib += 1
            return out

        def emit(lst):
            for fn, rd, wr in lst:
                if rd is None:
                    fn()
                else:
                    S.dve(fn, reads=rd, writes=wr)

        def load_cb():
            S.dma("pool", lambda e: e.dma_start(
                out=ph.Cb[:], in_=d_s5c[j, ct].rearrange("p (d r g c) -> p d r g c", d=2, r=2, g=4)), writes=[ph.BCb])

        if carry["post"] is None:
            load_cb()

        def capture_prep(ct_, d_):
            cap = []
            orig_add = S.add

            def fake_add(eng, fn, reads=(), writes=(), dma=False):
                cap.append((lambda eng=eng, fn=fn, reads=reads, writes=writes, dma=dma: orig_add(eng, fn, reads, writes, dma),
                            None, None))
            S.add = fake_add
            try:
                s5_prep(j, ct_, d_)
            finally:
                S.add = orig_add
            return cap

        prev_h2, prev_post = carry["h2"], carry["post"]
        carried = carry["post"] is not None
        for d in range(2):
            for gl in range(4):
                if gl == 0 and d == 0 and ct == 0:
                    s5_prep(j, ct, d)
                sP, postP = tile("P", d, gl)
                sS, postS = tile("S", d, gl)
                prep_next = []
                if gl == 3 and d == 0:
                    prep_next = capture_prep(ct, 1)
                elif gl == 3 and d == 1 and ct < 7:
                    prep_next = capture_prep(ct + 1, 0)
                half = len(sS) // 2
                h1, h2 = sS[:half], sS[half:]
                n2 = len(prev_h2)
                a_, b_ = int(n2 * 0.25), int(n2 * 0.65)
                np_ = int(len(sP) * 0.4)
                emit(prev_h2[:a_])
                emit(merge2(prev_h2[a_:b_], sP[:np_]))
                emit(merge2(merge2(merge2(prev_h2[b_:], h1), sP[np_:]), prep_next))
                if carried and d == 0 and gl == 0:
                    prev_post()
                    carry["evac"]()
                    load_cb()
                    postP()
                else:
                    postP()
                    if prev_post is not None:
                        prev_post()
                prev_h2, prev_post = h2, postS
        carry["h2"], carry["post"] = prev_h2, prev_post

        def evac_fn():
          for g in ("P", "S"):
              for tt in range(GROUPS[g]["ntok"] // 512):
                  ps, pb = psy[g][tt]
                  tm, tb = ph.t512.next()
                  S.dve(lambda e, ps=ps, tm=tm, tt=tt, g=g: e.scalar_tensor_tensor(
                      out=tm[:], in0=h[g][:, ct, ts(tt)], scalar=ssmd[:, j, ct:ct + 1], in1=ps[:], op0=ALU.mult, op1=ALU.add),
                      reads=[pb, bh[g][ct][tt], Bconst], writes=[tb])
                  S.act(lambda e, tm=tm, tt=tt, g=g: e.activation(out=h[g][:, ct, ts(tt)], in_=tm[:], func=AF.Gelu_apprx_tanh),
                        reads=[tb], writes=[bh[g][ct][tt]])
        carry["evac"] = evac_fn

    def s5_glu(g, i):
        ph = CURPH[0]
        G = GROUPS[g]
        cj = G["cj"]
        j = i // 2
        ntt = G["ntok"] // 512
        for oc in range(8):
            wt, wb = ph.wbig.next()
            wv = wt[:, 0:2048].rearrange("p (k z c) -> p k z c", k=8, z=2)
            for z in range(2):
                c0 = z * 1024 + oc * 128
                S.dma("pool", lambda e, wv=wv, z=z, c0=c0: e.dma_start(
                    out=wv[:, :, z, :], in_=d_wglu[j, :, c0:c0 + 128].rearrange("(k p) c -> p k c", p=128)), writes=[wb])
            for tt in range(ntt):
                pss = []
                for z in range(2):
                    ps, pb = psrot.next()
                    for kt in range(8):
                        S.pe(lambda e, ps=ps, wv=wv, z=z, kt=kt, tt=tt: e.matmul(
                            ps[:], wv[:, kt, z, :], h[g][:, kt, ts(tt)], start=(kt == 0), stop=(kt == 7)),
                            reads=[wb, bh[g][kt][tt]], writes=[pb])
                    pss.append((ps, pb))
                s2, s2b = ph.t512.next()
                S.act(lambda e, s2=s2, ps=pss[1][0], oc=oc: e.activation(
                    out=s2[:], in_=ps[:], func=AF.Sigmoid, bias=bglu[:, j, 8 + oc:9 + oc]),
                    reads=[pss[1][1], Bconst], writes=[s2b])
                S.dve(lambda e, s2=s2, ps=pss[0][0], oc=oc: e.scalar_tensor_tensor(
                    out=s2[:], in0=ps[:], scalar=bglu[:, j, oc:oc + 1], in1=s2[:], op0=ALU.add, op1=ALU.mult),
                    reads=[pss[0][1], s2b, Bconst], writes=[s2b])
                S.dve(lambda e, s2=s2, oc=oc, tt=tt: e.scalar_tensor_tensor(
                    out=x[g][:, oc, ts(tt)], in0=s2[:], scalar=modL[i][:, 16 + oc, cj:cj + 1], in1=x[g][:, oc, ts(tt)],
                    op0=ALU.mult, op1=ALU.add), reads=[s2b, Bmods[i], bx[g][oc][tt]], writes=[bx[g][oc][tt]])

    def s5_layer(i):
        j = i // 2
        begin_phase("s5m")
        s5_qprep(j)
        for li in ((1, 2) if i == 0 else (3,)):
            pending.extend(mod_chunk_emitters(li))
        for ct in range(8):
            s5_main_ct(j, ct)
        s5_flush()
        while pending:
            pending.pop(0)()
        check_stop(f"s5main{i}")
        begin_phase("s5g")
        for g in GROUPS:
            s5_glu(g, i)

    try:
        for i in range(DEPTH):
            if i == 0:
                begin_phase("mod")
                compute_mod(i)
            check_stop(f"mod{i}")
            if i % 2 == 0:
                begin_phase("norm")
                derive(i)
                for g in GROUPS:
                    norm_h(g, i, 1, list(range(GROUPS[g]["ntok"] // 512)))
                check_stop(f"norm{i}")
                s5_layer(i)
            else:
                begin_phase("four")
                derive(i)
                for g in GROUPS:
                    fourier(g, i)
            check_stop(f"mix{i}")
            begin_phase("ffn")
            ffn_all(i)
            check_stop(f"ffn{i}")
    except _Stop:
        pass
    out_ops = set()
    if stop is not None and stop.startswith("s5prep"):
        ph = CURPH[0]
        dB = dout("dbgB", [128, 1024])
        dT = dout("dbgT", [128, 8 * 256])
        dTab = dout("dbgTab", [128, 1280])
        dPw = dout("dbgPw", [128, 2 * 704])
        rw = [ph.Bprep, ph.Bq]
        op = S.dma("pool", lambda e: e.dma_start(out=dB, in_=ph.Bblk[:].rearrange("p a b c -> p (a b c)")), reads=rw); out_ops.add(op.idx)
        for k in range(8):
            op = S.dma("sp", lambda e, k=k: e.dma_start(out=dT[:, k * 256:(k + 1) * 256], in_=ph.tmpq[k][:]), reads=rw); out_ops.add(op.idx)
        op = S.dma("sp", lambda e: e.dma_start(out=dTab, in_=ph.tab[:].rearrange("p a b -> p (a b)")), reads=rw); out_ops.add(op.idx)
        op = S.dma("sp", lambda e: e.dma_start(out=dPw[:, 0:704], in_=ph.pwre[:].rearrange("p a b -> p (a b)")), reads=rw); out_ops.add(op.idx)
        op = S.dma("sp", lambda e: e.dma_start(out=dPw[:, 704:1408], in_=ph.pwim[:].rearrange("p a b -> p (a b)")), reads=rw); out_ops.add(op.idx)
        S.emit(final_wait_ops=out_ops)
        es.close()
        return nc

    ph = begin_phase("final")
    for g in GROUPS:
        ntt = GROUPS[g]["ntok"] // 512
        outs = {}

        def out_fn(ct, tt, outs=outs):
            tm, tb = ph.outrot.next()
            outs[(ct, tt)] = (tm, tb)
            return tm[:]

        for tt in range(ntt):
            norm(g, [tt], lambda ct: gfin[:, ct:ct + 1], lambda ct: None, out_fn, [Bconst],
                 lambda ct, tt, outs=outs: [outs[(ct, tt)][1]])
            for ct in range(8):
                tm, tb = outs[(ct, tt)]
                op = S.dma("sp", lambda e, tm=tm, ct=ct, tt=tt, g=g: e.dma_start(
                    out=d_y[g][ct * 128:(ct + 1) * 128, ts(tt)], in_=tm[:]), reads=[tb])
                out_ops.add(op.idx)
    for blk in range(4):
        ps, pb = psrot.next()
        S.pe(lambda e, ps=ps, blk=blk: e.transpose(ps[:, 0:128], finsb[:, blk * 128:(blk + 1) * 128], identf[:]),
             reads=[Bfin, Bconst], writes=[pb])
        tm, tb = ph.t512.next()
        S.act(lambda e, ps=ps, tm=tm: e.activation(out=tm[:, 0:128], in_=ps[:, 0:128], func=AF.Copy),
              reads=[pb], writes=[tb])
        op = S.dma("sp", lambda e, tm=tm, blk=blk: e.dma_start(out=d_fin[blk * 128:(blk + 1) * 128, :], in_=tm[:, 0:128]),
                   reads=[tb])
        out_ops.add(op.idx)

    if stop is not None:
        begin_phase("final")
        d_hd = {g: dout("hd" + g, [1024, GROUPS[g]["ntok"]]) for g in GROUPS}
        d_xd = {g: dout("xd" + g, [1024, GROUPS[g]["ntok"]]) for g in GROUPS}
        d_modo = dout("modo", [128, 96])
        allb = [b for g in GROUPS for ct in range(8) for b in bh[g][ct]] + [b for g in GROUPS for ct in range(8) for b in bx[g][ct]]
        for g in GROUPS:
            for ct in range(8):
                op = S.dma("pool", lambda e, g=g, ct=ct: e.dma_start(out=d_hd[g][ct * 128:(ct + 1) * 128, :], in_=h[g][:, ct, :]),
                           reads=allb)
                out_ops.add(op.idx)
                op = S.dma("sp", lambda e, g=g, ct=ct: e.dma_start(out=d_xd[g][ct * 128:(ct + 1) * 128, :], in_=x[g][:, ct, :]),
                           reads=allb)
                out_ops.add(op.idx)
        op = S.dma("sp", lambda e: e.dma_start(out=d_modo, in_=modL[0].rearrange("p a b -> p (a b)")), reads=[Bmods[0]])
        out_ops.add(op.idx)
    S.emit(final_wait_ops=out_ops)
    es.close()
    return nc


def _fm(v):
    v = np.asarray(v, np.float32)
    lead = v.shape[:-1]
    n = v.shape[-1] // 128
    r = v.reshape(lead + (n, 128))
    r = np.moveaxis(r, -1, 0)
    return np.ascontiguousarray(r.reshape(128, -1))


def _dft_tables():
    import ml_dtypes
    bf = ml_dtypes.bfloat16
    k = np.arange(256, dtype=np.float64)
    ang = 2 * np.pi * np.outer(k, k) / 256.0
    s1 = 1.0 / 16.0
    ckc = (np.cos(ang) * s1).reshape(2, 128, 256)
    cks = (-np.sin(ang) * s1).reshape(2, 128, 256)
    ck = np.stack([ckc, cks], 0)
    ck = np.ascontiguousarray(ck.transpose(2, 0, 1, 3).reshape(128, -1)).astype(np.float32)
    sp = 1.0 / 16.0
    clpc = (np.cos(ang) * sp).reshape(2, 128, 256)
    clps = (np.sin(ang) * sp).reshape(2, 128, 256)
    clp = np.stack([clpc, clps], 0)
    clp = np.ascontiguousarray(clp.transpose(2, 0, 1, 3).reshape(128, -1)).astype(np.float32)
    l = np.arange(2048, dtype=np.int64)
    m = np.outer(l, l) % 2048
    angL = 2 * np.pi * m.astype(np.float64) / 2048.0
    ss = 1.0 / math.sqrt(2048.0)
    cls = np.stack([np.cos(angL) * ss, np.sin(angL) * ss], 0).astype(np.float32).astype(bf)
    return ck, clp, cls


_CACHE = {}


def kernel(x_prompt, x_sample, state_ssm_re, state_ssm_im, c, c_ctx,
           w_ada, b_ada, g_mix, g_ffn,
           ssm_lam_re, ssm_lam_im, ssm_log_dt, ssm_b_re, ssm_b_im, ssm_c_re, ssm_c_im, ssm_d,
           w_glu, b_glu, w_fourier, b_fourier,
           w_up, conv_w, conv_b, w_down, g_final):
    f32 = np.float32
    A = lambda v: np.ascontiguousarray(np.asarray(v, f32))
    if "nc" not in _CACHE:
        _CACHE["nc"] = build_program()
        _CACHE["dft"] = _dft_tables()
    nc = _CACHE["nc"]
    ck, clp, cls = _CACHE["dft"]

    lam_re, lam_im, log_dt = A(ssm_lam_re), A(ssm_lam_im), A(ssm_log_dt)
    b_re, b_im, c_re, c_im = A(ssm_b_re), A(ssm_b_im), A(ssm_c_re), A(ssm_c_im)
    s5tab = np.zeros((2, 8, 128, 2, 5, 2, 2, 64), f32)
    s5c = np.zeros((2, 8, 128, 2, 2, 4, 128), f32)
    for ct in range(8):
        for gi in range(8):
            g = ct * 8 + gi
            half, gpl, gpar = gi // 4, (gi % 4) // 2, gi % 2
            gl = gi // 2
            rows = slice(gi * 16, gi * 16 + 16)
            s5tab[:, ct, rows, :, 3, gpl, gpar, :] = np.transpose(b_re[:, :, g], (0, 3, 1, 2))
            s5tab[:, ct, rows, :, 4, gpl, gpar, :] = np.transpose(b_im[:, :, g], (0, 3, 1, 2))
            prow = slice(gpar * 64, gpar * 64 + 64)
            s5c[:, ct, prow, :, 0, gl, gi * 16:gi * 16 + 16] = np.transpose(c_re[:, :, g], (0, 3, 1, 2))
            s5c[:, ct, prow, :, 1, gl, gi * 16:gi * 16 + 16] = np.transpose(c_im[:, :, g], (0, 3, 1, 2))
        for hf in range(2):
            for gpl in range(2):
                for gpar in range(2):
                    g = ct * 8 + hf * 4 + gpl * 2 + gpar
                    rows = slice(hf * 64, hf * 64 + 64)
                    s5tab[:, ct, rows, :, 0, gpl, gpar, :] = lam_re[:, None, :, g, :]
                    s5tab[:, ct, rows, :, 1, gpl, gpar, :] = lam_im[:, None, :, g, :]
                    s5tab[:, ct, rows, :, 2, gpl, gpar, :] = log_dt[:, None, :, g, None]
    s5tab = s5tab.reshape(2, 8, 128, -1)
    s5c = s5c.reshape(2, 8, 128, -1)

    def qlay(a):
        a = a.reshape(2, 2, 32, 2, 64)
        return np.ascontiguousarray(a.transpose(3, 4, 0, 1, 2).reshape(128, 128))

    ident = np.eye(128, dtype=f32)
    common = {
        "w_ada": A(w_ada), "badaT": _fm(b_ada), "gmixT": _fm(g_mix), "gffnT": _fm(g_ffn), "gfinT": _fm(g_final),
        "w_glu": A(w_glu), "bgluT": _fm(b_glu), "w_fourier": A(w_fourier), "bfourT": _fm(b_fourier),
        "w_up": A(w_up), "convwT": _fm(conv_w), "convbT": _fm(conv_b), "w_down": A(w_down), "ssmdT": _fm(ssm_d),
        "s5tab": s5tab, "s5c": s5c, "ident": ident, "ck": ck, "clp": clp, "cls": cls,
    }
    xp, xs = A(x_prompt), A(x_sample)
    st_re, st_im = A(state_ssm_re), A(state_ssm_im)
    cc, cctx = A(c), A(c_ctx)
    ldt_q = np.broadcast_to(log_dt[..., None], lam_re.shape)
    in_maps = []
    for core in range(8):
        b = core % 2
        m = dict(common)
        m["xpT"] = np.ascontiguousarray(xp[2 * core:2 * core + 2].reshape(512, 1024).T)
        m["xsT"] = np.ascontiguousarray(xs[b].T)
        cond = np.stack([cctx, cc[b]], -1)
        m["cond"] = np.ascontiguousarray(cond.reshape(8, 128, 2).transpose(1, 0, 2).reshape(128, 16))
        m["s5q"] = np.ascontiguousarray(np.stack(
            [qlay(lam_re), qlay(lam_im), qlay(np.ascontiguousarray(ldt_q)), qlay(st_re[b]), qlay(st_im[b])], 1
        ).reshape(128, 5 * 128))
        in_maps.append(m)
    res = run_bass_kernel_spmd(nc, in_maps, core_ids=list(range(8)))
    R = res.results
    if DEBUG_STOP is not None:
        return R
    y_prompt = np.empty((16, 256, 1024), f32)
    y_sample = np.empty((2, 2048, 1024), f32)
    new_re = np.empty((16, 2, 2, 64, 64), f32)
    new_im = np.empty((16, 2, 2, 64, 64), f32)
    for core in range(8):
        y_prompt[2 * core:2 * core + 2] = R[core]["ypT"].T.reshape(2, 256, 1024)
        fin = R[core]["fin"].reshape(2, 2, 32, 2, 2, 2, 64)
        for s in range(2):
            for r, dst in ((0, new_re), (1, new_im)):
                v = fin[:, :, :, r, s]
                dst[2 * core + s] = v.reshape(2, 2, 64, 64)
    for b in range(2):
        y_sample[b] = R[b]["ysT"].T
    return (y_prompt, y_sample, new_re, new_im)
```

```python
import math
from contextlib import ExitStack

import numpy as np
import concourse.bass as bass
import concourse.mybir as mybir
from concourse.bass_utils import run_bass_kernel_spmd

F32 = mybir.dt.float32
BF16 = mybir.dt.bfloat16
I32 = mybir.dt.int32
AF = mybir.ActivationFunctionType
ALU = mybir.AluOpType

D_MODEL = 1024
DEPTH = 4
D_FF = 2816
NH = D_FF // 128
EPS = 1e-6
COMPUTE = ("pe", "act", "dve", "pool")
NDSEM = 16
USE_POOL = False


class Buf:
    __slots__ = ("name", "lw", "rd")

    def __init__(self, name=""):
        self.name = name
        self.lw = None
        self.rd = []


class Op:
    __slots__ = ("eng", "fn", "deps", "dma", "idx", "sig", "cnt", "dsem", "dval", "prev_on_sem")

    def __init__(self, eng, fn, dma):
        self.eng = eng
        self.fn = fn
        self.dma = dma
        self.deps = set()
        self.sig = False
        self.cnt = None
        self.dsem = None
        self.dval = None
        self.prev_on_sem = None


class Sched:
    def __init__(self, nc):
        self.nc = nc
        self.ops = []
        self.phase_buf = Buf("phase")

    def add(self, eng, fn, reads=(), writes=(), dma=False):
        op = Op(eng, fn, dma)
        op.idx = len(self.ops)
        reads = list(reads)
        if self.phase_buf not in writes:
            reads.append(self.phase_buf)
        for b in reads:
            if b.lw is not None:
                op.deps.add(b.lw)
        for b in writes:
            if b.lw is not None:
                op.deps.add(b.lw)
            last = {}
            for r in b.rd:
                rop = self.ops[r]
                if rop.dma or rop.eng == "pool":
                    op.deps.add(r)
                elif r > last.get(rop.eng, -1):
                    last[rop.eng] = r
            op.deps.update(last.values())
        for b in reads:
            b.rd.append(op.idx)
        for b in writes:
            b.lw = op.idx
            b.rd = []
        op.deps.discard(op.idx)
        self.ops.append(op)
        return op

    def pe(self, fn, reads=(), writes=()):
        return self.add("pe", fn, reads, writes)

    def act(self, fn, reads=(), writes=()):
        return self.add("act", fn, reads, writes)

    def dve(self, fn, reads=(), writes=()):
        return self.add("dve", fn, reads, writes)

    def dma(self, q, fn, reads=(), writes=()):
        return self.add(q, fn, reads, writes, dma=True)

    def emit(self, final_wait_ops=()):
        nc = self.nc
        ops = self.ops
        for op in ops:
            for d in op.deps:
                dop = ops[d]
                if dop.dma:
                    continue
                if dop.eng == op.eng and dop.eng == "pe" and not op.dma:
                    continue
                dop.sig = True
        cnt = {e: 0 for e in COMPUTE}
        for op in ops:
            if not op.dma and op.sig:
                cnt[op.eng] += 1
                op.cnt = cnt[op.eng]
        self.maxcnt = dict(cnt)
        queues = sorted({op.eng for op in ops if op.dma})
        es = ExitStack()
        esem = {e: es.enter_context(nc.semaphore("c_" + e)) for e in COMPUTE}
        dsems = {q: [es.enter_context(nc.semaphore(f"d_{q}_{i}")) for i in range(NDSEM)] for q in queues}
        dcount = {q: [0] * NDSEM for q in queues}
        dlast = {q: [None] * NDSEM for q in queues}
        rr = {q: 0 for q in queues}
        for op in ops:
            if op.dma:
                q = op.eng
                i = rr[q]
                rr[q] = (i + 1) % NDSEM
                dcount[q][i] += 16
                op.dsem = dsems[q][i]
                op.dval = dcount[q][i]
                op.prev_on_sem = dlast[q][i]
                dlast[q][i] = op.idx
        engs = sorted({op.eng for op in ops})
        block = es.enter_context(nc.Block())
        engobj = {"pe": "tensor", "act": "scalar", "dve": "vector", "pool": "gpsimd", "sp": "sync"}
        fin_ops = [op for op in ops if op.dma and op.idx in final_wait_ops]

        def make(ename):
            def body(e):
                waited = {}

                def wait(sem, val):
                    k = id(sem)
                    if waited.get(k, 0) >= val:
                        return
                    waited[k] = val
                    e.wait_ge(sem, val)

                for op in ops:
                    if op.eng != ename:
                        continue
                    for d in sorted(op.deps):
                        dop = ops[d]
                        if dop.dma:
                            wait(dop.dsem, dop.dval)
                        else:
                            if dop.eng == ename and ename == "pe" and not op.dma:
                                continue
                            wait(esem[dop.eng], dop.cnt)
                    if op.dma:
                        if op.prev_on_sem is not None:
                            p = ops[op.prev_on_sem]
                            wait(p.dsem, p.dval)
                        ins = op.fn(e)
                        ins.then_inc(op.dsem, 16)
                    else:
                        ins = op.fn(e)
                        if op.sig:
                            ins.then_inc(esem[ename], 1)
                if ename == "sp":
                    for op in fin_ops:
                        wait(op.dsem, op.dval)
            return body

        if "sp" not in engs:
            engs.append("sp")
        for ename in engs:
            getattr(block, engobj[ename])(make(ename))
        es.close()


class Rot:
    def __init__(self, items):
        self.items = items
        self.i = 0

    def next(self):
        it = self.items[self.i]
        self.i = (self.i + 1) % len(self.items)
        return it


GROUPS = {
    "P": dict(ntok=512, L=256, nseq=2, R=256, cj=0, passes=[(0, 512)]),
    "S": dict(ntok=2048, L=2048, nseq=1, R=64, cj=1, passes=[(0, 1024), (1024, 1024)]),
}


DEBUG_STOP = None


class _Stop(Exception):
    pass


def build_program():
    nc = bass.Bass("TRN2", target_bir_lowering=False)
    stop = DEBUG_STOP

    def check_stop(name):
        if stop == name:
            raise _Stop()

    es = ExitStack()
    S = Sched(nc)

    def din(n, shape, dt=F32):
        return nc.dram_tensor(n, list(shape), dt, kind="ExternalInput").ap()

    def dout(n, shape):
        return nc.dram_tensor(n, list(shape), F32, kind="ExternalOutput").ap()

    def sb(n, shape, dt=F32):
        return es.enter_context(nc.sbuf_tensor("s_" + n, list(shape), dt))

    d_x = {"P": din("xpT", [1024, 512]), "S": din("xsT", [1024, 2048])}
    d_y = {"P": dout("ypT", [1024, 512]), "S": dout("ysT", [1024, 2048])}
    d_fin = dout("fin", [512, 128])
    d_cond = din("cond", [128, 16])
    d_wada = din("w_ada", [4, 1024, 6144])
    d_bada = din("badaT", [128, 4 * 48])
    d_gmix = din("gmixT", [128, 32])
    d_gffn = din("gffnT", [128, 32])
    d_gfin = din("gfinT", [128, 8])
    d_wglu = din("w_glu", [2, 1024, 2048])
    d_bglu = din("bgluT", [128, 32])
    d_wfour = din("w_fourier", [2, 1024, 1024])
    d_bfour = din("bfourT", [128, 16])
    d_wup = din("w_up", [4, 1024, 2 * D_FF])
    d_convw = din("convwT", [128, 4 * 3 * 44])
    d_convb = din("convbT", [128, 4 * 44])
    d_wdown = din("w_down", [4, D_FF, 1024])
    d_ssmd = din("ssmdT", [128, 16])
    d_s5tab = din("s5tab", [2, 8, 128, 2 * 5 * 256])
    d_s5c = din("s5c", [2, 8, 128, 2 * 2 * 4 * 128])
    d_s5q = din("s5q", [128, 5 * 128])
    d_ident = din("ident", [128, 128])
    d_ck = din("ck", [128, 2 * 2 * 256])
    d_clp = din("clp", [128, 2 * 2 * 256])
    d_cls = din("cls", [2, 2048, 2048], BF16)

    x = {g: sb("x" + g, [128, 8, GROUPS[g]["ntok"]]) for g in GROUPS}
    h = {g: sb("h" + g, [128, 8, GROUPS[g]["ntok"]], BF16) for g in GROUPS}
    bx = {g: [[Buf() for _ in range(GROUPS[g]["ntok"] // 512)] for _ in range(8)] for g in GROUPS}
    bh = {g: [[Buf() for _ in range(GROUPS[g]["ntok"] // 512)] for _ in range(8)] for g in GROUPS}

    ones_bf = sb("ones_bf", [128, 128], BF16)
    epsT = sb("epsT", [128, 1])
    halfpi = sb("halfpi", [128, 1])
    identf = sb("identf", [128, 128])
    cond = sb("cond", [128, 16])
    scond = sb("scond", [128, 8, 2], BF16)
    bada = sb("bada", [128, 4, 48])
    gmix = sb("gmix", [128, 4, 8])
    gffn = sb("gffn", [128, 4, 8])
    gfin = sb("gfin", [128, 8])
    bglu = sb("bglu", [128, 2, 16])
    bfour = sb("bfour", [128, 2, 8])
    ssmd = sb("ssmd", [128, 2, 8])
    s5q = sb("s5q", [128, 5, 128])
    modT_all = sb("modT", [128, 4, 48, 2])
    modL = [modT_all[:, li] for li in range(4)]
    Bmods = [Buf() for _ in range(4)]
    Bder = Buf()
    A1 = sb("A1", [128, 8, 2])
    A2 = sb("A2", [128, 8, 2])
    g1b = sb("g1b", [128, 8, 2])
    finsb = sb("finsb", [128, 512])
    Bconst = Buf("const")
    Bmod = Buf("mod")
    Bfin = Buf("fin")

    ARENA_F32 = 20300
    dummy = sb("dummy", [128, 2])
    arena = sb("arena", [128, ARENA_F32])

    class PH:
        pass

    CURPH = [None]

    def begin_phase(kind):
        ph = PH()
        CURPH[0] = ph
        S.dve(lambda e: e.memset(dummy[:], 0.0), writes=[S.phase_buf])
        off = [0]

        def alloc(shape, dt=F32):
            n = 1
            for d_ in shape[1:]:
                n *= d_
            nbytes = n * (4 if dt in (F32, I32) else 2)
            n4 = (nbytes + 3) // 4
            assert off[0] + n4 <= ARENA_F32, (kind, off[0], n4)
            v = arena[:, off[0]:off[0] + n4]
            off[0] += n4
            if dt != F32:
                v = v.bitcast(dt)
            if len(shape) == 3:
                v = v.rearrange("p (a b) -> p a b", a=shape[1])
            elif len(shape) == 4:
                v = v.rearrange("p (a b c) -> p a b c", a=shape[1], b=shape[2])
            elif len(shape) == 5:
                v = v.rearrange("p (a b c d) -> p a b c d", a=shape[1], b=shape[2], c=shape[3])
            return v

        def rot(n, shape, dt=F32):
            return Rot([(alloc(shape, dt), Buf()) for _ in range(n)])

        def normtmps(nb=2):
            ph.sqb = rot(2, [128, 512], BF16)
            ph.rsd = rot(nb, [128, 2, 512])

        if kind == "mod":
            ph.wbig = rot(2, [128, 6144], BF16)
            ph.wf32 = rot(2, [128, 6144])
        elif kind == "norm":
            normtmps()
            ph.t512 = rot(4, [128, 512])
        elif kind == "s5m":
            ph.XS = [alloc([128, 2, 2048]) for _ in range(2)]
            ph.BXS = [Buf(), Buf()]
            ph.BXS2 = [Buf(), Buf()]
            ph.BXP2 = Buf()
            ph.XP = alloc([128, 2, 512])
            ph.BXP = Buf()
            ph.HbS = [alloc([128, 2048], BF16) for _ in range(2)]
            ph.BHbS = [Buf(), Buf()]
            ph.HbP = [alloc([128, 512], BF16) for _ in range(2)]
            ph.BHbP = [Buf(), Buf()]
            ph.modw = alloc([128, 8, 128], BF16)
            ph.Bmodw = Buf()
            ph.tab = alloc([128, 5, 256])
            ph.Bblk = alloc([128, 2, 2, 256], BF16)
            ph.Cb = alloc([128, 2, 2, 4, 128], BF16)
            ph.BCb = Buf()
            ph.tmpq = [alloc([128, 256]) for _ in range(8)]
            ph.inj2re = alloc([128, 64])
            ph.inj2im = alloc([128, 64])
            ph.Bprep = Buf()
            ph.Btab = Buf()
            ph.t512 = rot(1, [128, 512])
            ph.pwre = alloc([128, 64, 11])
            ph.pwim = alloc([128, 64, 11])
            ph.pwimn = alloc([128, 64, 11])
            ph.injre = alloc([128, 64])
            ph.injim = alloc([128, 64])
            ph.Bq = Buf()
        elif kind == "s5g":
            ph.wbig = rot(2, [128, 2048], BF16)
            ph.t512 = rot(4, [128, 512])
        elif kind == "four":
            normtmps()
            ph.t512 = rot(4, [128, 512])
            ph.pq = alloc([128, 2, 16, 256], BF16)
            ph.Bpq = Buf()
            ph.wbig = rot(2, [128, 8192], BF16)
            ph.ck = alloc([128, 2, 2, 256], BF16)
            ph.clp = alloc([128, 2, 2, 256], BF16)
            ph.Bck = Buf()
            S.dma("pool", lambda e: e.dma_start(out=ph.ck, in_=d_ck.rearrange("p (a b c) -> p a b c", a=2, b=2)), writes=[ph.Bck])
            S.dma("pool", lambda e: e.dma_start(out=ph.clp, in_=d_clp.rearrange("p (a b c) -> p a b c", a=2, b=2)), writes=[ph.Bck])
        elif kind == "ffn":
            normtmps(1)
            ph.t512 = rot(6, [128, 512])
            ph.a_t = alloc([128, 6 * 2560], BF16)
            ph.Ba = [Buf() for _ in range(6)]
            ph.wu = rot(2, [128, 8, 2, 256], BF16)
            ph.wd = rot(1, [128, 6, 1024], BF16)
            ph.convw = alloc([128, 4, 3, 44])
            ph.convb = alloc([128, 4, 44])
            ph.Bcv = Buf()
            S.dma("sp", lambda e: e.dma_start(out=ph.convw, in_=d_convw.rearrange("p (a b c) -> p a b c", a=4, b=3)), writes=[ph.Bcv])
            S.dma("sp", lambda e: e.dma_start(out=ph.convb, in_=d_convb.rearrange("p (a b) -> p a b", a=4)), writes=[ph.Bcv])
        elif kind == "final":
            normtmps()
            ph.t512 = rot(4, [128, 512])
            ph.outrot = rot(8, [128, 512])
        return ph

    psb = [(es.enter_context(nc.psum_tensor(f"ps{i}", [128, 512], F32)), Buf()) for i in range(8)]
    psrot = Rot(psb)

    def ld(dst, src, bufs, q="sp"):
        S.dma(q, lambda e, dst=dst, src=src: e.dma_start(out=dst, in_=src), writes=bufs)

    ld(cond[:], d_cond, [Bconst])
    ld(bada[:], d_bada.rearrange("p (a b) -> p a b", a=4), [Bconst])
    ld(gmix[:], d_gmix.rearrange("p (a b) -> p a b", a=4), [Bconst])
    ld(gffn[:], d_gffn.rearrange("p (a b) -> p a b", a=4), [Bconst])
    ld(gfin[:], d_gfin, [Bconst])
    ld(bglu[:], d_bglu.rearrange("p (a b) -> p a b", a=2), [Bconst])
    ld(bfour[:], d_bfour.rearrange("p (a b) -> p a b", a=2), [Bconst])
    ld(ssmd[:], d_ssmd.rearrange("p (a b) -> p a b", a=2), [Bconst])
    ld(s5q[:], d_s5q.rearrange("p (a b) -> p a b", a=5), [Bconst])
    ld(identf[:], d_ident, [Bconst])
    for g in GROUPS:
        for ct in range(8):
            S.dma("sp", lambda e, g=g, ct=ct: e.dma_start(out=x[g][:, ct, :], in_=d_x[g][ct * 128:(ct + 1) * 128, :]),
                  writes=bx[g][ct])
    S.dve(lambda e: e.memset(ones_bf[:], 1.0), writes=[Bconst])
    S.dve(lambda e: e.memset(epsT[:], EPS), writes=[Bconst])
    S.dve(lambda e: e.memset(halfpi[:], math.pi / 2), writes=[Bconst])
    S.dve(lambda e: e.memset(finsb[:], 0.0), writes=[Bfin])
    S.act(lambda e: e.activation(out=scond[:], in_=cond[:].rearrange("p (a b) -> p a b", b=2), func=AF.Silu),
          reads=[Bconst], writes=[Bconst])

    def ts(tt):
        return slice(tt * 512, (tt + 1) * 512)

    def compute_mod(i):
        ph = CURPH[0]
        for c in range(8):
            wf, wfb = ph.wf32.next()
            wfv = wf[:, 0:6144].rearrange("p (k c) -> p k c", k=8)
            for hq, q_ in ((0, "sp"), (1, "act")):
                S.dma(q_, lambda e, wfv=wfv, c=c, hq=hq: e.dma_start(
                    out=wfv[:, hq * 4:(hq + 1) * 4, :],
                    in_=d_wada[i, hq * 512:(hq + 1) * 512, c * 768:(c + 1) * 768].rearrange("(k p) c -> p k c", p=128)),
                    writes=[wfb])
            wt, wb = ph.wbig.next()
            wv = wt[:, 0:6144].rearrange("p (k c) -> p k c", k=8)
            S.dve(lambda e, wt=wt, wf=wf: e.tensor_copy(out=wt[:, 0:6144], in_=wf[:, 0:6144]), reads=[wfb], writes=[wb])
            for t in range(6):
                m = c * 6 + t
                ps, pb = psrot.next()
                for kt in range(8):
                    S.pe(lambda e, ps=ps, wv=wv, t=t, kt=kt: e.matmul(
                        ps[:, 0:2], wv[:, kt, t * 128:(t + 1) * 128], scond[:, kt, :], start=(kt == 0), stop=(kt == 7)),
                        reads=[wb, Bconst], writes=[pb])
                S.act(lambda e, ps=ps, m=m: e.activation(out=modL[i][:, m, :], in_=ps[:, 0:2], func=AF.Identity,
                                                         bias=bada[:, i, m:m + 1]), reads=[pb, Bconst], writes=[Bmods[i]])

    def derive(i):
        for cj in range(2):
            S.dve(lambda e, cj=cj: e.scalar_tensor_tensor(out=A1[:, :, cj], in0=modL[i][:, 8:16, cj], scalar=1.0,
                                                          in1=gmix[:, i, :], op0=ALU.add, op1=ALU.mult),
                  reads=[Bmods[i], Bconst], writes=[Bder])
            S.dve(lambda e, cj=cj: e.scalar_tensor_tensor(out=A2[:, :, cj], in0=modL[i][:, 32:40, cj], scalar=1.0,
                                                          in1=gffn[:, i, :], op0=ALU.add, op1=ALU.mult),
                  reads=[Bmods[i], Bconst], writes=[Bder])
            if i % 2 == 1:
                S.dve(lambda e, cj=cj: e.tensor_tensor(out=g1b[:, :, cj], in0=modL[i][:, 16:24, cj], in1=bfour[:, i // 2, :],
                                                       op=ALU.mult), reads=[Bmods[i], Bconst], writes=[Bder])

    def mod_chunk_emitters(i):
        ems = []
        for m in range(48):
            def em(m=m):
                ph = CURPH[0]
                S.dma("pool", lambda e: e.dma_start(
                    out=ph.modw, in_=d_wada[i, :, m * 128:(m + 1) * 128].rearrange("(k p) c -> p k c", p=128)),
                    writes=[ph.Bmodw])
                ps, pb = psb[5 + (bu_rr[0] % 3)]
                bu_rr[0] += 1
                for kt in range(8):
                    S.pe(lambda e, kt=kt: e.matmul(ps[:, 0:2], ph.modw[:, kt, :], scond[:, kt, :], start=(kt == 0), stop=(kt == 7)),
                         reads=[ph.Bmodw, Bconst], writes=[pb])
                S.act(lambda e: e.activation(out=modL[i][:, m, :], in_=ps[:, 0:2], func=AF.Identity, bias=bada[:, i, m:m + 1]),
                      reads=[pb, Bconst], writes=[Bmods[i]])
            ems.append(em)
        return ems

    def norm(g, tts, scale_fn, bias_fn, out_fn, extra_reads, out_bufs_fn):
        ph = CURPH[0]
        for tt in tts:
            ps, pb = psrot.next()
            for ct in range(8):
                sq, sqbuf = ph.sqb.next()
                S.act(lambda e, sq=sq, ct=ct, tt=tt: e.activation(out=sq[:], in_=x[g][:, ct, ts(tt)], func=AF.Square),
                      reads=[bx[g][ct][tt]], writes=[sqbuf])
                S.pe(lambda e, ps=ps, sq=sq, ct=ct: e.matmul(ps[:], ones_bf[:], sq[:], start=(ct == 0), stop=(ct == 7)),
                     reads=[sqbuf, Bconst], writes=[pb])
            rsd, Brsd = ph.rsd.next()
            S.act(lambda e, ps=ps, rsd=rsd: e.activation(out=rsd[:, 0, :], in_=ps[:], func=AF.Sqrt, scale=1.0 / D_MODEL, bias=epsT[:]),
                  reads=[pb, Bconst], writes=[Brsd])
            S.dve(lambda e, rsd=rsd: e.reciprocal(out=rsd[:, 1, :], in_=rsd[:, 0, :]), reads=[Brsd], writes=[Brsd])
            for ct in range(8):
                tm, tb = ph.t512.next()
                S.dve(lambda e, tm=tm, ct=ct, tt=tt, rsd=rsd: e.tensor_tensor(out=tm[:], in0=x[g][:, ct, ts(tt)], in1=rsd[:, 1, :],
                                                                             op=ALU.mult),
                      reads=[bx[g][ct][tt], Brsd], writes=[tb])
                bias = bias_fn(ct)
                kw = {} if bias is None else {"bias": bias}
                oap = out_fn(ct, tt)
                sc = scale_fn(ct)
                S.act(lambda e, tm=tm, kw=kw, oap=oap, sc=sc: e.activation(out=oap, in_=tm[:], func=AF.Identity,
                                                                           scale=sc, **kw),
                      reads=[tb] + extra_reads, writes=out_bufs_fn(ct, tt))

    def norm_h(g, i, which, tts):
        cj = GROUPS[g]["cj"]
        A = A1 if which == 1 else A2
        sh0 = 0 if which == 1 else 24
        norm(g, tts, lambda ct: A[:, ct, cj:cj + 1], lambda ct: modL[i][:, sh0 + ct, cj:cj + 1],
             lambda ct, tt: h[g][:, ct, ts(tt)], [Bmods[i], Bder], lambda ct, tt: [bh[g][ct][tt]])

    def ffn_all(i):
        ph = CURPH[0]
        tiles = [("S", 0), ("S", 1), ("S", 2), ("S", 3), ("P", 0)]
        toff = {("S", 0): 0, ("S", 1): 512, ("S", 2): 1024, ("S", 3): 1536, ("P", 0): 2048}
        for g in GROUPS:
            norm_h(g, i, 2, list(range(GROUPS[g]["ntok"] // 512)))
        quarters = [(0, 6), (6, 6), (12, 5), (17, 5)]
        for (j0q, nhq) in quarters:
            a3 = ph.a_t[:, 0:nhq * 2560].rearrange("p (j t) -> p j t", j=nhq)
            for c0j in range(0, nhq, 2):
                ncj = min(2, nhq - c0j)
                wt, wb = ph.wu.next()
                for gv in range(2):
                    c0 = gv * D_FF + (j0q + c0j) * 128
                    S.dma("pool", lambda e, wt=wt, gv=gv, c0=c0, ncj=ncj: e.dma_start(
                        out=wt[:, :, gv, 0:ncj * 128], in_=d_wup[i, :, c0:c0 + ncj * 128].rearrange("(k p) c -> p k c", p=128)),
                        writes=[wb])
                for cj in range(ncj):
                    jl = c0j + cj
                    j = j0q + jl
                    for (g, tt) in tiles:
                        R = GROUPS[g]["R"]
                        pss = []
                        for gv in range(2):
                            ps, pb = psrot.next()
                            for kt in range(8):
                                S.pe(lambda e, ps=ps, wt=wt, gv=gv, kt=kt, tt=tt, g=g, cj=cj: e.matmul(
                                    ps[:], wt[:, kt, gv, cj * 128:(cj + 1) * 128], h[g][:, kt, ts(tt)],
                                    start=(kt == 0), stop=(kt == 7)), reads=[wb, bh[g][kt][tt]], writes=[pb])
                            pss.append((ps, pb))
                        outs = []
                        for gv in range(2):
                            ps, pb = pss[gv]
                            col = gv * NH + j
                            tm, tb = ph.t512.next()
                            S.act(lambda e, ps=ps, tm=tm, col=col: e.activation(
                                out=tm[:], in_=ps[:], func=AF.Identity, scale=ph.convw[:, i, 1, col:col + 1],
                                bias=ph.convb[:, i, col:col + 1]), reads=[pb, ph.Bcv], writes=[tb])
                            outs.append((tm, tb))
                        for kk_ in (0, 2):
                            for gv in range(2):
                                ps, pb = pss[gv]
                                tm, tb = outs[gv]
                                col = gv * NH + j
                                p3 = ps[:].rearrange("p (r t) -> p r t", t=R)
                                t3 = tm[:].rearrange("p (r t) -> p r t", t=R)
                                if kk_ == 0:
                                    S.dve(lambda e, p3=p3, t3=t3, col=col, R=R: e.scalar_tensor_tensor(
                                        out=t3[:, :, 1:R], in0=p3[:, :, 0:R - 1], scalar=ph.convw[:, i, 0, col:col + 1],
                                        in1=t3[:, :, 1:R], op0=ALU.mult, op1=ALU.add), reads=[pb, tb, ph.Bcv], writes=[tb])
                                else:
                                    S.dve(lambda e, p3=p3, t3=t3, col=col, R=R: e.scalar_tensor_tensor(
                                        out=t3[:, :, 0:R - 1], in0=p3[:, :, 1:R], scalar=ph.convw[:, i, 2, col:col + 1],
                                        in1=t3[:, :, 0:R - 1], op0=ALU.mult, op1=ALU.add), reads=[pb, tb, ph.Bcv], writes=[tb])
                        (gc, gb), (vc, vb) = outs
                        sg, sgb = ph.t512.next()
                        S.act(lambda e, sg=sg, gc=gc: e.activation(out=sg[:], in_=gc[:], func=AF.Silu),
                              reads=[gb], writes=[sgb])
                        to = toff[(g, tt)]
                        S.dve(lambda e, sg=sg, vc=vc, jl=jl, to=to, a3=a3: e.tensor_tensor(
                            out=a3[:, jl, to:to + 512], in0=sg[:], in1=vc[:], op=ALU.mult),
                            reads=[sgb, vb], writes=[ph.Ba[jl]])
            wt, wb = ph.wd.next()
            S.dma("pool", lambda e, wt=wt, j0q=j0q, nhq=nhq: e.dma_start(
                out=wt[:, 0:nhq, :], in_=d_wdown[i, j0q * 128:(j0q + nhq) * 128, :].rearrange("(j p) c -> p j c", p=128)),
                writes=[wb])
            for ct in range(8):
                for (g, tt) in tiles:
                    cjx = GROUPS[g]["cj"]
                    to = toff[(g, tt)]
                    ps, pb = psrot.next()
                    for jl in range(nhq):
                        S.pe(lambda e, ps=ps, wt=wt, jl=jl, to=to, ct=ct, a3=a3, nhq=nhq: e.matmul(
                            ps[:], wt[:, jl, ct * 128:(ct + 1) * 128], a3[:, jl, to:to + 512], start=(jl == 0),
                            stop=(jl == nhq - 1)), reads=[wb, ph.Ba[jl]], writes=[pb])
                    S.dve(lambda e, ps=ps, ct=ct, tt=tt, g=g, cjx=cjx: e.scalar_tensor_tensor(
                        out=x[g][:, ct, ts(tt)], in0=ps[:], scalar=modL[i][:, 40 + ct, cjx:cjx + 1],
                        in1=x[g][:, ct, ts(tt)], op0=ALU.mult, op1=ALU.add),
                        reads=[pb, Bmods[i], bx[g][ct][tt]], writes=[bx[g][ct][tt]])

    def fourier(g, i):
        ph = CURPH[0]
        G = GROUPS[g]
        cj = G["cj"]
        jf = i // 2
        ntok, L, nseq = G["ntok"], G["L"], G["nseq"]
        nlt = ntok // 128
        ntt = ntok // 512
        norm_h(g, i, 1, list(range(ntt)))
        lts = L // 128
        for q in range(4):
            for lt in range(nlt):
                for tb_ in range(2):
                    ps, pb = psrot.next()
                    for kk in range(2):
                        S.pe(lambda e, ps=ps, kk=kk, lt=lt, tb_=tb_, q=q: e.matmul(
                            ps[:, 0:256], h[g][:, 2 * q + kk, lt * 128:(lt + 1) * 128], ph.ck[:, tb_, kk, :],
                            start=(kk == 0), stop=(kk == 1)),
                            reads=[bh[g][2 * q + kk][lt // 4], ph.Bck], writes=[pb])
                    if tb_ == 0:
                        S.act(lambda e, ps=ps, lt=lt: e.activation(out=ph.pq[:, 0, lt, :], in_=ps[:, 0:256], func=AF.Copy),
                              reads=[pb], writes=[ph.Bpq])
                    else:
                        S.dve(lambda e, ps=ps, lt=lt: e.tensor_copy(out=ph.pq[:, 1, lt, :], in_=ps[:, 0:256]),
                              reads=[pb], writes=[ph.Bpq])
            for s in range(nseq):
                for lb in range(L // 256):
                    if g == "S":
                        wt, wb = ph.wbig.next()
                        tv = wt[:, :].rearrange("p (a l c) -> p a l c", a=2, l=16)
                        for tb_ in range(2):
                            S.dma("sp", lambda e, tv=tv, tb_=tb_, lb=lb: e.dma_start(
                                out=tv[:, tb_, :, :],
                                in_=d_cls[tb_, :, lb * 256:(lb + 1) * 256].rearrange("(l p) c -> p l c", p=128)),
                                writes=[wb])
                        tabs = lambda tb_, l, tv=tv: tv[:, tb_, l, :]
                        tbuf = wb
                    else:
                        tabs = lambda tb_, l: ph.clp[:, tb_, l, :]
                        tbuf = ph.Bck
                    for kk in range(2):
                        ps, pb = psrot.next()
                        n = 0
                        for tb_ in range(2):
                            for l in range(lts):
                                S.pe(lambda e, ps=ps, tb_=tb_, l=l, kk=kk, s=s, tabs=tabs, n=n: e.matmul(
                                    ps[:, 0:256], ph.pq[:, tb_, s * lts + l, kk * 128:(kk + 1) * 128], tabs(tb_, l),
                                    start=(n == 0), stop=(n == 2 * lts - 1)), reads=[ph.Bpq, tbuf], writes=[pb])
                                n += 1
                        tok0 = s * L + lb * 256
                        S.act(lambda e, ps=ps, kk=kk, tok0=tok0, q=q: e.activation(
                            out=h[g][:, 2 * q + kk, tok0:tok0 + 256], in_=ps[:, 0:256], func=AF.Copy),
                            reads=[pb], writes=[bh[g][2 * q + kk][tok0 // 512]])
        for oc in range(8):
            wt, wb = ph.wbig.next()
            wv = wt[:, 0:1024].rearrange("p (k c) -> p k c", k=8)
            S.dma("pool", lambda e, wv=wv, oc=oc: e.dma_start(
                out=wv, in_=d_wfour[jf, :, oc * 128:(oc + 1) * 128].rearrange("(k p) c -> p k c", p=128)), writes=[wb])
            for tt in range(ntt):
                ps, pb = psrot.next()
                for kt in range(8):
                    S.pe(lambda e, ps=ps, wv=wv, kt=kt, tt=tt: e.matmul(
                        ps[:], wv[:, kt, :], h[g][:, kt, ts(tt)], start=(kt == 0), stop=(kt == 7)),
                        reads=[wb, bh[g][kt][tt]], writes=[pb])
                tm, tb = ph.t512.next()
                S.act(lambda e, ps=ps, tm=tm, oc=oc: e.activation(
                    out=tm[:], in_=ps[:], func=AF.Identity, scale=modL[i][:, 16 + oc, cj:cj + 1],
                    bias=g1b[:, oc, cj:cj + 1]), reads=[pb, Bmods[i], Bder], writes=[tb])
                S.dve(lambda e, tm=tm, oc=oc, tt=tt: e.tensor_tensor(
                    out=x[g][:, oc, ts(tt)], in0=x[g][:, oc, ts(tt)], in1=tm[:], op=ALU.add),
                    reads=[tb, bx[g][oc][tt]], writes=[bx[g][oc][tt]])

    TWO_PI = 2.0 * math.pi

    def cexp_ops(xr, xi, W, are, aim, T, bufs):
        ph = CURPH[0]
        rw = bufs
        t0, t1, t2, t3 = [t[:, 0:W] for t in T[:4]]
        ti = T[2][:, 0:W].bitcast(I32)
        S.act(lambda e: e.activation(out=t0, in_=xr, func=AF.Exp), reads=rw, writes=rw)
        S.dve(lambda e: e.tensor_scalar(out=ti, in0=xi, scalar1=1.0 / TWO_PI, scalar2=None, op0=ALU.mult),
              reads=rw, writes=rw)
        S.dve(lambda e: e.tensor_copy(out=t1, in_=ti), reads=rw, writes=rw)
        S.dve(lambda e: e.scalar_tensor_tensor(out=t1, in0=t1, scalar=-TWO_PI, in1=xi, op0=ALU.mult, op1=ALU.add),
              reads=rw, writes=rw)
        S.dve(lambda e: e.tensor_scalar(out=t1, in0=t1, scalar1=math.pi, scalar2=-math.pi, op0=ALU.min, op1=ALU.max),
              reads=rw, writes=rw)
        S.act(lambda e: e.activation(out=t2, in_=t1, func=AF.Sin), reads=rw, writes=rw)
        S.act(lambda e: e.activation(out=t3, in_=t1, func=AF.Abs), reads=rw, writes=rw)
        S.act(lambda e: e.activation(out=t3, in_=t3, func=AF.Sin, scale=-1.0, bias=halfpi[:]), reads=rw + [Bconst], writes=rw)
        S.dve(lambda e: e.tensor_tensor(out=are, in0=t0, in1=t3, op=ALU.mult), reads=rw, writes=rw)
        S.dve(lambda e: e.tensor_tensor(out=aim, in0=t0, in1=t2, op=ALU.mult), reads=rw, writes=rw)

    def s5_qprep(j):
        ph = CURPH[0]
        rw = [ph.Bq, ph.Bprep, Bconst]
        sl = slice(j * 64, (j + 1) * 64)
        T = ph.tmpq
        dtv = T[4][:, 0:64]
        xr = T[5][:, 0:64]
        xi = T[6][:, 0:64]
        S.act(lambda e: e.activation(out=dtv, in_=s5q[:, 2, sl], func=AF.Exp), reads=rw, writes=rw)
        S.dve(lambda e: e.tensor_tensor(out=xr, in0=s5q[:, 0, sl], in1=dtv, op=ALU.mult), reads=rw, writes=rw)
        S.dve(lambda e: e.tensor_tensor(out=xi, in0=s5q[:, 1, sl], in1=dtv, op=ALU.mult), reads=rw, writes=rw)
        cexp_ops(xr, xi, 64, ph.pwre[:, :, 0], ph.pwim[:, :, 0], T, rw)
        tA = T[0][:, 0:64]
        tB = T[1][:, 0:64]
        for k in range(10):
            S.dve(lambda e, k=k: e.tensor_tensor(out=tA, in0=ph.pwre[:, :, k], in1=ph.pwre[:, :, k], op=ALU.mult), reads=rw, writes=rw)
            S.dve(lambda e, k=k: e.tensor_tensor(out=tB, in0=ph.pwim[:, :, k], in1=ph.pwim[:, :, k], op=ALU.mult), reads=rw, writes=rw)
            S.dve(lambda e, k=k: e.tensor_tensor(out=ph.pwre[:, :, k + 1], in0=tA, in1=tB, op=ALU.subtract), reads=rw, writes=rw)
            S.dve(lambda e, k=k: e.scalar_tensor_tensor(out=ph.pwim[:, :, k + 1], in0=ph.pwre[:, :, k], scalar=2.0,
                                                        in1=ph.pwim[:, :, k], op0=ALU.mult, op1=ALU.mult), reads=rw, writes=rw)
        S.dve(lambda e: e.tensor_scalar(out=ph.pwimn[:], in0=ph.pwim[:], scalar1=-1.0, scalar2=None, op0=ALU.mult),
              reads=rw, writes=rw)
        S.dve(lambda e: e.tensor_tensor(out=tA, in0=ph.pwre[:, :, 0], in1=s5q[:, 3, sl], op=ALU.mult), reads=rw, writes=rw)
        S.dve(lambda e: e.tensor_tensor(out=tB, in0=ph.pwim[:, :, 0], in1=s5q[:, 4, sl], op=ALU.mult), reads=rw, writes=rw)
        S.dve(lambda e: e.tensor_tensor(out=ph.injre[:], in0=tA, in1=tB, op=ALU.subtract), reads=rw, writes=rw)
        S.dve(lambda e: e.tensor_tensor(out=tA, in0=ph.pwre[:, :, 0], in1=s5q[:, 4, sl], op=ALU.mult), reads=rw, writes=rw)
        S.dve(lambda e: e.tensor_tensor(out=tB, in0=ph.pwim[:, :, 0], in1=s5q[:, 3, sl], op=ALU.mult), reads=rw, writes=rw)
        S.dve(lambda e: e.tensor_tensor(out=ph.injim[:], in0=tA, in1=tB, op=ALU.add), reads=rw, writes=rw)
        S.dve(lambda e: e.tensor_tensor(out=tA, in0=ph.pwre[:, :, 0], in1=ph.injre[:], op=ALU.mult), reads=rw, writes=rw)
        S.dve(lambda e: e.tensor_tensor(out=tB, in0=ph.pwim[:, :, 0], in1=ph.injim[:], op=ALU.mult), reads=rw, writes=rw)
        S.dve(lambda e: e.tensor_tensor(out=ph.inj2re[:], in0=tA, in1=tB, op=ALU.subtract), reads=rw, writes=rw)
        S.dve(lambda e: e.tensor_tensor(out=tA, in0=ph.pwre[:, :, 0], in1=ph.injim[:], op=ALU.mult), reads=rw, writes=rw)
        S.dve(lambda e: e.tensor_tensor(out=tB, in0=ph.pwim[:, :, 0], in1=ph.injre[:], op=ALU.mult), reads=rw, writes=rw)
        S.dve(lambda e: e.tensor_tensor(out=ph.inj2im[:], in0=tA, in1=tB, op=ALU.add), reads=rw, writes=rw)

    def s5_prep(j, ct, d):
        ph = CURPH[0]
        rw = [ph.Bprep]
        S.dma("sp", lambda e: e.dma_start(
            out=ph.tab[:], in_=d_s5tab[j, ct].rearrange("p (d t c) -> p d t c", d=2, t=5)[:, d]), writes=[ph.Btab])
        S.dve(lambda e: e.memset(dummy[:], 0.0), reads=[ph.Btab], writes=[ph.Bprep])
        T = ph.tmpq
        W = 256
        lamre, lamim, logdt, bre, bim = [ph.tab[:, k, :] for k in range(5)]
        dtv, xr, xi, are, aim = T[4][:, :], T[5][:, :], T[6][:, :], T[7][:, :], T[4][:, :]
        S.act(lambda e: e.activation(out=dtv, in_=logdt, func=AF.Exp), reads=rw, writes=rw)
        S.dve(lambda e: e.tensor_tensor(out=xr, in0=lamre, in1=dtv, op=ALU.mult), reads=rw, writes=rw)
        S.dve(lambda e: e.tensor_tensor(out=xi, in0=lamim, in1=dtv, op=ALU.mult), reads=rw, writes=rw)
        cexp_ops(xr, xi, W, are, aim, T, rw)
        nr, den, cr, ci, t5 = T[7][:, :], T[0][:, :], T[1][:, :], T[2][:, :], T[3][:, :]
        S.dve(lambda e: e.tensor_scalar(out=nr, in0=are, scalar1=-1.0, scalar2=None, op0=ALU.add), reads=rw, writes=rw)
        S.dve(lambda e: e.tensor_tensor(out=den, in0=lamre, in1=lamre, op=ALU.mult), reads=rw, writes=rw)
        S.dve(lambda e: e.tensor_tensor(out=t5, in0=lamim, in1=lamim, op=ALU.mult), reads=rw, writes=rw)
        S.dve(lambda e: e.tensor_tensor(out=den, in0=den, in1=t5, op=ALU.add), reads=rw, writes=rw)
        S.dve(lambda e: e.reciprocal(out=den, in_=den), reads=rw, writes=rw)
        S.dve(lambda e: e.tensor_tensor(out=cr, in0=nr, in1=lamre, op=ALU.mult), reads=rw, writes=rw)
        S.dve(lambda e: e.tensor_tensor(out=t5, in0=aim, in1=lamim, op=ALU.mult), reads=rw, writes=rw)
        S.dve(lambda e: e.tensor_tensor(out=cr, in0=cr, in1=t5, op=ALU.add), reads=rw, writes=rw)
        S.dve(lambda e: e.tensor_tensor(out=cr, in0=cr, in1=den, op=ALU.mult), reads=rw, writes=rw)
        S.dve(lambda e: e.tensor_tensor(out=ci, in0=aim, in1=lamre, op=ALU.mult), reads=rw, writes=rw)
        S.dve(lambda e: e.tensor_tensor(out=t5, in0=nr, in1=lamim, op=ALU.mult), reads=rw, writes=rw)
        S.dve(lambda e: e.tensor_tensor(out=ci, in0=ci, in1=t5, op=ALU.subtract), reads=rw, writes=rw)
        S.dve(lambda e: e.tensor_tensor(out=ci, in0=ci, in1=den, op=ALU.mult), reads=rw, writes=rw)
        u0, u1 = T[5][:, :], T[6][:, :]
        b32r, b32i = T[0][:, :], T[3][:, :]
        S.dve(lambda e: e.tensor_tensor(out=u0, in0=cr, in1=bre, op=ALU.mult), reads=rw, writes=rw)
        S.dve(lambda e: e.tensor_tensor(out=u1, in0=ci, in1=bim, op=ALU.mult), reads=rw, writes=rw)
        S.dve(lambda e: e.tensor_tensor(out=b32r, in0=u0, in1=u1, op=ALU.subtract), reads=rw, writes=rw)
        S.dve(lambda e: e.tensor_tensor(out=u0, in0=cr, in1=bim, op=ALU.mult), reads=rw, writes=rw)
        S.dve(lambda e: e.tensor_tensor(out=u1, in0=ci, in1=bre, op=ALU.mult), reads=rw, writes=rw)
        S.dve(lambda e: e.tensor_tensor(out=b32i, in0=u0, in1=u1, op=ALU.add), reads=rw + [ph.Btab], writes=rw)
        S.act(lambda e: e.activation(out=ph.Bblk[:, 0, 0, :], in_=b32r, func=AF.Copy), reads=rw, writes=rw)
        S.act(lambda e: e.activation(out=ph.Bblk[:, 0, 1, :], in_=b32i, func=AF.Copy), reads=rw, writes=rw)
        S.dve(lambda e: e.tensor_tensor(out=u0, in0=nr, in1=b32r, op=ALU.mult), reads=rw, writes=rw)
        S.dve(lambda e: e.tensor_tensor(out=u0, in0=u0, in1=b32r, op=ALU.add), reads=rw, writes=rw)
        S.dve(lambda e: e.tensor_tensor(out=u1, in0=aim, in1=b32i, op=ALU.mult), reads=rw, writes=rw)
        S.dve(lambda e: e.tensor_tensor(out=ph.Bblk[:, 1, 0, :], in0=u0, in1=u1, op=ALU.subtract), reads=rw, writes=rw)
        S.dve(lambda e: e.tensor_tensor(out=u0, in0=nr, in1=b32i, op=ALU.mult), reads=rw, writes=rw)
        S.dve(lambda e: e.tensor_tensor(out=u0, in0=u0, in1=b32i, op=ALU.add), reads=rw, writes=rw)
        S.dve(lambda e: e.tensor_tensor(out=u1, in0=aim, in1=b32r, op=ALU.mult), reads=rw, writes=rw)
        S.dve(lambda e: e.tensor_tensor(out=ph.Bblk[:, 1, 1, :], in0=u0, in1=u1, op=ALU.add), reads=rw, writes=rw)

    bu_rr = [0]
    tile_no = [0]
    pending = []
    carry = {"h2": [], "post": None, "evac": None}

    def s5_flush():
        for fn, rd, wr in carry["h2"]:
            if rd is None:
                fn()
            else:
                S.dve(fn, reads=rd, writes=wr)
        carry["post"]()
        carry["evac"]()
        carry["h2"], carry["post"], carry["evac"] = [], None, None

    def s5_main_ct(j, ct):
        ph = CURPH[0]
        psy = {"S": psb[0:4], "P": psb[4:5]}
        nmm = {"S": 0, "P": 0}

        def tile(g, d, gl):
            G = GROUPS[g]
            ntok, L, nseq = G["ntok"], G["L"], G["nseq"]
            ntt = ntok // 512
            nlev = int(math.log2(L))
            gp = ct * 4 + gl
            half, gpl = gl // 2, gl % 2
            tix = d * 32 + gp
            hs = slice(64 * half, 64 * half + 64)
            if g == "S":
                k_ = tile_no[0] % 2
                tile_no[0] += 1
                xs, Bset, Bset2 = ph.XS[k_], ph.BXS[k_], ph.BXS2[k_]
                Hb, BHb = ph.HbS, ph.BHbS
            else:
                xs, Bset, Bset2 = ph.XP, ph.BXP, ph.BXP2
                Hb, BHb = ph.HbP, ph.BHbP
            head = []
            qraw, qpair = (0, 1) if d == 0 else (1, 0)
            cs = slice(gpl * 128, (gpl + 1) * 128)
            for r in range(2):
                for tt in range(ntt):
                    ps, pb = psb[5 + (bu_rr[0] % 3)]
                    bu_rr[0] += 1
                    hv = h[g][hs, ct, ts(tt)].rearrange("p (m q) -> p m q", q=2)
                    xv = xs[:, r, ts(tt)].rearrange("p (m q) -> p m q", q=2)
                    rd_ = [ph.Bprep, bh[g][ct][tt]]
                    S.pe(lambda e, ps=ps, r=r, hv=hv: e.matmul(ps[:, 0:256], ph.Bblk[hs, 0, r, cs], hv[:, :, qraw],
                                                               start=True, stop=True), reads=rd_, writes=[pb])
                    S.pe(lambda e, ps=ps, r=r, hv=hv: e.matmul(ps[:, 256:512], ph.Bblk[hs, 0, r, cs], hv[:, :, qpair],
                                                               start=True, stop=False), reads=rd_, writes=[pb])
                    S.pe(lambda e, ps=ps, r=r, hv=hv: e.matmul(ps[:, 256:512], ph.Bblk[hs, 1, r, cs], hv[:, :, qraw],
                                                               start=False, stop=True), reads=rd_, writes=[pb])
                    S.act(lambda e, ps=ps, xv=xv: e.activation(out=xv[:, :, qraw], in_=ps[:, 0:256], func=AF.Copy),
                          reads=[pb], writes=[(Bset, Bset2)[r]])
                    S.act(lambda e, ps=ps, xv=xv: e.activation(out=xv[:, :, qpair], in_=ps[:, 256:512], func=AF.Copy),
                          reads=[pb], writes=[(Bset, Bset2)[r]])
            if pending:
                pending.pop(0)()
            col = 0 if d == 0 else L - 1
            if g == "S":
                col2 = 1 if d == 0 else L - 2
                for r, inj, c_ in ((0, ph.injre, col), (1, ph.injim, col), (0, ph.inj2re, col2), (1, ph.inj2im, col2)):
                    head.append((lambda e, r=r, inj=inj, c_=c_: e.tensor_tensor(
                        out=xs[:, r, c_:c_ + 1], in0=xs[:, r, c_:c_ + 1], in1=inj[:, tix:tix + 1], op=ALU.add),
                        [(Bset, Bset2)[r], ph.Bq], [(Bset, Bset2)[r]]))
            else:
                for r in range(2):
                    o = ((j * 2 + d) * 32 + gp) * 4 + r * 2
                    src = xs[:, r, 0:ntok].rearrange("p (s t) -> p s t", s=nseq)[:, :, col]
                    head.append((lambda e, o=o, src=src: e.tensor_copy(out=finsb[:, o:o + 2], in_=src),
                                 [(Bset, Bset2)[r]], [Bfin]))

            def level(k, down):
                dd = 1 << k
                M = L // (2 * dd)
                v = xs[:, :, 0:ntok].rearrange("p r (s m q) -> p r s m q", s=nseq, q=2 * dd)
                if d == 0:
                    if not down:
                        tg, sr, n = v[:, :, :, :, 2 * dd - 1], v[:, :, :, :, dd - 1], M
                    else:
                        tg, sr, n = v[:, :, :, 1:M, dd - 1], v[:, :, :, 0:M - 1, 2 * dd - 1], M - 1
                else:
                    if not down:
                        tg, sr, n = v[:, :, :, :, 0], v[:, :, :, :, dd], M
                    else:
                        tg, sr, n = v[:, :, :, 0:M - 1, dd], v[:, :, :, 1:M, 0], M - 1
                if n == 0:
                    return
                pr = ph.pwre[:, tix, k:k + 1]
                pi_ = ph.pwim[:, tix, k:k + 1]
                pin = ph.pwimn[:, tix, k:k + 1]
                scan.append((lambda e: e.scalar_tensor_tensor(out=tg, in0=sr, scalar=pr, in1=tg, op0=ALU.mult, op1=ALU.add),
                             [Bset, Bset2, ph.Bq], [Bset, Bset2]))
                scan.append((lambda e: e.scalar_tensor_tensor(out=tg[:, 0], in0=sr[:, 1], scalar=pin, in1=tg[:, 0],
                                                              op0=ALU.mult, op1=ALU.add), [Bset, ph.Bq], [Bset]))
                scan.append((lambda e: e.scalar_tensor_tensor(out=tg[:, 1], in0=sr[:, 0], scalar=pi_, in1=tg[:, 1],
                                                              op0=ALU.mult, op1=ALU.add), [Bset2, ph.Bq], [Bset2]))

            scan = list(head)
            for k in range(1, nlev):
                level(k, False)
            for k in range(nlev - 2, -1, -1):
                level(k, True)

            def post():
                S.act(lambda e: e.activation(out=Hb[0][:, 0:ntok], in_=xs[:, 0, 0:ntok], func=AF.Copy),
                      reads=[Bset, Bset2], writes=[BHb[0]])
                S.act(lambda e: e.activation(out=Hb[1][:, 0:ntok], in_=xs[:, 1, 0:ntok], func=AF.Copy, scale=-1.0),
                      reads=[Bset, Bset2], writes=[BHb[1]])
                for r in range(2):
                    for tt in range(ntt):
                        ps, pb = psy[g][tt]
                        S.pe(lambda e, ps=ps, r=r, tt=tt, first=(nmm[g] == 0), last=(nmm[g] == 15): e.matmul(
                            ps[:], ph.Cb[:, d, r, gl, :], Hb[r][:, ts(tt)], start=first, stop=last),
                            reads=[ph.BCb, BHb[r]], writes=[pb])
                    nmm[g] += 1
            return scan, post

        def merge2(a, b):
            out = []
            ia = ib = 0
            while ia < len(a) or ib < len(b):
                if ia < len(a) and (ib >= len(b) or ia * len(b) <= ib * len(a)):
                    out.append(a[ia]); ia += 1
                else:
                    out.append(b[ib]); ib += 1
            return out

        def emit(lst):
            for fn, rd, wr in lst:
                if rd is None:
                    fn()
                else:
                    S.dve(fn, reads=rd, writes=wr)

        def load_cb():
            S.dma("pool", lambda e: e.dma_start(
                out=ph.Cb[:], in_=d_s5c[j, ct].rearrange("p (d r g c) -> p d r g c", d=2, r=2, g=4)), writes=[ph.BCb])

        if carry["post"] is None:
            load_cb()

        def capture_prep(ct_, d_):
            cap = []
            orig_add = S.add

            def fake_add(eng, fn, reads=(), writes=(), dma=False):
                cap.append((lambda eng=eng, fn=fn, reads=reads, writes=writes, dma=dma: orig_add(eng, fn, reads, writes, dma),
                            None, None))
            S.add = fake_add
            try:
                s5_prep(j, ct_, d_)
            finally:
                S.add = orig_add
            return cap

        prev_h2, prev_post = carry["h2"], carry["post"]
        carried = carry["post"] is not None
        for d in range(2):
            for gl in range(4):
                if gl == 0 and d == 0 and ct == 0:
                    s5_prep(j, ct, d)
                sP, postP = tile("P", d, gl)
                sS, postS = tile("S", d, gl)
                prep_next = []
                if gl == 3 and d == 0:
                    prep_next = capture_prep(ct, 1)
                elif gl == 3 and d == 1 and ct < 7:
                    prep_next = capture_prep(ct + 1, 0)
                half = len(sS) // 2
                h1, h2 = sS[:half], sS[half:]
                n2 = len(prev_h2)
                a_, b_ = int(n2 * 0.32), int(n2 * 0.78)
                np_ = int(len(sP) * 0.45)
                emit(prev_h2[:a_])
                emit(merge2(prev_h2[a_:b_], sP[:np_]))
                emit(merge2(merge2(merge2(prev_h2[b_:], h1), sP[np_:]), prep_next))
                if carried and d == 0 and gl == 0:
                    prev_post()
                    carry["evac"]()
                    load_cb()
                    postP()
                else:
                    postP()
                    if prev_post is not None:
                        prev_post()
                prev_h2, prev_post = h2, postS
        carry["h2"], carry["post"] = prev_h2, prev_post

        def evac_fn():
          for g in ("P", "S"):
              for tt in range(GROUPS[g]["ntok"] // 512):
                  ps, pb = psy[g][tt]
                  tm, tb = ph.t512.next()
                  S.dve(lambda e, ps=ps, tm=tm, tt=tt, g=g: e.scalar_tensor_tensor(
                      out=tm[:], in0=h[g][:, ct, ts(tt)], scalar=ssmd[:, j, ct:ct + 1], in1=ps[:], op0=ALU.mult, op1=ALU.add),
                      reads=[pb, bh[g][ct][tt], Bconst], writes=[tb])
                  S.act(lambda e, tm=tm, tt=tt, g=g: e.activation(out=h[g][:, ct, ts(tt)], in_=tm[:], func=AF.Gelu_apprx_tanh),
                        reads=[tb], writes=[bh[g][ct][tt]])
        carry["evac"] = evac_fn

    def s5_glu(g, i):
        ph = CURPH[0]
        G = GROUPS[g]
        cj = G["cj"]
        j = i // 2
        ntt = G["ntok"] // 512
        for oc in range(8):
            wt, wb = ph.wbig.next()
            wv = wt[:, 0:2048].rearrange("p (k z c) -> p k z c", k=8, z=2)
            for z in range(2):
                c0 = z * 1024 + oc * 128
                S.dma("pool", lambda e, wv=wv, z=z, c0=c0: e.dma_start(
                    out=wv[:, :, z, :], in_=d_wglu[j, :, c0:c0 + 128].rearrange("(k p) c -> p k c", p=128)), writes=[wb])
            for tt in range(ntt):
                pss = []
                for z in range(2):
                    ps, pb = psrot.next()
                    for kt in range(8):
                        S.pe(lambda e, ps=ps, wv=wv, z=z, kt=kt, tt=tt: e.matmul(
                            ps[:], wv[:, kt, z, :], h[g][:, kt, ts(tt)], start=(kt == 0), stop=(kt == 7)),
                            reads=[wb, bh[g][kt][tt]], writes=[pb])
                    pss.append((ps, pb))
                s2, s2b = ph.t512.next()
                S.act(lambda e, s2=s2, ps=pss[1][0], oc=oc: e.activation(
                    out=s2[:], in_=ps[:], func=AF.Sigmoid, bias=bglu[:, j, 8 + oc:9 + oc]),
                    reads=[pss[1][1], Bconst], writes=[s2b])
                S.dve(lambda e, s2=s2, ps=pss[0][0], oc=oc: e.scalar_tensor_tensor(
                    out=s2[:], in0=ps[:], scalar=bglu[:, j, oc:oc + 1], in1=s2[:], op0=ALU.add, op1=ALU.mult),
                    reads=[pss[0][1], s2b, Bconst], writes=[s2b])
                S.dve(lambda e, s2=s2, oc=oc, tt=tt: e.scalar_tensor_tensor(
                    out=x[g][:, oc, ts(tt)], in0=s2[:], scalar=modL[i][:, 16 + oc, cj:cj + 1], in1=x[g][:, oc, ts(tt)],
                    op0=ALU.mult, op1=ALU.add), reads=[s2b, Bmods[i], bx[g][oc][tt]], writes=[bx[g][oc][tt]])

    def s5_layer(i):
        j = i // 2
        begin_phase("s5m")
        s5_qprep(j)
        for li in ((1, 2) if i == 0 else (3,)):
            pending.extend(mod_chunk_emitters(li))
        for ct in range(8):
            s5_main_ct(j, ct)
        s5_flush()
        while pending:
            pending.pop(0)()
        check_stop(f"s5main{i}")
        begin_phase("s5g")
        for g in GROUPS:
            s5_glu(g, i)

    try:
        for i in range(DEPTH):
            if i == 0:
                begin_phase("mod")
                compute_mod(i)
            check_stop(f"mod{i}")
            if i % 2 == 0:
                begin_phase("norm")
                derive(i)
                for g in GROUPS:
                    norm_h(g, i, 1, list(range(GROUPS[g]["ntok"] // 512)))
                check_stop(f"norm{i}")
                s5_layer(i)
            else:
                begin_phase("four")
                derive(i)
                for g in GROUPS:
                    fourier(g, i)
            check_stop(f"mix{i}")
            begin_phase("ffn")
            ffn_all(i)
            check_stop(f"ffn{i}")
    except _Stop:
        pass
    out_ops = set()
    if stop is not None and stop.startswith("s5prep"):
        ph = CURPH[0]
        dB = dout("dbgB", [128, 1024])
        dT = dout("dbgT", [128, 8 * 256])
        dTab = dout("dbgTab", [128, 1280])
        dPw = dout("dbgPw", [128, 2 * 704])
        rw = [ph.Bprep, ph.Bq]
        op = S.dma("pool", lambda e: e.dma_start(out=dB, in_=ph.Bblk[:].rearrange("p a b c -> p (a b c)")), reads=rw); out_ops.add(op.idx)
        for k in range(8):
            op = S.dma("sp", lambda e, k=k: e.dma_start(out=dT[:, k * 256:(k + 1) * 256], in_=ph.tmpq[k][:]), reads=rw); out_ops.add(op.idx)
        op = S.dma("sp", lambda e: e.dma_start(out=dTab, in_=ph.tab[:].rearrange("p a b -> p (a b)")), reads=rw); out_ops.add(op.idx)
        op = S.dma("sp", lambda e: e.dma_start(out=dPw[:, 0:704], in_=ph.pwre[:].rearrange("p a b -> p (a b)")), reads=rw); out_ops.add(op.idx)
        op = S.dma("sp", lambda e: e.dma_start(out=dPw[:, 704:1408], in_=ph.pwim[:].rearrange("p a b -> p (a b)")), reads=rw); out_ops.add(op.idx)
        S.emit(final_wait_ops=out_ops)
        es.close()
        return nc

    ph = begin_phase("final")
    for g in GROUPS:
        ntt = GROUPS[g]["ntok"] // 512
        outs = {}

        def out_fn(ct, tt, outs=outs):
            tm, tb = ph.outrot.next()
            outs[(ct, tt)] = (tm, tb)
            return tm[:]

        for tt in range(ntt):
            norm(g, [tt], lambda ct: gfin[:, ct:ct + 1], lambda ct: None, out_fn, [Bconst],
                 lambda ct, tt, outs=outs: [outs[(ct, tt)][1]])
            for ct in range(8):
                tm, tb = outs[(ct, tt)]
                op = S.dma("sp", lambda e, tm=tm, ct=ct, tt=tt, g=g: e.dma_start(
                    out=d_y[g][ct * 128:(ct + 1) * 128, ts(tt)], in_=tm[:]), reads=[tb])
                out_ops.add(op.idx)
    for blk in range(4):
        ps, pb = psrot.next()
        S.pe(lambda e, ps=ps, blk=blk: e.transpose(ps[:, 0:128], finsb[:, blk * 128:(blk + 1) * 128], identf[:]),
             reads=[Bfin, Bconst], writes=[pb])
        tm, tb = ph.t512.next()
        S.act(lambda e, ps=ps, tm=tm: e.activation(out=tm[:, 0:128], in_=ps[:, 0:128], func=AF.Copy),
              reads=[pb], writes=[tb])
        op = S.dma("sp", lambda e, tm=tm, blk=blk: e.dma_start(out=d_fin[blk * 128:(blk + 1) * 128, :], in_=tm[:, 0:128]),
                   reads=[tb])
        out_ops.add(op.idx)

    if stop is not None:
        begin_phase("final")
        d_hd = {g: dout("hd" + g, [1024, GROUPS[g]["ntok"]]) for g in GROUPS}
        d_xd = {g: dout("xd" + g, [1024, GROUPS[g]["ntok"]]) for g in GROUPS}
        d_modo = dout("modo", [128, 96])
        allb = [b for g in GROUPS for ct in range(8) for b in bh[g][ct]] + [b for g in GROUPS for ct in range(8) for b in bx[g][ct]]
        for g in GROUPS:
            for ct in range(8):
                op = S.dma("pool", lambda e, g=g, ct=ct: e.dma_start(out=d_hd[g][ct * 128:(ct + 1) * 128, :], in_=h[g][:, ct, :]),
                           reads=allb)
                out_ops.add(op.idx)
                op = S.dma("sp", lambda e, g=g, ct=ct: e.dma_start(out=d_xd[g][ct * 128:(ct + 1) * 128, :], in_=x[g][:, ct, :]),
                           reads=allb)
                out_ops.add(op.idx)
        op = S.dma("sp", lambda e: e.dma_start(out=d_modo, in_=modL[0].rearrange("p a b -> p (a b)")), reads=[Bmods[0]])
        out_ops.add(op.idx)
    S.emit(final_wait_ops=out_ops)
    es.close()
    return nc


def _fm(v):
    v = np.asarray(v, np.float32)
    lead = v.shape[:-1]
    n = v.shape[-1] // 128
    r = v.reshape(lead + (n, 128))
    r = np.moveaxis(r, -1, 0)
    return np.ascontiguousarray(r.reshape(128, -1))


def _dft_tables():
    import ml_dtypes
    bf = ml_dtypes.bfloat16
    k = np.arange(256, dtype=np.float64)
    ang = 2 * np.pi * np.outer(k, k) / 256.0
    s1 = 1.0 / 16.0
    ckc = (np.cos(ang) * s1).reshape(2, 128, 256)
    cks = (-np.sin(ang) * s1).reshape(2, 128, 256)
    ck = np.stack([ckc, cks], 0)
    ck = np.ascontiguousarray(ck.transpose(2, 0, 1, 3).reshape(128, -1)).astype(np.float32)
    sp = 1.0 / 16.0
    clpc = (np.cos(ang) * sp).reshape(2, 128, 256)
    clps = (np.sin(ang) * sp).reshape(2, 128, 256)
    clp = np.stack([clpc, clps], 0)
    clp = np.ascontiguousarray(clp.transpose(2, 0, 1, 3).reshape(128, -1)).astype(np.float32)
    l = np.arange(2048, dtype=np.int64)
    m = np.outer(l, l) % 2048
    angL = 2 * np.pi * m.astype(np.float64) / 2048.0
    ss = 1.0 / math.sqrt(2048.0)
    cls = np.stack([np.cos(angL) * ss, np.sin(angL) * ss], 0).astype(np.float32).astype(bf)
    return ck, clp, cls


_CACHE = {}


def kernel(x_prompt, x_sample, state_ssm_re, state_ssm_im, c, c_ctx,
           w_ada, b_ada, g_mix, g_ffn,
           ssm_lam_re, ssm_lam_im, ssm_log_dt, ssm_b_re, ssm_b_im, ssm_c_re, ssm_c_im, ssm_d,
           w_glu, b_glu, w_fourier, b_fourier,
           w_up, conv_w, conv_b, w_down, g_final):
    f32 = np.float32
    A = lambda v: np.ascontiguousarray(np.asarray(v, f32))
    if "nc" not in _CACHE:
        _CACHE["nc"] = build_program()
        _CACHE["dft"] = _dft_tables()
    nc = _CACHE["nc"]
    ck, clp, cls = _CACHE["dft"]

    lam_re, lam_im, log_dt = A(ssm_lam_re), A(ssm_lam_im), A(ssm_log_dt)
    b_re, b_im, c_re, c_im = A(ssm_b_re), A(ssm_b_im), A(ssm_c_re), A(ssm_c_im)
    s5tab = np.zeros((2, 8, 128, 2, 5, 2, 2, 64), f32)
    s5c = np.zeros((2, 8, 128, 2, 2, 4, 128), f32)
    for ct in range(8):
        for gi in range(8):
            g = ct * 8 + gi
            half, gpl, gpar = gi // 4, (gi % 4) // 2, gi % 2
            gl = gi // 2
            rows = slice(gi * 16, gi * 16 + 16)
            s5tab[:, ct, rows, :, 3, gpl, gpar, :] = np.transpose(b_re[:, :, g], (0, 3, 1, 2))
            s5tab[:, ct, rows, :, 4, gpl, gpar, :] = np.transpose(b_im[:, :, g], (0, 3, 1, 2))
            prow = slice(gpar * 64, gpar * 64 + 64)
            s5c[:, ct, prow, :, 0, gl, gi * 16:gi * 16 + 16] = np.transpose(c_re[:, :, g], (0, 3, 1, 2))
            s5c[:, ct, prow, :, 1, gl, gi * 16:gi * 16 + 16] = np.transpose(c_im[:, :, g], (0, 3, 1, 2))
        for hf in range(2):
            for gpl in range(2):
                for gpar in range(2):
                    g = ct * 8 + hf * 4 + gpl * 2 + gpar
                    rows = slice(hf * 64, hf * 64 + 64)
                    s5tab[:, ct, rows, :, 0, gpl, gpar, :] = lam_re[:, None, :, g, :]
                    s5tab[:, ct, rows, :, 1, gpl, gpar, :] = lam_im[:, None, :, g, :]
                    s5tab[:, ct, rows, :, 2, gpl, gpar, :] = log_dt[:, None, :, g, None]
    s5tab = s5tab.reshape(2, 8, 128, -1)
    s5c = s5c.reshape(2, 8, 128, -1)

    def qlay(a):
        a = a.reshape(2, 2, 32, 2, 64)
        return np.ascontiguousarray(a.transpose(3, 4, 0, 1, 2).reshape(128, 128))

    ident = np.eye(128, dtype=f32)
    common = {
        "w_ada": A(w_ada), "badaT": _fm(b_ada), "gmixT": _fm(g_mix), "gffnT": _fm(g_ffn), "gfinT": _fm(g_final),
        "w_glu": A(w_glu), "bgluT": _fm(b_glu), "w_fourier": A(w_fourier), "bfourT": _fm(b_fourier),
        "w_up": A(w_up), "convwT": _fm(conv_w), "convbT": _fm(conv_b), "w_down": A(w_down), "ssmdT": _fm(ssm_d),
        "s5tab": s5tab, "s5c": s5c, "ident": ident, "ck": ck, "clp": clp, "cls": cls,
    }
    xp, xs = A(x_prompt), A(x_sample)
    st_re, st_im = A(state_ssm_re), A(state_ssm_im)
    cc, cctx = A(c), A(c_ctx)
    ldt_q = np.broadcast_to(log_dt[..., None], lam_re.shape)
    in_maps = []
    for core in range(8):
        b = core % 2
        m = dict(common)
        m["xpT"] = np.ascontiguousarray(xp[2 * core:2 * core + 2].reshape(512, 1024).T)
        m["xsT"] = np.ascontiguousarray(xs[b].T)
        cond = np.stack([cctx, cc[b]], -1)
        m["cond"] = np.ascontiguousarray(cond.reshape(8, 128, 2).transpose(1, 0, 2).reshape(128, 16))
        m["s5q"] = np.ascontiguousarray(np.stack(
            [qlay(lam_re), qlay(lam_im), qlay(np.ascontiguousarray(ldt_q)), qlay(st_re[b]), qlay(st_im[b])], 1
        ).reshape(128, 5 * 128))
        in_maps.append(m)
    res = run_bass_kernel_spmd(nc, in_maps, core_ids=list(range(8)))
    R = res.results
    if DEBUG_STOP is not None:
        return R
    y_prompt = np.empty((16, 256, 1024), f32)
    y_sample = np.empty((2, 2048, 1024), f32)
    new_re = np.empty((16, 2, 2, 64, 64), f32)
    new_im = np.empty((16, 2, 2, 64, 64), f32)
    for core in range(8):
        y_prompt[2 * core:2 * core + 2] = R[core]["ypT"].T.reshape(2, 256, 1024)
        fin = R[core]["fin"].reshape(2, 2, 32, 2, 2, 2, 64)
        for s in range(2):
            for r, dst in ((0, new_re), (1, new_im)):
                v = fin[:, :, :, r, s]
                dst[2 * core + s] = v.reshape(2, 2, 64, 64)
    for b in range(2):
        y_sample[b] = R[b]["ysT"].T
    return (y_prompt, y_sample, new_re, new_im)
```
